# Optimizing a Trainium2 kernel written in Bass

```python
import math
import jax, jax.numpy as jnp
from jax import lax
import numpy as np

D_MODEL = 1024
BATCH = 8
SEQ = 2048
DEPTH = 4
DEC_BATCH = 128
DEC_SEQ = 4
PAST_LEN = 16384
PAGE_SIZE = 128

N_META = 16
D_RNN = D_MODEL
N_RNN_HEADS = 16
RNN_HEAD_DIM = D_RNN // N_RNN_HEADS
RNN_CONV_W = 4
LRU_C = 8.0
D_CONV = D_MODEL
N_CONV_GROUPS = 16
SC_CONV_W = 3
D_FF = int(math.ceil((8 * D_MODEL / 3) / 256) * 256)
EPS = 1e-6
SPLITS = (D_RNN, D_RNN, D_CONV, D_CONV, D_CONV, D_MODEL, D_MODEL)
D_IN = sum(SPLITS)

kernel_name = "hawk_shortconv_parallel_meta_decoder_step"


def rmsnorm(x, g):
    xf = x.astype(jnp.float32)
    y = xf * lax.rsqrt(jnp.mean(xf * xf, axis=-1, keepdims=True) + EPS)
    return (y * g.astype(jnp.float32)).astype(x.dtype)


def causal_dwconv(x, buf, w, b=None):
    width = w.shape[0]
    t = x.shape[1]
    xp = jnp.concatenate([buf.astype(x.dtype), x], axis=1)
    out = xp[:, 0:t] * w[0]
    for k in range(1, width):
        out = out + xp[:, k:k + t] * w[k]
    if b is not None:
        out = out + b
    return out, xp[:, xp.shape[1] - (width - 1):]


def rglru(x, h0, wa, ba, wx, bx, lam):
    bsz, t, _ = x.shape
    xh = x.reshape(bsz, t, N_RNN_HEADS, RNN_HEAD_DIM)
    r = jax.nn.sigmoid(jnp.einsum('bthi,hij->bthj', xh, wa).reshape(bsz, t, D_RNN).astype(jnp.float32) + ba.astype(jnp.float32))
    i = jax.nn.sigmoid(jnp.einsum('bthi,hij->bthj', xh, wx).reshape(bsz, t, D_RNN).astype(jnp.float32) + bx.astype(jnp.float32))
    log_a = -LRU_C * r * jax.nn.softplus(-lam.astype(jnp.float32))
    a = jnp.exp(log_a)
    mult = jnp.sqrt(jnp.maximum(-jnp.expm1(2.0 * log_a), 1e-12))
    u = mult * i * x.astype(jnp.float32)
    u = jnp.concatenate([u[:, :1] + a[:, :1] * h0.astype(jnp.float32)[:, None], u[:, 1:]], axis=1)

    def combine(left, right):
        a1, b1 = left
        a2, b2 = right
        return a1 * a2, a2 * b1 + b2

    _, h = lax.associative_scan(combine, (a, u), axis=1)
    return h, h[:, -1]


def trunk(x, h_st, rconv_st, sconv_st,
          norm1_g, w_in, rnn_conv_w, rnn_conv_b, gate_a_w, gate_a_b, gate_x_w, gate_x_b, lru_lambda,
          w_branch_a, sc_conv_w, w_branch_b, w_out, norm2_g, w_ff_gate, w_ff_up, w_ff_down, final_norm_g):
    dt = x.dtype
    offs = list(np.cumsum(SPLITS)[:-1])
    hs, rcs, scs = [], [], []
    for l in range(DEPTH):
        u = rmsnorm(x, norm1_g[l])
        p = u @ w_in[l]
        xr, gr, bc, cc, hc, ga, gb = jnp.split(p, offs, axis=-1)
        xr, new_rc = causal_dwconv(xr, rconv_st[l], rnn_conv_w[l], rnn_conv_b[l])
        hseq, h_last = rglru(xr, h_st[l], gate_a_w[l], gate_a_b[l], gate_x_w[l], gate_x_b[l], lru_lambda[l])
        ya = hseq.astype(dt) * jax.nn.gelu(gr)
        vc, new_sc = causal_dwconv(cc * hc, sconv_st[l], sc_conv_w[l])
        yb = bc * vc
        m = jax.nn.sigmoid(ga) * (ya @ w_branch_a[l]) + jax.nn.sigmoid(gb) * (yb @ w_branch_b[l])
        x = x + m @ w_out[l]
        v = rmsnorm(x, norm2_g[l])
        x = x + (jax.nn.silu(v @ w_ff_gate[l]) * (v @ w_ff_up[l])) @ w_ff_down[l]
        hs.append(h_last.astype(dt))
        rcs.append(new_rc)
        scs.append(new_sc)
    return rmsnorm(x, final_norm_g), jnp.stack(hs), jnp.stack(rcs), jnp.stack(scs)


def setup_inputs(seed: int = 0) -> dict:
    key = jax.random.key(seed)
    ks = jax.random.split(key, 32)
    f = jnp.float32
    nrm = lambda k, shape, s: jax.random.normal(k, shape, f) * s
    a8 = jax.random.uniform(ks[10], (DEPTH, D_RNN), f, 0.9, 0.999)
    a_base = a8 ** (1.0 / LRU_C)
    lru_lambda = jnp.log(a_base) - jnp.log1p(-a_base)
    return {
        "x_prompt": nrm(ks[0], (BATCH, SEQ, D_MODEL), 1.0),
        "x_sample": nrm(ks[1], (DEC_BATCH, DEC_SEQ, D_MODEL), 1.0),
        "state_rnn_h": nrm(ks[2], (DEPTH, DEC_BATCH, D_RNN), 0.5),
        "state_rnn_conv": nrm(ks[3], (DEPTH, DEC_BATCH, RNN_CONV_W - 1, D_RNN), 1.0),
        "state_sc_conv": nrm(ks[4], (DEPTH, DEC_BATCH, SC_CONV_W - 1, D_CONV), 1.0),
        "meta_tokens": nrm(ks[5], (N_META, D_MODEL), 1.0),
        "norm1_g": 1.0 + nrm(ks[6], (DEPTH, D_MODEL), 0.02),
        "w_in": nrm(ks[7], (DEPTH, D_MODEL, D_IN), D_MODEL ** -0.5),
        "rnn_conv_w": nrm(ks[8], (DEPTH, RNN_CONV_W, D_RNN), RNN_CONV_W ** -0.5),
        "rnn_conv_b": nrm(ks[9], (DEPTH, D_RNN), 0.02),
        "gate_a_w": nrm(ks[11], (DEPTH, N_RNN_HEADS, RNN_HEAD_DIM, RNN_HEAD_DIM), RNN_HEAD_DIM ** -0.5),
        "gate_a_b": nrm(ks[12], (DEPTH, D_RNN), 0.02),
        "gate_x_w": nrm(ks[13], (DEPTH, N_RNN_HEADS, RNN_HEAD_DIM, RNN_HEAD_DIM), RNN_HEAD_DIM ** -0.5),
        "gate_x_b": nrm(ks[14], (DEPTH, D_RNN), 0.02),
        "lru_lambda": lru_lambda,
        "w_branch_a": nrm(ks[15], (DEPTH, D_RNN, D_MODEL), D_RNN ** -0.5),
        "sc_conv_w": nrm(ks[16], (DEPTH, SC_CONV_W, D_CONV), SC_CONV_W ** -0.5),
        "w_branch_b": nrm(ks[17], (DEPTH, D_CONV, D_MODEL), D_CONV ** -0.5),
        "w_out": nrm(ks[18], (DEPTH, D_MODEL, D_MODEL), D_MODEL ** -0.5),
        "norm2_g": 1.0 + nrm(ks[19], (DEPTH, D_MODEL), 0.02),
        "w_ff_gate": nrm(ks[20], (DEPTH, D_MODEL, D_FF), D_MODEL ** -0.5),
        "w_ff_up": nrm(ks[21], (DEPTH, D_MODEL, D_FF), D_MODEL ** -0.5),
        "w_ff_down": nrm(ks[22], (DEPTH, D_FF, D_MODEL), D_FF ** -0.5),
        "final_norm_g": 1.0 + nrm(ks[23], (D_MODEL,), 0.02),
    }


def reference(x_prompt, x_sample, state_rnn_h, state_rnn_conv, state_sc_conv, meta_tokens,
              norm1_g, w_in, rnn_conv_w, rnn_conv_b, gate_a_w, gate_a_b, gate_x_w, gate_x_b, lru_lambda,
              w_branch_a, sc_conv_w, w_branch_b, w_out, norm2_g, w_ff_gate, w_ff_up, w_ff_down, final_norm_g):
    weights = (norm1_g, w_in, rnn_conv_w, rnn_conv_b, gate_a_w, gate_a_b, gate_x_w, gate_x_b, lru_lambda,
               w_branch_a, sc_conv_w, w_branch_b, w_out, norm2_g, w_ff_gate, w_ff_up, w_ff_down, final_norm_g)
    bp = x_prompt.shape[0]
    dt = x_prompt.dtype
    meta = jnp.broadcast_to(meta_tokens.astype(dt)[None], (bp, N_META, D_MODEL))
    xp = jnp.concatenate([meta, x_prompt], axis=1)
    h0 = jnp.zeros((DEPTH, bp, D_RNN), dt)
    rc0 = jnp.zeros((DEPTH, bp, RNN_CONV_W - 1, D_RNN), dt)
    sc0 = jnp.zeros((DEPTH, bp, SC_CONV_W - 1, D_CONV), dt)
    yp, rnn_h_prompt, rnn_conv_prompt, sc_conv_prompt = trunk(xp, h0, rc0, sc0, *weights)
    y_prompt = yp[:, N_META:]
    y_sample, rnn_h_sample, rnn_conv_sample, sc_conv_sample = trunk(
        x_sample, state_rnn_h, state_rnn_conv, state_sc_conv, *weights)
    return (y_prompt, y_sample, rnn_h_prompt, rnn_conv_prompt, sc_conv_prompt,
            rnn_h_sample, rnn_conv_sample, sc_conv_sample)
```

```python
import numpy as np
from contextlib import ExitStack
import concourse.bass as bass
import concourse.mybir as mybir
from concourse.bass_utils import run_bass_kernel_spmd

F32 = mybir.dt.float32
BF16 = mybir.dt.bfloat16
AF = mybir.ActivationFunctionType
ALU = mybir.AluOpType

D = 1024
NCH = 8
DFF = 2816
NFF = 22
DEPTH = 4
NMETA = 16
SEQ = 2048
TP = NMETA + SEQ
NS = 16
ST = 4
TS = NS * ST
TTOT = TP + TS
NCORES = 8
EPS = 1e-6
OFF = dict(xr=0, gr=1024, bc=2048, cc=3072, hc=4096, ga=5120, gb=6144)
TPG = TP // 2
TW = 344
GROUPS = [
    dict(off=0, n=TPG, tiles=[(0, TW), (TW, TW), (2 * TW, TW)], samp=False),
    dict(off=TPG, n=TPG + TS, tiles=[(0, TW), (TW, TW), (2 * TW, TW + TS)], samp=True),
]
GW = TPG + TS
NVEC = 13
NSTI = 96
NSTO = 102
W_GATES = 2 * 8 * 128
W_MIX = 5 * 1024
W_MRG = 4 * 1024
W_OUT = 4 * 1024
W_FFN = 4 * 1024
W_DN = NFF * 128
WS_LAYER = W_GATES + 8 * W_MIX + 8 * W_MRG + 2 * W_OUT + 11 * W_FFN + 8 * W_DN
WBUF = 5120
XRW = 1160
SB0 = 1040
NSQ = 3

SAME_SYNC_ALL = True


class Dep:
    __slots__ = ("w", "r")

    def __init__(self):
        self.w = None
        self.r = {}


class Sched:
    ENGS = ("pe", "act", "dve", "pool", "sp")

    def __init__(self):
        self.streams = {e: [] for e in self.ENGS}
        self.tick = {e: 0 for e in self.ENGS}
        self.known = {e: {} for e in self.ENGS}
        self.dma_cnt = {}

    def _waits(self, eng, reads, writes, small):
        waits = {}

        def need(k, v):
            if k == eng and (eng == "pe" or not (small or SAME_SYNC_ALL)):
                return
            if self.known[eng].get(k, 0) >= v:
                return
            if waits.get(k, 0) < v:
                waits[k] = v

        for t in reads:
            if t.w is not None:
                need(*t.w)
        for t in writes:
            if t.w is not None:
                need(*t.w)
            for k, v in t.r.items():
                need(k, v)
        for k, v in waits.items():
            self.known[eng][k] = v
        return list(waits.items())

    def _mark(self, tok, reads, writes):
        k, v = tok
        for t in reads:
            t.r[k] = v
        for t in writes:
            t.w = tok
            t.r = {}

    def op(self, eng, fn, reads=(), writes=(), small=False):
        waits = self._waits(eng, reads, writes, small)
        self.tick[eng] += 1
        tok = (eng, self.tick[eng])
        self.streams[eng].append((waits, fn, eng, 1))
        self._mark(tok, reads, writes)

    def dma(self, eng, fn, semkey, reads=(), writes=()):
        waits = self._waits(eng, reads, writes, False)
        self.dma_cnt[semkey] = self.dma_cnt.get(semkey, 0) + 16
        tok = (semkey, self.dma_cnt[semkey])
        self.streams[eng].append((waits, fn, semkey, 16))
        self._mark(tok, reads, writes)


def build_program(depth=DEPTH):
    nc = bass.Bass("TRN2", target_bir_lowering=False)
    xp_d = nc.dram_tensor("xp", [128, NCH, SEQ], F32, kind="ExternalInput").ap()
    meta_d = nc.dram_tensor("meta", [128, NCH, NMETA], F32, kind="ExternalInput").ap()
    xs_d = nc.dram_tensor("xs", [128, NCH, TS], F32, kind="ExternalInput").ap()
    stin_d = nc.dram_tensor("stin", [DEPTH, 128, NCH * NSTI], F32, kind="ExternalInput").ap()
    vecs_d = nc.dram_tensor("vecs", [128, NVEC * DEPTH * 8 + 8], F32, kind="ExternalInput").ap()
    ws_d = nc.dram_tensor("ws", [DEPTH, 128, WS_LAYER], F32, kind="ExternalInput").ap()
    y_d = nc.dram_tensor("y", [128, NCH, TTOT], F32, kind="ExternalOutput").ap()
    sto_d = nc.dram_tensor("sto", [DEPTH, 128, NCH * NSTO], F32, kind="ExternalOutput").ap()

    S = Sched()
    es = ExitStack()

    def sb(name, shape, dt):
        return es.enter_context(nc.sbuf_tensor(name, shape, dt))

    x_sb = sb("x_sb", [128, NCH, TTOT], F32)
    u_sb = sb("u_sb", [128, NCH, GW], BF16)
    R = sb("R", [128, 24, GW], BF16)
    wbuf = [sb("wbuf0", [128, WBUF], BF16), sb("wbuf1", [128, WBUF], BF16)]
    gw = sb("gw", [128, W_GATES], BF16)
    vec_sb = sb("vec_sb", [128, NVEC * DEPTH * 8 + 8], F32)
    der_sb = sb("der_sb", [128, 4 * DEPTH * 8], F32)
    dtmp = [sb("dtmp%d" % i, [128, DEPTH * 8], F32) for i in range(6)]
    ones_bf = sb("ones_bf", [128, 128], BF16)
    xr_sb = sb("xr_sb", [128, XRW], F32)
    xc = sb("xc", [128, GW], F32)
    xcb = sb("xcb", [128, GW], BF16)
    vc = sb("vc", [128, GW], F32)
    bA = sb("bA", [128, GW], F32)
    bB = sb("bB", [128, GW], F32)
    bC = sb("bC", [128, GW], F32)
    bD = sb("bD", [128, GW], F32)
    bD1 = sb("bD1", [128, GW], F32)
    sqb = [sb("sqb%d" % i, [128, 416], BF16) for i in range(NSQ)]
    st_in = sb("st_in", [128, NCH, NSTI], F32)
    st_out = sb("st_out", [128, NCH, NSTO], F32)
    hcar = sb("hcar", [128, NCH], F32)
    xhalo = sb("xhalo", [128, NCH, 3], F32)
    chhalo = sb("chhalo", [128, NCH, 2], F32)
    tmp16 = sb("tmp16", [128, NS], F32)
    banks = [es.enter_context(nc.psum_tensor("ps%d" % i, [128, 512], F32)) for i in range(8)]

    d_x = [[Dep() for _ in range(6)] for _ in range(NCH)]
    d_u = [[Dep() for _ in range(3)] for _ in range(NCH)]
    d_R = [[Dep() for _ in range(3)] for _ in range(24)]
    d_wbuf = [Dep(), Dep()]
    d_gw = Dep()
    d_vec = Dep()
    d_der = Dep()
    d_dtmp = [Dep() for _ in range(6)]
    d_ones = Dep()
    d_xr = [Dep() for _ in range(3)]
    d_xrh = Dep()
    d_xc = [Dep() for _ in range(3)]
    d_xcb = [Dep() for _ in range(3)]
    d_vc = [Dep() for _ in range(3)]
    d_bA = [Dep() for _ in range(3)]
    d_bB = [Dep() for _ in range(3)]
    d_bC = [Dep() for _ in range(3)]
    d_bD = [Dep() for _ in range(3)]
    d_bD1 = [Dep() for _ in range(3)]
    d_sqb = [Dep() for _ in range(NSQ)]
    d_stin = Dep()
    d_stout = Dep()
    d_hcar = Dep()
    d_xhalo = Dep()
    d_chhalo = Dep()
    d_tmp16 = Dep()
    d_bank = [Dep() for _ in range(8)]
    d_y = Dep()

    bank_ctr = [0]

    held = set()

    def next_bank():
        while True:
            b = bank_ctr[0] % 8
            bank_ctr[0] += 1
            if b not in held:
                return banks[b], d_bank[b]

    def hold_bank():
        bk = next_bank()
        held.add(banks.index(bk[0]))
        return bk

    def release_bank(bk):
        held.discard(banks.index(bk[0]))

    def V(l, i, c):
        o = (i * DEPTH + l) * 8 + c
        return vec_sb[:, o:o + 1]

    def DER(kind, l, c):
        o = kind * DEPTH * 8 + l * 8 + c
        return der_sb[:, o:o + 1]

    batches = []
    for l in range(depth):
        for g in range(2):
            off = W_GATES
            for _ in range(8):
                batches.append((l, off, W_MIX)); off += W_MIX
            for _ in range(8):
                batches.append((l, off, W_MRG)); off += W_MRG
            for _ in range(2):
                batches.append((l, off, W_OUT)); off += W_OUT
            for _ in range(11):
                batches.append((l, off, W_FFN)); off += W_FFN
            for _ in range(8):
                batches.append((l, off, W_DN)); off += W_DN
            assert off == WS_LAYER
    bstate = dict(issued=0, used=0)

    def issue_batch():
        i = bstate["issued"]
        if i >= len(batches):
            return
        l, off, n = batches[i]
        b = i % 2
        src = ws_d[l, :, off:off + n]
        dst = wbuf[b][:, 0:n]
        S.dma("pool", lambda e, src=src, dst=dst: e.dma_start(out=dst, in_=src, max_dma_last_dim=4096),
              "w%d" % b, reads=(), writes=(d_wbuf[b],))
        bstate["issued"] += 1

    def next_batch():
        i = bstate["used"]
        while bstate["issued"] <= min(i, len(batches) - 1):
            issue_batch()
        b = i % 2
        bstate["used"] += 1
        return wbuf[b], d_wbuf[b]

    def prefetch():
        if bstate["issued"] < bstate["used"] + 1:
            issue_batch()

    def mm_group(bank, d_b, n, pairs, reads):
        def fn(e, bank=bank, n=n, pairs=pairs):
            last = None
            for i, (lt, rh) in enumerate(pairs):
                last = e.matmul(out=bank[:, 0:n], lhsT=lt, rhs=rh, start=(i == 0), stop=(i == len(pairs) - 1))
            return last
        S.op("pe", fn, reads=reads, writes=(d_b,))

    def act(out, in_, func, reads, writes, bias=None, scale=None, small=False):
        kw = {}
        if bias is not None:
            kw["bias"] = bias
        if scale is not None:
            kw["scale"] = scale
        S.op("act", lambda e: e.activation(out=out, in_=in_, func=func, **kw), reads=reads, writes=writes, small=small)

    def tt(out, in0, in1, op, reads, writes, small=False):
        S.op("dve", lambda e: e.tensor_tensor(out=out, in0=in0, in1=in1, op=op), reads=reads, writes=writes, small=small)

    def ts(out, in0, s1, op0, reads, writes, s2=None, op1=None, small=False):
        if op1 is None:
            S.op("dve", lambda e: e.tensor_scalar(out=out, in0=in0, scalar1=s1, scalar2=None, op0=op0),
                 reads=reads, writes=writes, small=small)
        else:
            S.op("dve", lambda e: e.tensor_scalar(out=out, in0=in0, scalar1=s1, scalar2=s2, op0=op0, op1=op1),
                 reads=reads, writes=writes, small=small)

    def stt(out, in0, scalar, in1, op0, op1, reads, writes, small=False):
        S.op("dve", lambda e: e.scalar_tensor_tensor(out=out, in0=in0, scalar=scalar, in1=in1, op0=op0, op1=op1),
             reads=reads, writes=writes, small=small)

    def cp(out, in_, reads, writes, small=True):
        S.op("dve", lambda e: e.tensor_copy(out=out, in_=in_), reads=reads, writes=writes, small=small)

    S.dma("sp", lambda e: e.dma_start(out=vec_sb[:], in_=vecs_d), "ldv", writes=(d_vec,))
    for c0 in range(0, NCH, 2):
        S.dma("sp", lambda e, c0=c0: e.dma_start(out=x_sb[:, c0:c0 + 2, NMETA:TP], in_=xp_d[:, c0:c0 + 2, :]), "ldx%d" % c0,
              writes=[d_x[c][t] for c in (c0, c0 + 1) for t in range(6)])
    S.dma("sp", lambda e: e.dma_start(out=x_sb[:, :, 0:NMETA], in_=meta_d), "ldm",
          writes=[d_x[c][0] for c in range(NCH)])
    S.dma("sp", lambda e: e.dma_start(out=x_sb[:, :, TP:TTOT], in_=xs_d), "lds",
          writes=[d_x[c][5] for c in range(NCH)])
    S.op("dve", lambda e: e.memset(ones_bf[:], 1.0), writes=(d_ones,))

    NL = DEPTH * 8
    lam = vec_sb[:, 8 * NL:9 * NL]
    t0, t1, t2, t3, t4, t5 = [t[:] for t in dtmp]
    dd = d_dtmp
    ts(t0, lam, -1.0, ALU.mult, [d_vec], [dd[0]], small=True)
    tt(t0, t0, lam, ALU.min, [d_vec, dd[0]], [dd[0]], small=True)
    act(t1, t0, AF.Exp, [dd[0]], [dd[1]], small=True)
    ts(t2, t1, 2.0, ALU.add, [dd[1]], [dd[2]], small=True)
    S.op("dve", lambda e: e.reciprocal(out=t2, in_=t2), reads=[dd[2]], writes=[dd[2]], small=True)
    tt(t3, t1, t2, ALU.mult, [dd[1], dd[2]], [dd[3]], small=True)
    tt(t4, t3, t3, ALU.mult, [dd[3]], [dd[4]], small=True)
    S.op("dve", lambda e: e.memset(t5, 0.0), writes=[dd[5]], small=True)
    for k in range(9, 0, -1):
        stt(t5, t5, 1.0 / (2 * k + 1), t4, ALU.add, ALU.mult, [dd[5], dd[4]], [dd[5]], small=True)
    stt(t5, t5, 1.0, t3, ALU.add, ALU.mult, [dd[5], dd[3]], [dd[5]], small=True)
    ts(t0, lam, -1.0, ALU.mult, [d_vec, dd[0]], [dd[0]], s2=0.0, op1=ALU.max, small=True)
    stt(t1, t5, 2.0, t0, ALU.mult, ALU.add, [dd[5], dd[0], dd[1]], [dd[1]], small=True)
    ts(der_sb[:, 0 * NL:1 * NL], vec_sb[:, 6 * NL:7 * NL], 0.5, ALU.mult, [d_vec], [d_der], small=True)
    ts(der_sb[:, 1 * NL:2 * NL], vec_sb[:, 7 * NL:8 * NL], 0.5, ALU.mult, [d_vec], [d_der], small=True)
    ts(der_sb[:, 2 * NL:3 * NL], t1, -4.0, ALU.mult, [dd[1]], [d_der], small=True)
    ts(der_sb[:, 3 * NL:4 * NL], t1, 2.0, ALU.mult, [dd[1]], [d_der], small=True)

    sq_ctr = [0]

    def norm_sq_chunk(g, t, c, bank, d_b):
        G = GROUPS[g]
        o, n = G["tiles"][t]
        gt = g * 3 + t
        go = G["off"] + o
        q = sq_ctr[0] % NSQ
        sq_ctr[0] += 1
        act(sqb[q][:, 0:n], x_sb[:, c, go:go + n], AF.Square, [d_x[c][gt]], [d_sqb[q]])
        S.op("pe", lambda e: e.matmul(out=bank[:, 0:n], lhsT=ones_bf[:], rhs=sqb[q][:, 0:n],
                                      start=(c == 0), stop=(c == NCH - 1)),
             reads=[d_sqb[q], d_ones], writes=[d_b])

    def norm_finish(l, g, t, gi, bank, d_b, final=False):
        G = GROUPS[g]
        o, n = G["tiles"][t]
        gt = g * 3 + t
        go = G["off"] + o
        act(bA[:, o:o + n], bank[:, 0:n], AF.Ln, [d_b], [d_bA[t]], bias=EPS, scale=1.0 / D)
        act(bA[:, o:o + n], bA[:, o:o + n], AF.Exp, [d_bA[t]], [d_bA[t]], scale=-0.5)
        for c in range(NCH):
            if final:
                fo = NVEC * DEPTH * 8 + c
                stt(x_sb[:, c, go:go + n], x_sb[:, c, go:go + n], vec_sb[:, fo:fo + 1], bA[:, o:o + n], ALU.mult, ALU.mult,
                    [d_x[c][gt], d_bA[t], d_vec], [d_x[c][gt]])
            else:
                stt(u_sb[:, c, o:o + n], x_sb[:, c, go:go + n], V(l, gi, c), bA[:, o:o + n], ALU.mult, ALU.mult,
                    [d_x[c][gt], d_bA[t], d_vec], [d_u[c][t]])

    def norm_tile(l, g, t, gi, final=False):
        bank, d_b = next_bank()
        for c in range(NCH):
            norm_sq_chunk(g, t, c, bank, d_b)
        norm_finish(l, g, t, gi, bank, d_b, final)

    def conv_taps(G, width, wvec, l, c, k0, outbuf, d_out):
        xs3 = xr_sb[:, SB0:SB0 + NS * 7].rearrange("p (s t) -> p s t", t=7)
        xo = outbuf[:, TPG:TPG + TS].rearrange("p (s t) -> p s t", t=ST)
        rd = list(d_xr) + [d_xrh, d_vec]
        for k in range(k0, width):
            wk = V(l, wvec + (width - 1 - k), c)
            if k == 0:
                ts(outbuf[:, 0:TPG], xr_sb[:, 3:3 + TPG], wk, ALU.mult, rd, list(d_out))
            else:
                stt(outbuf[:, 0:TPG], xr_sb[:, 3 - k:3 - k + TPG], wk, outbuf[:, 0:TPG], ALU.mult, ALU.add,
                    rd + list(d_out), list(d_out))
            if G["samp"]:
                if k == 0:
                    ts(xo, xs3[:, :, 3:7], wk, ALU.mult, rd, [d_out[2]])
                else:
                    stt(xo, xs3[:, :, 3 - k:7 - k], wk, xo, ALU.mult, ALU.add, rd + [d_out[2]], [d_out[2]])

    def mixer(l, g):
        G = GROUPS[g]
        tiles = G["tiles"]
        samp_g = G["samp"]
        xs3 = xr_sb[:, SB0:SB0 + NS * 7].rearrange("p (s t) -> p s t", t=7)
        part2b_prev = [None]
        NG = G["n"]
        ALLA, ALLB, ALLC, ALLD = list(d_bA), list(d_bB), list(d_bC), list(d_bD)

        for c in range(NCH):
            wb, d_wb = next_batch()
            prefetch()
            it = lambda i, wb=wb: wb[:, i * 1024:(i + 1) * 1024]
            gD, d_gD = (bD, d_bD) if c % 2 == 0 else (bD1, d_bD1)
            ALLG = list(d_gD)

            def mmw(item, t, d_wb=d_wb):
                o, n = tiles[t]
                bank, d_b = next_bank()
                mm_group(bank, d_b, n, [(item[:, k * 128:(k + 1) * 128], u_sb[:, k, o:o + n]) for k in range(NCH)],
                         [d_wb] + [d_u[k][t] for k in range(NCH)])
                return bank, d_b, o, n

            if g == 0:
                S.op("dve", lambda e: e.memset(xr_sb[:, 0:3], 0.0), reads=[], writes=[d_xrh], small=True)
            else:
                cp(xr_sb[:, 0:3], xhalo[:, c, :], [d_xhalo], [d_xrh])
            if samp_g:
                cp(xs3[:, :, 0:3], st_in[:, c, 16:64].rearrange("p (s t) -> p s t", t=3), [d_stin], [d_xrh])
            xrb = []
            for t in range(3):
                bank, d_b, o, n = mmw(it(0), t)
                xrb.append((bank, d_b, o, n))
                samp = samp_g and t == 2
                npz = TW if samp else n
                act(xr_sb[:, 3 + o:3 + o + npz], bank[:, 0:npz], AF.Copy, [d_b], [d_xr[t]])
                if samp:
                    act(xs3[:, :, 3:7], bank[:, TW:TW + TS].rearrange("p (s t) -> p s t", t=ST), AF.Copy, [d_b], [d_xr[t]])
            for t, (bank, d_b, o, n) in enumerate(xrb):
                act(xc[:, o:o + n], bank[:, 0:n], AF.Identity, [d_b, d_vec], [d_xc[t]], bias=V(l, 5, c), scale=V(l, 4, c))
            conv_taps(G, 4, 1, l, c, 1, xc, d_xc)
            if g == 0:
                cp(xhalo[:, c, :], xr_sb[:, 3 + TPG - 3:3 + TPG], [d_xr[2]], [d_xhalo])
            else:
                cp(st_out[:, c, 1:4], xr_sb[:, 3 + TPG - 3:3 + TPG], [d_xr[2]], [d_stout])
                cp(st_out[:, c, 22:70].rearrange("p (s t) -> p s t", t=3), xs3[:, :, 4:7], [d_xr[2]], [d_stout])

            if part2b_prev[0] is not None:
                part2b_prev[0][2]()
            for t in range(3):
                bank, d_b, o, n = mmw(it(1), t)
                act(gD[:, o:o + n], bank[:, 0:n], AF.Copy, [d_b], [d_gD[t]])
            act(xcb[:, 0:NG], xc[:, 0:NG], AF.Copy, list(d_xc), list(d_xcb))

            if g == 1:
                cp(xr_sb[:, 1:3], chhalo[:, c, :], [d_chhalo], [d_xrh])
            if samp_g:
                cp(xs3[:, :, 1:3], st_in[:, c, 64:96].rearrange("p (s t) -> p s t", t=2), [d_stin], [d_xrh])
            for t in range(3):
                bank, d_b, o, n = mmw(it(2), t)
                samp = samp_g and t == 2
                npz = TW if samp else n
                act(xr_sb[:, 3 + o:3 + o + npz], bank[:, 0:npz], AF.Copy, [d_b], [d_xr[t]])
                if samp:
                    act(xs3[:, :, 3:7], bank[:, TW:TW + TS].rearrange("p (s t) -> p s t", t=ST), AF.Copy, [d_b], [d_xr[t]])
            if part2b_prev[0] is not None:
                part2b_prev[0][0]()
            for t in range(3):
                bank, d_b, o, n = mmw(it(3), t)
                samp = samp_g and t == 2
                npz = TW if samp else n
                tt(xr_sb[:, 3 + o:3 + o + npz], xr_sb[:, 3 + o:3 + o + npz], bank[:, 0:npz], ALU.mult, [d_b, d_xr[t]], [d_xr[t]])
                if samp:
                    tt(xs3[:, :, 3:7], xs3[:, :, 3:7], bank[:, TW:TW + TS].rearrange("p (s t) -> p s t", t=ST), ALU.mult,
                       [d_b, d_xr[t]], [d_xr[t]])
            conv_taps(G, 3, 9, l, c, 0, vc, d_vc)
            for t in range(3):
                bank, d_b, o, n = mmw(it(4), t)
                tt(R[:, 8 + c, o:o + n], bank[:, 0:n], vc[:, o:o + n], ALU.mult, [d_b, d_vc[t]], [d_R[8 + c][t]])
            if g == 0:
                cp(chhalo[:, c, :], xr_sb[:, 3 + TPG - 2:3 + TPG], [d_xr[2]], [d_chhalo])
            else:
                cp(st_out[:, c, 4:6], xr_sb[:, 3 + TPG - 2:3 + TPG], [d_xr[2]], [d_stout])
                cp(st_out[:, c, 70:102].rearrange("p (s t) -> p s t", t=2), xs3[:, :, 5:7], [d_xr[2]], [d_stout])
            if part2b_prev[0] is not None:
                part2b_prev[0][1]()
                part2b_prev[0] = None

            for t, (o, n) in enumerate(tiles):
                br, d_br = next_bank()
                S.op("pe", lambda e, br=br, n=n, o=o, c=c: e.matmul(out=br[:, 0:n], lhsT=gw[:, c * 128:(c + 1) * 128],
                                                                     rhs=xcb[:, o:o + n], start=True, stop=True),
                     reads=[d_gw, d_xcb[t]], writes=[d_br])
                bi, d_bi = next_bank()
                S.op("pe", lambda e, bi=bi, n=n, o=o, c=c: e.matmul(out=bi[:, 0:n], lhsT=gw[:, 1024 + c * 128:1024 + (c + 1) * 128],
                                                                     rhs=xcb[:, o:o + n], start=True, stop=True),
                     reads=[d_gw, d_xcb[t]], writes=[d_bi])
                act(bA[:, o:o + n], br[:, 0:n], AF.Tanh, [d_br, d_der], [d_bA[t]], bias=DER(0, l, c), scale=0.5)
                act(bC[:, o:o + n], bi[:, 0:n], AF.Tanh, [d_bi, d_der], [d_bC[t]], bias=DER(1, l, c), scale=0.5)
            stt(bC[:, 0:NG], bC[:, 0:NG], 1.0, xc[:, 0:NG], ALU.add, ALU.mult, ALLC + list(d_xc), ALLC)

            def part2a_tail(c=c):
                act(bB[:, 0:NG], bA[:, 0:NG], AF.Tanh, ALLA + [d_der], ALLB, bias=DER(3, l, c), scale=DER(3, l, c))
                act(bA[:, 0:NG], bA[:, 0:NG], AF.Exp, ALLA + ALLB + [d_der], ALLA, bias=DER(2, l, c), scale=DER(2, l, c))

            def part2b_act(c=c, gD=gD, ALLG=ALLG):
                act(bB[:, 0:NG], bB[:, 0:NG], AF.Sqrt, ALLB, ALLB, scale=0.25)
                act(gD[:, 0:NG], gD[:, 0:NG], AF.Gelu_apprx_tanh, ALLG, ALLG)

            def part2b_dve(c=c, gD=gD, ALLG=ALLG):
                stt(bB[:, 0:NG], bA[:, 0:NG], 1.0, bB[:, 0:NG], ALU.add, ALU.mult, ALLA + ALLB, ALLB)
                stt(bC[:, 0:NG], bB[:, 0:NG], 0.5e-6, bC[:, 0:NG], ALU.max, ALU.mult, ALLB + ALLC, ALLC)
                if samp_g:
                    a3 = bA[:, TPG:TPG + TS].rearrange("p (s t) -> p s t", t=ST)
                    u3 = bC[:, TPG:TPG + TS].rearrange("p (s t) -> p s t", t=ST)
                    h3 = bB[:, TPG:TPG + TS].rearrange("p (s t) -> p s t", t=ST)
                    tt(tmp16[:], a3[:, :, 0], st_in[:, c, 0:NS], ALU.mult, [d_bA[2], d_stin], [d_tmp16], small=True)
                    tt(u3[:, :, 0], u3[:, :, 0], tmp16[:], ALU.add, [d_tmp16, d_bC[2]], [d_bC[2]], small=True)
                    S.op("dve", lambda e, a3=a3: e.memset(a3[:, :, 0], 0.0), reads=[d_tmp16], writes=[d_bA[2]], small=True)
                for t, (o, n) in enumerate(tiles):
                    samp = samp_g and t == 2
                    npz = TW if samp else n
                    if t == 0:
                        init = 0.0 if g == 0 else hcar[:, c:c + 1]
                        rdi = [] if g == 0 else [d_hcar]
                    else:
                        init = bB[:, o - 1:o]
                        rdi = [d_bB[t - 1]]
                    S.op("dve", lambda e, o=o, npz=npz, init=init: e.tensor_tensor_scan(
                        out=bB[:, o:o + npz], data0=bA[:, o:o + npz], data1=bC[:, o:o + npz], initial=init,
                        op0=ALU.mult, op1=ALU.add), reads=[d_bA[t], d_bC[t], d_bB[t]] + rdi, writes=[d_bB[t]])
                    if samp:
                        S.op("dve", lambda e: e.tensor_tensor_scan(
                            out=bB[:, TPG:TPG + TS], data0=bA[:, TPG:TPG + TS], data1=bC[:, TPG:TPG + TS], initial=0.0,
                            op0=ALU.mult, op1=ALU.add), reads=[d_bA[t], d_bC[t], d_bB[t]], writes=[d_bB[t]], small=True)
                        cp(st_out[:, c, 0:1], bB[:, TPG - 1:TPG], [d_bB[t]], [d_stout])
                        cp(st_out[:, c, 6:22], h3[:, :, ST - 1], [d_bB[t]], [d_stout])
                    if g == 0 and t == 2:
                        cp(hcar[:, c:c + 1], bB[:, TPG - 1:TPG], [d_bB[t]], [d_hcar])
                tt(R[:, c, 0:NG], bB[:, 0:NG], gD[:, 0:NG], ALU.mult, ALLB + ALLG, list(d_R[c]))

            part2b_prev[0] = (part2b_act, part2b_dve, part2a_tail)
        return part2b_prev[0]

    def merge(l, g, tail):
        G = GROUPS[g]
        tiles = G["tiles"]
        for c in range(NCH):
            wb, d_wb = next_batch()
            prefetch()
            it = lambda i, wb=wb: wb[:, i * 1024:(i + 1) * 1024]
            sA, dA, sC, dC = (xc, d_xc, vc, d_vc) if c == 0 else (bA, d_bA, bC, d_bC)
            for gi_item, dsig, sigbuf in ((0, dA, sA), (1, dC, sC)):
                if c == 0 and gi_item == 1 and tail is not None:
                    pass
                for t, (o, n) in enumerate(tiles):
                    bank, d_b = next_bank()
                    mm_group(bank, d_b, n, [(it(gi_item)[:, k * 128:(k + 1) * 128], u_sb[:, k, o:o + n]) for k in range(NCH)],
                             [d_wb] + [d_u[k][t] for k in range(NCH)])
                    if c == 0:
                        S.op("dve", lambda e, sigbuf=sigbuf, bank=bank, o=o, n=n: e.tensor_copy(out=sigbuf[:, o:o + n], in_=bank[:, 0:n]),
                             reads=[d_b], writes=[dsig[t]])
                        act(sigbuf[:, o:o + n], sigbuf[:, o:o + n], AF.Sigmoid, [dsig[t]], [dsig[t]])
                    else:
                        act(sigbuf[:, o:o + n], bank[:, 0:n], AF.Sigmoid, [d_b], [dsig[t]])
            if c == 0:
                tail[1]()
            for t, (o, n) in enumerate(tiles):
                bank, d_b = next_bank()
                mm_group(bank, d_b, n, [(it(2)[:, k * 128:(k + 1) * 128], R[:, 8 + k, o:o + n]) for k in range(NCH)],
                         [d_wb] + [d_R[8 + k][t] for k in range(NCH)])
                tt(bD[:, o:o + n], bank[:, 0:n], sC[:, o:o + n], ALU.mult, [d_b, dC[t]], [d_bD[t]])
            for t, (o, n) in enumerate(tiles):
                bank, d_b = next_bank()
                mm_group(bank, d_b, n, [(it(3)[:, k * 128:(k + 1) * 128], R[:, k, o:o + n]) for k in range(NCH)],
                         [d_wb] + [d_R[k][t] for k in range(NCH)])
                tt(bB[:, o:o + n], bank[:, 0:n], sA[:, o:o + n], ALU.mult, [d_b, dA[t]], [d_bB[t]])
            for t, (o, n) in enumerate(tiles):
                tt(R[:, 16 + c, o:o + n], bB[:, o:o + n], bD[:, o:o + n], ALU.add, [d_bB[t], d_bD[t]], [d_R[16 + c][t]])

    def wout_norm2(l, g):
        G = GROUPS[g]
        tiles = G["tiles"]
        nb = [hold_bank() for _ in range(3)]
        for bi in range(2):
            wb, d_wb = next_batch()
            prefetch()
            for j in range(4):
                oc = bi * 4 + j
                item = wb[:, j * 1024:(j + 1) * 1024]
                for t, (o, n) in enumerate(tiles):
                    gt = g * 3 + t
                    go = G["off"] + o
                    bank, d_b = next_bank()
                    mm_group(bank, d_b, n, [(item[:, k * 128:(k + 1) * 128], R[:, 16 + k, o:o + n]) for k in range(NCH)],
                             [d_wb] + [d_R[16 + k][t] for k in range(NCH)])
                    tt(x_sb[:, oc, go:go + n], x_sb[:, oc, go:go + n], bank[:, 0:n], ALU.add, [d_b, d_x[oc][gt]], [d_x[oc][gt]])
                if oc > 0:
                    for t in range(3):
                        norm_sq_chunk(g, t, oc - 1, nb[t][0], nb[t][1])
        for t in range(3):
            norm_sq_chunk(g, t, NCH - 1, nb[t][0], nb[t][1])
        for t in range(3):
            norm_finish(l, g, t, 12, nb[t][0], nb[t][1])
            release_bank(nb[t])

    def ffn(l, g, nxt):
        G = GROUPS[g]
        tiles = G["tiles"]
        sbufs = [(bB, d_bB), (bC, d_bC), (bD, d_bD)]
        for bi in range(11):
            wb, d_wb = next_batch()
            prefetch()
            for jj in range(2):
                j = bi * 2 + jj
                gate = wb[:, (2 * jj) * 1024:(2 * jj + 1) * 1024]
                up = wb[:, (2 * jj + 1) * 1024:(2 * jj + 2) * 1024]
                sbuf_, dsb = sbufs[j % 3]
                for t, (o, n) in enumerate(tiles):
                    bank, d_b = next_bank()
                    mm_group(bank, d_b, n, [(gate[:, k * 128:(k + 1) * 128], u_sb[:, k, o:o + n]) for k in range(NCH)],
                             [d_wb] + [d_u[k][t] for k in range(NCH)])
                    act(sbuf_[:, o:o + n], bank[:, 0:n], AF.Silu, [d_b], [dsb[t]])
                for t, (o, n) in enumerate(tiles):
                    bank, d_b = next_bank()
                    mm_group(bank, d_b, n, [(up[:, k * 128:(k + 1) * 128], u_sb[:, k, o:o + n]) for k in range(NCH)],
                             [d_wb] + [d_u[k][t] for k in range(NCH)])
                    tt(R[:, j, o:o + n], bank[:, 0:n], sbuf_[:, o:o + n], ALU.mult, [d_b, dsb[t]], [d_R[j][t]])
        hoist = {1: 0, 3: 1, 5: 2}
        for oc in range(NCH):
            wb, d_wb = next_batch()
            prefetch()
            for t, (o, n) in enumerate(tiles):
                gt = g * 3 + t
                go = G["off"] + o
                bank, d_b = next_bank()
                mm_group(bank, d_b, n, [(wb[:, k * 128:(k + 1) * 128], R[:, k, o:o + n]) for k in range(NFF)],
                         [d_wb] + [d_R[k][t] for k in range(NFF)])
                tt(x_sb[:, oc, go:go + n], x_sb[:, oc, go:go + n], bank[:, 0:n], ALU.add, [d_b, d_x[oc][gt]], [d_x[oc][gt]])
            if nxt is not None and oc in hoist:
                norm_tile(nxt[0], nxt[1], hoist[oc], 0)

    seq = [(l, g) for l in range(depth) for g in range(2)]
    for t in range(3):
        norm_tile(0, 0, t, 0)
    for i, (l, g) in enumerate(seq):
        if g == 0:
            S.dma("pool", lambda e, l=l: e.dma_start(out=gw[:], in_=ws_d[l, :, 0:W_GATES], max_dma_last_dim=4096), "gwl",
                  writes=[d_gw])
            S.dma("sp", lambda e, l=l: e.dma_start(out=st_in[:].rearrange("p c n -> p (c n)"), in_=stin_d[l]), "ldst",
                  writes=[d_stin])
        tail = mixer(l, g)
        tail[2]()
        tail[0]()
        merge(l, g, tail)
        if g == 1:
            S.dma("sp", lambda e, l=l: e.dma_start(out=sto_d[l], in_=st_out[:].rearrange("p c n -> p (c n)")), "st",
                  reads=[d_stout])
        wout_norm2(l, g)
        ffn(l, g, seq[i + 1] if i + 1 < len(seq) else None)

    for g in range(2):
        for t in range(3):
            norm_tile(0, g, t, 0, final=True)
    for c0 in range(0, NCH, 2):
        S.dma("sp", lambda e, c0=c0: e.dma_start(out=y_d[:, c0:c0 + 2, :], in_=x_sb[:, c0:c0 + 2, :]), "st",
              reads=[d_x[c][t] for c in (c0, c0 + 1) for t in range(6)])

    sem_keys = list(Sched.ENGS) + sorted(S.dma_cnt.keys())
    sems = {k: es.enter_context(nc.semaphore("s_" + k)) for k in sem_keys}

    def emit(name, e):
        for waits, fn, key, inc in S.streams[name]:
            for k, v in waits:
                e.wait_ge(sems[k], v)
            ins = fn(e)
            ins.then_inc(sems[key], inc)
        if name == "sp":
            for k, v in S.dma_cnt.items():
                e.wait_ge(sems[k], v)

    with nc.Block() as block:
        @block.tensor
        def _(e):
            emit("pe", e)

        @block.scalar
        def _(e):
            emit("act", e)

        @block.vector
        def _(e):
            emit("dve", e)

        @block.gpsimd
        def _(e):
            emit("pool", e)

        @block.sync
        def _(e):
            emit("sp", e)
    es.close()
    return nc


def _fm(a):
    T = a.shape[0]
    return np.ascontiguousarray(a.reshape(T, NCH, 128).transpose(2, 1, 0))


def _pack_weights(inp):
    ws = np.zeros((DEPTH, 128, WS_LAYER), np.float32)
    for l in range(DEPTH):
        off = 0
        gwl = np.zeros((128, 2, 8, 128), np.float32)
        for gi, name in enumerate(("gate_a_w", "gate_x_w")):
            w = inp[name][l]
            w2 = w.reshape(8, 2, 64, 64)
            gwl[0:64, gi, :, 0:64] = w2[:, 0].transpose(1, 0, 2)
            gwl[64:128, gi, :, 64:128] = w2[:, 1].transpose(1, 0, 2)
        ws[l, :, off:off + W_GATES] = gwl.reshape(128, W_GATES); off += W_GATES

        def item(W, col0):
            K = W.shape[0]
            blk = W[:, col0:col0 + 128].reshape(K // 128, 128, 128)
            return blk.transpose(1, 0, 2).reshape(128, K)

        w_in = inp["w_in"][l]
        for c in range(8):
            for s in ("xr", "gr", "cc", "hc", "bc"):
                ws[l, :, off:off + 1024] = item(w_in, OFF[s] + c * 128); off += 1024
        wa, wbm = inp["w_branch_a"][l], inp["w_branch_b"][l]
        for c in range(8):
            ws[l, :, off:off + 1024] = item(w_in, OFF["ga"] + c * 128); off += 1024
            ws[l, :, off:off + 1024] = item(w_in, OFF["gb"] + c * 128); off += 1024
            ws[l, :, off:off + 1024] = item(wbm, c * 128); off += 1024
            ws[l, :, off:off + 1024] = item(wa, c * 128); off += 1024
        wo = inp["w_out"][l]
        for c in range(8):
            ws[l, :, off:off + 1024] = item(wo, c * 128); off += 1024
        wg, wu = inp["w_ff_gate"][l], inp["w_ff_up"][l]
        for j in range(NFF):
            ws[l, :, off:off + 1024] = item(wg, j * 128); off += 1024
            ws[l, :, off:off + 1024] = item(wu, j * 128); off += 1024
        wd = inp["w_ff_down"][l]
        for c in range(8):
            ws[l, :, off:off + W_DN] = item(wd, c * 128); off += W_DN
        assert off == WS_LAYER
    return ws


def _pack_vecs(inp):
    v = np.zeros((128, NVEC * DEPTH * 8 + 8), np.float32)
    rows = []
    for l in range(DEPTH):
        rows.append([inp["norm1_g"][l]] + [inp["rnn_conv_w"][l, k] for k in range(4)] + [inp["rnn_conv_b"][l],
                    inp["gate_a_b"][l], inp["gate_x_b"][l], inp["lru_lambda"][l]] +
                    [inp["sc_conv_w"][l, k] for k in range(3)] + [inp["norm2_g"][l]])
    for i in range(NVEC):
        for l in range(DEPTH):
            o = (i * DEPTH + l) * 8
            v[:, o:o + 8] = np.asarray(rows[l][i]).reshape(8, 128).T
    v[:, NVEC * DEPTH * 8:] = np.asarray(inp["final_norm_g"]).reshape(8, 128).T
    return v


def kernel(**inputs):
    inp = {k: np.asarray(v) for k, v in inputs.items()}
    nc = build_program(DEPTH)
    ws = _pack_weights(inp)
    vecs = _pack_vecs(inp)
    meta = _fm(inp["meta_tokens"].astype(np.float32))
    in_maps = []
    for i in range(NCORES):
        xp = _fm(inp["x_prompt"][i])
        xs = _fm(inp["x_sample"][i * NS:(i + 1) * NS].reshape(TS, D))
        stin = np.zeros((DEPTH, 128, NCH, NSTI), np.float32)
        h0 = inp["state_rnn_h"][:, i * NS:(i + 1) * NS]
        rc = inp["state_rnn_conv"][:, i * NS:(i + 1) * NS]
        sc = inp["state_sc_conv"][:, i * NS:(i + 1) * NS]
        stin[:, :, :, 0:16] = h0.reshape(DEPTH, NS, NCH, 128).transpose(0, 3, 2, 1)
        stin[:, :, :, 16:64] = rc.reshape(DEPTH, NS, 3, NCH, 128).transpose(0, 4, 3, 1, 2).reshape(DEPTH, 128, NCH, 48)
        stin[:, :, :, 64:96] = sc.reshape(DEPTH, NS, 2, NCH, 128).transpose(0, 4, 3, 1, 2).reshape(DEPTH, 128, NCH, 32)
        in_maps.append(dict(xp=xp, meta=meta, xs=xs, stin=np.ascontiguousarray(stin.reshape(DEPTH, 128, NCH * NSTI)),
                            vecs=vecs, ws=ws))
    res = run_bass_kernel_spmd(nc, in_maps, core_ids=list(range(NCORES)))
    y_prompt = np.zeros((NCORES, SEQ, D), np.float32)
    y_sample = np.zeros((NCORES * NS, ST, D), np.float32)
    rnn_h_p = np.zeros((DEPTH, NCORES, D), np.float32)
    rnn_c_p = np.zeros((DEPTH, NCORES, 3, D), np.float32)
    sc_c_p = np.zeros((DEPTH, NCORES, 2, D), np.float32)
    rnn_h_s = np.zeros((DEPTH, NCORES * NS, D), np.float32)
    rnn_c_s = np.zeros((DEPTH, NCORES * NS, 3, D), np.float32)
    sc_c_s = np.zeros((DEPTH, NCORES * NS, 2, D), np.float32)
    for i in range(NCORES):
        r = res.results[i]
        y = np.asarray(r["y"]).reshape(128, NCH, TTOT)
        yt = y.transpose(2, 1, 0).reshape(TTOT, D)
        y_prompt[i] = yt[NMETA:TP]
        y_sample[i * NS:(i + 1) * NS] = yt[TP:].reshape(NS, ST, D)
        so = np.asarray(r["sto"]).reshape(DEPTH, 128, NCH, NSTO)
        so = so.transpose(0, 3, 2, 1).reshape(DEPTH, NSTO, D)
        rnn_h_p[:, i] = so[:, 0]
        rnn_c_p[:, i] = so[:, 1:4]
        sc_c_p[:, i] = so[:, 4:6]
        rnn_h_s[:, i * NS:(i + 1) * NS] = so[:, 6:22]
        rnn_c_s[:, i * NS:(i + 1) * NS] = so[:, 22:70].reshape(DEPTH, NS, 3, D)
        sc_c_s[:, i * NS:(i + 1) * NS] = so[:, 70:102].reshape(DEPTH, NS, 2, D)
    return (y_prompt, y_sample, rnn_h_p, rnn_c_p, sc_c_p, rnn_h_s, rnn_c_s, sc_c_s)
```

```python
import numpy as np
from contextlib import ExitStack
import concourse.bass as bass
import concourse.mybir as mybir
from concourse.bass_utils import run_bass_kernel_spmd

F32 = mybir.dt.float32
BF16 = mybir.dt.bfloat16
AF = mybir.ActivationFunctionType
ALU = mybir.AluOpType

D = 1024
NCH = 8
DFF = 2816
NFF = 22
DEPTH = 4
NMETA = 16
SEQ = 2048
TP = NMETA + SEQ
NS = 16
ST = 4
TS = NS * ST
TTOT = TP + TS
NCORES = 8
EPS = 1e-6
OFF = dict(xr=0, gr=1024, bc=2048, cc=3072, hc=4096, ga=5120, gb=6144)
TPG = TP // 2
TW = 344
GROUPS = [
    dict(off=0, n=TPG, tiles=[(0, TW), (TW, TW), (2 * TW, TW)], samp=False),
    dict(off=TPG, n=TPG + TS, tiles=[(0, TW), (TW, TW), (2 * TW, TW + TS)], samp=True),
]
GW = TPG + TS
NVEC = 13
NSTI = 96
NSTO = 102
W_GATES = 2 * 8 * 128
W_MIX = 5 * 1024
W_MRG = 4 * 1024
W_OUT = 4 * 1024
W_FFN = 4 * 1024
W_DN = NFF * 128
WS_LAYER = W_GATES + 8 * W_MIX + 8 * W_MRG + 2 * W_OUT + 11 * W_FFN + 8 * W_DN
WBUF = 5120
XRW = 1160
SB0 = 1040
NSQ = 3

SAME_SYNC_ALL = True


class Dep:
    __slots__ = ("w", "r")

    def __init__(self):
        self.w = None
        self.r = {}


class Sched:
    ENGS = ("pe", "act", "dve", "pool", "sp")

    def __init__(self):
        self.streams = {e: [] for e in self.ENGS}
        self.tick = {e: 0 for e in self.ENGS}
        self.known = {e: {} for e in self.ENGS}
        self.dma_cnt = {}

    def _waits(self, eng, reads, writes, small):
        waits = {}

        def need(k, v):
            if k == eng and (eng == "pe" or not (small or SAME_SYNC_ALL)):
                return
            if self.known[eng].get(k, 0) >= v:
                return
            if waits.get(k, 0) < v:
                waits[k] = v

        for t in reads:
            if t.w is not None:
                need(*t.w)
        for t in writes:
            if t.w is not None:
                need(*t.w)
            for k, v in t.r.items():
                need(k, v)
        for k, v in waits.items():
            self.known[eng][k] = v
        return list(waits.items())

    def _mark(self, tok, reads, writes):
        k, v = tok
        for t in reads:
            t.r[k] = v
        for t in writes:
            t.w = tok
            t.r = {}

    def op(self, eng, fn, reads=(), writes=(), small=False):
        waits = self._waits(eng, reads, writes, small)
        self.tick[eng] += 1
        tok = (eng, self.tick[eng])
        self.streams[eng].append((waits, fn, eng, 1))
        self._mark(tok, reads, writes)

    def dma(self, eng, fn, semkey, reads=(), writes=()):
        waits = self._waits(eng, reads, writes, False)
        self.dma_cnt[semkey] = self.dma_cnt.get(semkey, 0) + 16
        tok = (semkey, self.dma_cnt[semkey])
        self.streams[eng].append((waits, fn, semkey, 16))
        self._mark(tok, reads, writes)


def build_program(depth=DEPTH):
    nc = bass.Bass("TRN2", target_bir_lowering=False)
    xp_d = nc.dram_tensor("xp", [128, NCH, SEQ], F32, kind="ExternalInput").ap()
    meta_d = nc.dram_tensor("meta", [128, NCH, NMETA], F32, kind="ExternalInput").ap()
    xs_d = nc.dram_tensor("xs", [128, NCH, TS], F32, kind="ExternalInput").ap()
    stin_d = nc.dram_tensor("stin", [DEPTH, 128, NCH * NSTI], F32, kind="ExternalInput").ap()
    vecs_d = nc.dram_tensor("vecs", [128, NVEC * DEPTH * 8 + 8], F32, kind="ExternalInput").ap()
    ws_d = nc.dram_tensor("ws", [DEPTH, 128, WS_LAYER], F32, kind="ExternalInput").ap()
    y_d = nc.dram_tensor("y", [128, NCH, TTOT], F32, kind="ExternalOutput").ap()
    sto_d = nc.dram_tensor("sto", [DEPTH, 128, NCH * NSTO], F32, kind="ExternalOutput").ap()

    S = Sched()
    es = ExitStack()

    def sb(name, shape, dt):
        return es.enter_context(nc.sbuf_tensor(name, shape, dt))

    x_sb = sb("x_sb", [128, NCH, TTOT], F32)
    u_sb = sb("u_sb", [128, NCH, GW], BF16)
    R = sb("R", [128, 24, GW], BF16)
    wbuf = [sb("wbuf0", [128, WBUF], BF16), sb("wbuf1", [128, WBUF], BF16)]
    gw = sb("gw", [128, W_GATES], BF16)
    vec_sb = sb("vec_sb", [128, NVEC * DEPTH * 8 + 8], F32)
    der_sb = sb("der_sb", [128, 4 * DEPTH * 8], F32)
    dtmp = [sb("dtmp%d" % i, [128, DEPTH * 8], F32) for i in range(6)]
    ones_bf = sb("ones_bf", [128, 128], BF16)
    xr_sb = sb("xr_sb", [128, XRW], F32)
    xc = sb("xc", [128, GW], F32)
    xcb = sb("xcb", [128, GW], BF16)
    vc = sb("vc", [128, GW], F32)
    bA = sb("bA", [128, GW], F32)
    bB = sb("bB", [128, GW], F32)
    bC = sb("bC", [128, GW], F32)
    bD = sb("bD", [128, GW], F32)
    bD1 = sb("bD1", [128, GW], F32)
    sqb = [sb("sqb%d" % i, [128, 416], BF16) for i in range(NSQ)]
    st_in = sb("st_in", [128, NCH, NSTI], F32)
    st_out = sb("st_out", [128, NCH, NSTO], F32)
    hcar = sb("hcar", [128, NCH], F32)
    xhalo = sb("xhalo", [128, NCH, 3], F32)
    chhalo = sb("chhalo", [128, NCH, 2], F32)
    tmp16 = sb("tmp16", [128, NS], F32)
    banks = [es.enter_context(nc.psum_tensor("ps%d" % i, [128, 512], F32)) for i in range(8)]

    d_x = [[Dep() for _ in range(6)] for _ in range(NCH)]
    d_u = [[Dep() for _ in range(3)] for _ in range(NCH)]
    d_R = [[Dep() for _ in range(3)] for _ in range(24)]
    d_wbuf = [Dep(), Dep()]
    d_gw = Dep()
    d_vec = Dep()
    d_der = Dep()
    d_dtmp = [Dep() for _ in range(6)]
    d_ones = Dep()
    d_xr = [Dep() for _ in range(3)]
    d_xrh = Dep()
    d_xc = [Dep() for _ in range(3)]
    d_xcb = [Dep() for _ in range(3)]
    d_vc = [Dep() for _ in range(3)]
    d_bA = [Dep() for _ in range(3)]
    d_bB = [Dep() for _ in range(3)]
    d_bC = [Dep() for _ in range(3)]
    d_bD = [Dep() for _ in range(3)]
    d_bD1 = [Dep() for _ in range(3)]
    d_sqb = [Dep() for _ in range(NSQ)]
    d_stin = Dep()
    d_stout = Dep()
    d_hcar = Dep()
    d_xhalo = Dep()
    d_chhalo = Dep()
    d_tmp16 = Dep()
    d_bank = [Dep() for _ in range(8)]
    d_y = Dep()

    bank_ctr = [0]

    held = set()

    def next_bank():
        while True:
            b = bank_ctr[0] % 8
            bank_ctr[0] += 1
            if b not in held:
                return banks[b], d_bank[b]

    def hold_bank():
        bk = next_bank()
        held.add(banks.index(bk[0]))
        return bk

    def release_bank(bk):
        held.discard(banks.index(bk[0]))

    def V(l, i, c):
        o = (i * DEPTH + l) * 8 + c
        return vec_sb[:, o:o + 1]

    def DER(kind, l, c):
        o = kind * DEPTH * 8 + l * 8 + c
        return der_sb[:, o:o + 1]

    batches = []
    for l in range(depth):
        for g in range(2):
            off = W_GATES
            for _ in range(8):
                batches.append((l, off, W_MIX)); off += W_MIX
            for _ in range(8):
                batches.append((l, off, W_MRG)); off += W_MRG
            for _ in range(2):
                batches.append((l, off, W_OUT)); off += W_OUT
            for _ in range(11):
                batches.append((l, off, W_FFN)); off += W_FFN
            for _ in range(8):
                batches.append((l, off, W_DN)); off += W_DN
            assert off == WS_LAYER
    bstate = dict(issued=0, used=0)

    def issue_batch():
        i = bstate["issued"]
        if i >= len(batches):
            return
        l, off, n = batches[i]
        b = i % 2
        src = ws_d[l, :, off:off + n]
        dst = wbuf[b][:, 0:n]
        S.dma("pool", lambda e, src=src, dst=dst: e.dma_start(out=dst, in_=src, max_dma_last_dim=4096),
              "w%d" % b, reads=(), writes=(d_wbuf[b],))
        bstate["issued"] += 1

    def next_batch():
        i = bstate["used"]
        while bstate["issued"] <= min(i, len(batches) - 1):
            issue_batch()
        b = i % 2
        bstate["used"] += 1
        return wbuf[b], d_wbuf[b]

    def prefetch():
        if bstate["issued"] < bstate["used"] + 1:
            issue_batch()

    def mm_group(bank, d_b, n, pairs, reads):
        def fn(e, bank=bank, n=n, pairs=pairs):
            last = None
            for i, (lt, rh) in enumerate(pairs):
                last = e.matmul(out=bank[:, 0:n], lhsT=lt, rhs=rh, start=(i == 0), stop=(i == len(pairs) - 1))
            return last
        S.op("pe", fn, reads=reads, writes=(d_b,))

    def act(out, in_, func, reads, writes, bias=None, scale=None, small=False):
        kw = {}
        if bias is not None:
            kw["bias"] = bias
        if scale is not None:
            kw["scale"] = scale
        S.op("act", lambda e: e.activation(out=out, in_=in_, func=func, **kw), reads=reads, writes=writes, small=small)

    def tt(out, in0, in1, op, reads, writes, small=False):
        S.op("dve", lambda e: e.tensor_tensor(out=out, in0=in0, in1=in1, op=op), reads=reads, writes=writes, small=small)

    def ts(out, in0, s1, op0, reads, writes, s2=None, op1=None, small=False):
        if op1 is None:
            S.op("dve", lambda e: e.tensor_scalar(out=out, in0=in0, scalar1=s1, scalar2=None, op0=op0),
                 reads=reads, writes=writes, small=small)
        else:
            S.op("dve", lambda e: e.tensor_scalar(out=out, in0=in0, scalar1=s1, scalar2=s2, op0=op0, op1=op1),
                 reads=reads, writes=writes, small=small)

    def stt(out, in0, scalar, in1, op0, op1, reads, writes, small=False):
        S.op("dve", lambda e: e.scalar_tensor_tensor(out=out, in0=in0, scalar=scalar, in1=in1, op0=op0, op1=op1),
             reads=reads, writes=writes, small=small)

    def cp(out, in_, reads, writes, small=True):
        S.op("dve", lambda e: e.tensor_copy(out=out, in_=in_), reads=reads, writes=writes, small=small)

    S.dma("sp", lambda e: e.dma_start(out=vec_sb[:], in_=vecs_d), "ldv", writes=(d_vec,))
    for c0 in range(0, NCH, 2):
        S.dma("sp", lambda e, c0=c0: e.dma_start(out=x_sb[:, c0:c0 + 2, NMETA:TP], in_=xp_d[:, c0:c0 + 2, :]), "ldx%d" % c0,
              writes=[d_x[c][t] for c in (c0, c0 + 1) for t in range(6)])
    S.dma("sp", lambda e: e.dma_start(out=x_sb[:, :, 0:NMETA], in_=meta_d), "ldm",
          writes=[d_x[c][0] for c in range(NCH)])
    S.dma("sp", lambda e: e.dma_start(out=x_sb[:, :, TP:TTOT], in_=xs_d), "lds",
          writes=[d_x[c][5] for c in range(NCH)])
    S.op("dve", lambda e: e.memset(ones_bf[:], 1.0), writes=(d_ones,))

    NL = DEPTH * 8
    lam = vec_sb[:, 8 * NL:9 * NL]
    t0, t1, t2, t3, t4, t5 = [t[:] for t in dtmp]
    dd = d_dtmp
    ts(t0, lam, -1.0, ALU.mult, [d_vec], [dd[0]], small=True)
    tt(t0, t0, lam, ALU.min, [d_vec, dd[0]], [dd[0]], small=True)
    act(t1, t0, AF.Exp, [dd[0]], [dd[1]], small=True)
    ts(t2, t1, 2.0, ALU.add, [dd[1]], [dd[2]], small=True)
    S.op("dve", lambda e: e.reciprocal(out=t2, in_=t2), reads=[dd[2]], writes=[dd[2]], small=True)
    tt(t3, t1, t2, ALU.mult, [dd[1], dd[2]], [dd[3]], small=True)
    tt(t4, t3, t3, ALU.mult, [dd[3]], [dd[4]], small=True)
    S.op("dve", lambda e: e.memset(t5, 0.0), writes=[dd[5]], small=True)
    for k in range(9, 0, -1):
        stt(t5, t5, 1.0 / (2 * k + 1), t4, ALU.add, ALU.mult, [dd[5], dd[4]], [dd[5]], small=True)
    stt(t5, t5, 1.0, t3, ALU.add, ALU.mult, [dd[5], dd[3]], [dd[5]], small=True)
    ts(t0, lam, -1.0, ALU.mult, [d_vec, dd[0]], [dd[0]], s2=0.0, op1=ALU.max, small=True)
    stt(t1, t5, 2.0, t0, ALU.mult, ALU.add, [dd[5], dd[0], dd[1]], [dd[1]], small=True)
    ts(der_sb[:, 0 * NL:1 * NL], vec_sb[:, 6 * NL:7 * NL], 0.5, ALU.mult, [d_vec], [d_der], small=True)
    ts(der_sb[:, 1 * NL:2 * NL], vec_sb[:, 7 * NL:8 * NL], 0.5, ALU.mult, [d_vec], [d_der], small=True)
    ts(der_sb[:, 2 * NL:3 * NL], t1, -4.0, ALU.mult, [dd[1]], [d_der], small=True)
    ts(der_sb[:, 3 * NL:4 * NL], t1, 2.0, ALU.mult, [dd[1]], [d_der], small=True)

    sq_ctr = [0]

    def norm_sq_chunk(g, t, c, bank, d_b):
        G = GROUPS[g]
        o, n = G["tiles"][t]
        gt = g * 3 + t
        go = G["off"] + o
        q = sq_ctr[0] % NSQ
        sq_ctr[0] += 1
        act(sqb[q][:, 0:n], x_sb[:, c, go:go + n], AF.Square, [d_x[c][gt]], [d_sqb[q]])
        S.op("pe", lambda e: e.matmul(out=bank[:, 0:n], lhsT=ones_bf[:], rhs=sqb[q][:, 0:n],
                                      start=(c == 0), stop=(c == NCH - 1)),
             reads=[d_sqb[q], d_ones], writes=[d_b])

    def norm_finish(l, g, t, gi, bank, d_b, final=False):
        G = GROUPS[g]
        o, n = G["tiles"][t]
        gt = g * 3 + t
        go = G["off"] + o
        act(bA[:, o:o + n], bank[:, 0:n], AF.Ln, [d_b], [d_bA[t]], bias=EPS, scale=1.0 / D)
        act(bA[:, o:o + n], bA[:, o:o + n], AF.Exp, [d_bA[t]], [d_bA[t]], scale=-0.5)
        for c in range(NCH):
            if final:
                fo = NVEC * DEPTH * 8 + c
                stt(x_sb[:, c, go:go + n], x_sb[:, c, go:go + n], vec_sb[:, fo:fo + 1], bA[:, o:o + n], ALU.mult, ALU.mult,
                    [d_x[c][gt], d_bA[t], d_vec], [d_x[c][gt]])
            else:
                stt(u_sb[:, c, o:o + n], x_sb[:, c, go:go + n], V(l, gi, c), bA[:, o:o + n], ALU.mult, ALU.mult,
                    [d_x[c][gt], d_bA[t], d_vec], [d_u[c][t]])

    def norm_tile(l, g, t, gi, final=False):
        bank, d_b = next_bank()
        for c in range(NCH):
            norm_sq_chunk(g, t, c, bank, d_b)
        norm_finish(l, g, t, gi, bank, d_b, final)

    def conv_taps(G, width, wvec, l, c, k0, outbuf, d_out, ks=None):
        xs3 = xr_sb[:, SB0:SB0 + NS * 7].rearrange("p (s t) -> p s t", t=7)
        xo = outbuf[:, TPG:TPG + TS].rearrange("p (s t) -> p s t", t=ST)
        rd = list(d_xr) + [d_xrh, d_vec]
        for k in (range(k0, width) if ks is None else ks):
            wk = V(l, wvec + (width - 1 - k), c)
            if k == 0:
                ts(outbuf[:, 0:TPG], xr_sb[:, 3:3 + TPG], wk, ALU.mult, rd, list(d_out))
            else:
                stt(outbuf[:, 0:TPG], xr_sb[:, 3 - k:3 - k + TPG], wk, outbuf[:, 0:TPG], ALU.mult, ALU.add,
                    rd + list(d_out), list(d_out))
            if G["samp"]:
                if k == 0:
                    ts(xo, xs3[:, :, 3:7], wk, ALU.mult, rd, [d_out[2]])
                else:
                    stt(xo, xs3[:, :, 3 - k:7 - k], wk, xo, ALU.mult, ALU.add, rd + [d_out[2]], [d_out[2]])

    def mixer(l, g):
        G = GROUPS[g]
        tiles = G["tiles"]
        samp_g = G["samp"]
        xs3 = xr_sb[:, SB0:SB0 + NS * 7].rearrange("p (s t) -> p s t", t=7)
        part2b_prev = [None]
        NG = G["n"]
        ALLA, ALLB, ALLC, ALLD = list(d_bA), list(d_bB), list(d_bC), list(d_bD)

        for c in range(NCH):
            wb, d_wb = next_batch()
            prefetch()
            it = lambda i, wb=wb: wb[:, i * 1024:(i + 1) * 1024]
            gD, d_gD = (bD, d_bD) if c % 2 == 0 else (bD1, d_bD1)
            ALLG = list(d_gD)

            def mmw(item, t, d_wb=d_wb):
                o, n = tiles[t]
                bank, d_b = next_bank()
                mm_group(bank, d_b, n, [(item[:, k * 128:(k + 1) * 128], u_sb[:, k, o:o + n]) for k in range(NCH)],
                         [d_wb] + [d_u[k][t] for k in range(NCH)])
                return bank, d_b, o, n

            if g == 0:
                S.op("dve", lambda e: e.memset(xr_sb[:, 0:3], 0.0), reads=[], writes=[d_xrh], small=True)
            else:
                cp(xr_sb[:, 0:3], xhalo[:, c, :], [d_xhalo], [d_xrh])
            if samp_g:
                cp(xs3[:, :, 0:3], st_in[:, c, 16:64].rearrange("p (s t) -> p s t", t=3), [d_stin], [d_xrh])
            xrb = []
            for t in range(3):
                bank, d_b, o, n = mmw(it(0), t)
                xrb.append((bank, d_b, o, n))
                samp = samp_g and t == 2
                npz = TW if samp else n
                act(xr_sb[:, 3 + o:3 + o + npz], bank[:, 0:npz], AF.Copy, [d_b], [d_xr[t]])
                if samp:
                    act(xs3[:, :, 3:7], bank[:, TW:TW + TS].rearrange("p (s t) -> p s t", t=ST), AF.Copy, [d_b], [d_xr[t]])
            for t, (bank, d_b, o, n) in enumerate(xrb):
                act(xc[:, o:o + n], bank[:, 0:n], AF.Identity, [d_b, d_vec], [d_xc[t]], bias=V(l, 5, c), scale=V(l, 4, c))
            conv_taps(G, 4, 1, l, c, 1, xc, d_xc)
            if g == 0:
                cp(xhalo[:, c, :], xr_sb[:, 3 + TPG - 3:3 + TPG], [d_xr[2]], [d_xhalo])
            else:
                cp(st_out[:, c, 1:4], xr_sb[:, 3 + TPG - 3:3 + TPG], [d_xr[2]], [d_stout])
                cp(st_out[:, c, 22:70].rearrange("p (s t) -> p s t", t=3), xs3[:, :, 4:7], [d_xr[2]], [d_stout])

            if part2b_prev[0] is not None:
                part2b_prev[0][2]()
            for t in range(3):
                bank, d_b, o, n = mmw(it(1), t)
                act(gD[:, o:o + n], bank[:, 0:n], AF.Copy, [d_b], [d_gD[t]])
            act(xcb[:, 0:NG], xc[:, 0:NG], AF.Copy, list(d_xc), list(d_xcb))

            if g == 1:
                cp(xr_sb[:, 1:3], chhalo[:, c, :], [d_chhalo], [d_xrh])
            if samp_g:
                cp(xs3[:, :, 1:3], st_in[:, c, 64:96].rearrange("p (s t) -> p s t", t=2), [d_stin], [d_xrh])
            for t in range(3):
                bank, d_b, o, n = mmw(it(2), t)
                samp = samp_g and t == 2
                npz = TW if samp else n
                act(xr_sb[:, 3 + o:3 + o + npz], bank[:, 0:npz], AF.Copy, [d_b], [d_xr[t]])
                if samp:
                    act(xs3[:, :, 3:7], bank[:, TW:TW + TS].rearrange("p (s t) -> p s t", t=ST), AF.Copy, [d_b], [d_xr[t]])
            if part2b_prev[0] is not None:
                part2b_prev[0][0]()
            act(gD[:, 0:NG], gD[:, 0:NG], AF.Gelu_apprx_tanh, ALLG, ALLG)
            for t in range(3):
                bank, d_b, o, n = mmw(it(3), t)
                samp = samp_g and t == 2
                npz = TW if samp else n
                tt(xr_sb[:, 3 + o:3 + o + npz], xr_sb[:, 3 + o:3 + o + npz], bank[:, 0:npz], ALU.mult, [d_b, d_xr[t]], [d_xr[t]])
                if samp:
                    tt(xs3[:, :, 3:7], xs3[:, :, 3:7], bank[:, TW:TW + TS].rearrange("p (s t) -> p s t", t=ST), ALU.mult,
                       [d_b, d_xr[t]], [d_xr[t]])
            steps = list(part2b_prev[0][1]) if part2b_prev[0] is not None else []

            def step():
                if steps:
                    steps.pop(0)()
            for k in range(3):
                conv_taps(G, 3, 9, l, c, 0, vc, d_vc, ks=[k])
                step()
            for t in range(3):
                bank, d_b, o, n = mmw(it(4), t)
                tt(R[:, 8 + c, o:o + n], bank[:, 0:n], vc[:, o:o + n], ALU.mult, [d_b, d_vc[t]], [d_R[8 + c][t]])
                step()
            if g == 0:
                cp(chhalo[:, c, :], xr_sb[:, 3 + TPG - 2:3 + TPG], [d_xr[2]], [d_chhalo])
            else:
                cp(st_out[:, c, 4:6], xr_sb[:, 3 + TPG - 2:3 + TPG], [d_xr[2]], [d_stout])
                cp(st_out[:, c, 70:102].rearrange("p (s t) -> p s t", t=2), xs3[:, :, 5:7], [d_xr[2]], [d_stout])
            while steps:
                step()
            part2b_prev[0] = None

            for t, (o, n) in enumerate(tiles):
                br, d_br = next_bank()
                S.op("pe", lambda e, br=br, n=n, o=o, c=c: e.matmul(out=br[:, 0:n], lhsT=gw[:, c * 128:(c + 1) * 128],
                                                                     rhs=xcb[:, o:o + n], start=True, stop=True),
                     reads=[d_gw, d_xcb[t]], writes=[d_br])
                bi, d_bi = next_bank()
                S.op("pe", lambda e, bi=bi, n=n, o=o, c=c: e.matmul(out=bi[:, 0:n], lhsT=gw[:, 1024 + c * 128:1024 + (c + 1) * 128],
                                                                     rhs=xcb[:, o:o + n], start=True, stop=True),
                     reads=[d_gw, d_xcb[t]], writes=[d_bi])
                act(bA[:, o:o + n], br[:, 0:n], AF.Tanh, [d_br, d_der], [d_bA[t]], bias=DER(0, l, c), scale=0.5)
                act(bC[:, o:o + n], bi[:, 0:n], AF.Tanh, [d_bi, d_der], [d_bC[t]], bias=DER(1, l, c), scale=0.5)
            stt(bC[:, 0:NG], bC[:, 0:NG], 1.0, xc[:, 0:NG], ALU.add, ALU.mult, ALLC + list(d_xc), ALLC)

            def part2a_tail(c=c):
                act(bB[:, 0:NG], bA[:, 0:NG], AF.Tanh, ALLA + [d_der], ALLB, bias=DER(3, l, c), scale=DER(3, l, c))
                act(bA[:, 0:NG], bA[:, 0:NG], AF.Exp, ALLA + ALLB + [d_der], ALLA, bias=DER(2, l, c), scale=DER(2, l, c))

            def part2b_act(c=c, gD=gD, ALLG=ALLG):
                act(bB[:, 0:NG], bB[:, 0:NG], AF.Sqrt, ALLB, ALLB, scale=0.25)

            def s_w(c=c):
                stt(bB[:, 0:NG], bA[:, 0:NG], 1.0, bB[:, 0:NG], ALU.add, ALU.mult, ALLA + ALLB, ALLB)

            def s_uu(c=c):
                stt(bC[:, 0:NG], bB[:, 0:NG], 0.5e-6, bC[:, 0:NG], ALU.max, ALU.mult, ALLB + ALLC, ALLC)
                if samp_g:
                    a3 = bA[:, TPG:TPG + TS].rearrange("p (s t) -> p s t", t=ST)
                    u3 = bC[:, TPG:TPG + TS].rearrange("p (s t) -> p s t", t=ST)
                    tt(tmp16[:], a3[:, :, 0], st_in[:, c, 0:NS], ALU.mult, [d_bA[2], d_stin], [d_tmp16], small=True)
                    tt(u3[:, :, 0], u3[:, :, 0], tmp16[:], ALU.add, [d_tmp16, d_bC[2]], [d_bC[2]], small=True)
                    S.op("dve", lambda e, a3=a3: e.memset(a3[:, :, 0], 0.0), reads=[d_tmp16], writes=[d_bA[2]], small=True)

            def s_scan(t, c=c):
                o, n = tiles[t]
                samp = samp_g and t == 2
                npz = TW if samp else n
                if t == 0:
                    init = 0.0 if g == 0 else hcar[:, c:c + 1]
                    rdi = [] if g == 0 else [d_hcar]
                else:
                    init = bB[:, o - 1:o]
                    rdi = [d_bB[t - 1]]
                S.op("dve", lambda e, o=o, npz=npz, init=init: e.tensor_tensor_scan(
                    out=bB[:, o:o + npz], data0=bA[:, o:o + npz], data1=bC[:, o:o + npz], initial=init,
                    op0=ALU.mult, op1=ALU.add), reads=[d_bA[t], d_bC[t], d_bB[t]] + rdi, writes=[d_bB[t]])
                if samp:
                    h3 = bB[:, TPG:TPG + TS].rearrange("p (s t) -> p s t", t=ST)
                    S.op("dve", lambda e: e.tensor_tensor_scan(
                        out=bB[:, TPG:TPG + TS], data0=bA[:, TPG:TPG + TS], data1=bC[:, TPG:TPG + TS], initial=0.0,
                        op0=ALU.mult, op1=ALU.add), reads=[d_bA[t], d_bC[t], d_bB[t]], writes=[d_bB[t]], small=True)
                    cp(st_out[:, c, 0:1], bB[:, TPG - 1:TPG], [d_bB[t]], [d_stout])
                    cp(st_out[:, c, 6:22], h3[:, :, ST - 1], [d_bB[t]], [d_stout])
                if g == 0 and t == 2:
                    cp(hcar[:, c:c + 1], bB[:, TPG - 1:TPG], [d_bB[t]], [d_hcar])

            def s_ya(c=c, gD=gD, ALLG=ALLG):
                tt(R[:, c, 0:NG], bB[:, 0:NG], gD[:, 0:NG], ALU.mult, ALLB + ALLG, list(d_R[c]))

            part2b_dve = [s_w, s_uu, lambda: s_scan(0), lambda: s_scan(1), lambda: s_scan(2), s_ya]

            part2b_prev[0] = (part2b_act, part2b_dve, part2a_tail)
        return part2b_prev[0]

    def merge(l, g, tail):
        G = GROUPS[g]
        tiles = G["tiles"]
        for c in range(NCH):
            wb, d_wb = next_batch()
            prefetch()
            it = lambda i, wb=wb: wb[:, i * 1024:(i + 1) * 1024]
            sA, dA, sC, dC = (xc, d_xc, vc, d_vc) if c == 0 else (bA, d_bA, bC, d_bC)
            for gi_item, dsig, sigbuf in ((0, dA, sA), (1, dC, sC)):
                if c == 0 and gi_item == 1 and tail is not None:
                    pass
                for t, (o, n) in enumerate(tiles):
                    bank, d_b = next_bank()
                    mm_group(bank, d_b, n, [(it(gi_item)[:, k * 128:(k + 1) * 128], u_sb[:, k, o:o + n]) for k in range(NCH)],
                             [d_wb] + [d_u[k][t] for k in range(NCH)])
                    if c == 0:
                        S.op("dve", lambda e, sigbuf=sigbuf, bank=bank, o=o, n=n: e.tensor_copy(out=sigbuf[:, o:o + n], in_=bank[:, 0:n]),
                             reads=[d_b], writes=[dsig[t]])
                        act(sigbuf[:, o:o + n], sigbuf[:, o:o + n], AF.Sigmoid, [dsig[t]], [dsig[t]])
                    else:
                        act(sigbuf[:, o:o + n], bank[:, 0:n], AF.Sigmoid, [d_b], [dsig[t]])
            if c == 0:
                for st_ in tail[1]:
                    st_()
            for t, (o, n) in enumerate(tiles):
                bank, d_b = next_bank()
                mm_group(bank, d_b, n, [(it(2)[:, k * 128:(k + 1) * 128], R[:, 8 + k, o:o + n]) for k in range(NCH)],
                         [d_wb] + [d_R[8 + k][t] for k in range(NCH)])
                tt(bD[:, o:o + n], bank[:, 0:n], sC[:, o:o + n], ALU.mult, [d_b, dC[t]], [d_bD[t]])
            for t, (o, n) in enumerate(tiles):
                bank, d_b = next_bank()
                mm_group(bank, d_b, n, [(it(3)[:, k * 128:(k + 1) * 128], R[:, k, o:o + n]) for k in range(NCH)],
                         [d_wb] + [d_R[k][t] for k in range(NCH)])
                tt(bB[:, o:o + n], bank[:, 0:n], sA[:, o:o + n], ALU.mult, [d_b, dA[t]], [d_bB[t]])
            for t, (o, n) in enumerate(tiles):
                tt(R[:, 16 + c, o:o + n], bB[:, o:o + n], bD[:, o:o + n], ALU.add, [d_bB[t], d_bD[t]], [d_R[16 + c][t]])

    def wout_norm2(l, g):
        G = GROUPS[g]
        tiles = G["tiles"]
        nb = [hold_bank() for _ in range(3)]
        for bi in range(2):
            wb, d_wb = next_batch()
            prefetch()
            for j in range(4):
                oc = bi * 4 + j
                item = wb[:, j * 1024:(j + 1) * 1024]
                for t, (o, n) in enumerate(tiles):
                    gt = g * 3 + t
                    go = G["off"] + o
                    bank, d_b = next_bank()
                    mm_group(bank, d_b, n, [(item[:, k * 128:(k + 1) * 128], R[:, 16 + k, o:o + n]) for k in range(NCH)],
                             [d_wb] + [d_R[16 + k][t] for k in range(NCH)])
                    tt(x_sb[:, oc, go:go + n], x_sb[:, oc, go:go + n], bank[:, 0:n], ALU.add, [d_b, d_x[oc][gt]], [d_x[oc][gt]])
                if oc > 0:
                    for t in range(3):
                        norm_sq_chunk(g, t, oc - 1, nb[t][0], nb[t][1])
        for t in range(3):
            norm_sq_chunk(g, t, NCH - 1, nb[t][0], nb[t][1])
        for t in range(3):
            norm_finish(l, g, t, 12, nb[t][0], nb[t][1])
            release_bank(nb[t])

    def ffn(l, g, nxt):
        G = GROUPS[g]
        tiles = G["tiles"]
        sbufs = [(bB, d_bB), (bC, d_bC), (bD, d_bD)]
        for bi in range(11):
            wb, d_wb = next_batch()
            prefetch()
            for jj in range(2):
                j = bi * 2 + jj
                gate = wb[:, (2 * jj) * 1024:(2 * jj + 1) * 1024]
                up = wb[:, (2 * jj + 1) * 1024:(2 * jj + 2) * 1024]
                sbuf_, dsb = sbufs[j % 3]
                for t, (o, n) in enumerate(tiles):
                    bank, d_b = next_bank()
                    mm_group(bank, d_b, n, [(gate[:, k * 128:(k + 1) * 128], u_sb[:, k, o:o + n]) for k in range(NCH)],
                             [d_wb] + [d_u[k][t] for k in range(NCH)])
                    act(sbuf_[:, o:o + n], bank[:, 0:n], AF.Silu, [d_b], [dsb[t]])
                for t, (o, n) in enumerate(tiles):
                    bank, d_b = next_bank()
                    mm_group(bank, d_b, n, [(up[:, k * 128:(k + 1) * 128], u_sb[:, k, o:o + n]) for k in range(NCH)],
                             [d_wb] + [d_u[k][t] for k in range(NCH)])
                    tt(R[:, j, o:o + n], bank[:, 0:n], sbuf_[:, o:o + n], ALU.mult, [d_b, dsb[t]], [d_R[j][t]])
        hoist = {1: 0, 3: 1, 5: 2}
        for oc in range(NCH):
            wb, d_wb = next_batch()
            prefetch()
            for t, (o, n) in enumerate(tiles):
                gt = g * 3 + t
                go = G["off"] + o
                bank, d_b = next_bank()
                mm_group(bank, d_b, n, [(wb[:, k * 128:(k + 1) * 128], R[:, k, o:o + n]) for k in range(NFF)],
                         [d_wb] + [d_R[k][t] for k in range(NFF)])
                tt(x_sb[:, oc, go:go + n], x_sb[:, oc, go:go + n], bank[:, 0:n], ALU.add, [d_b, d_x[oc][gt]], [d_x[oc][gt]])
            if nxt is not None and oc in hoist:
                norm_tile(nxt[0], nxt[1], hoist[oc], 0)

    seq = [(l, g) for l in range(depth) for g in range(2)]
    for t in range(3):
        norm_tile(0, 0, t, 0)
    for i, (l, g) in enumerate(seq):
        if g == 0:
            S.dma("pool", lambda e, l=l: e.dma_start(out=gw[:], in_=ws_d[l, :, 0:W_GATES], max_dma_last_dim=4096), "gwl",
                  writes=[d_gw])
            S.dma("sp", lambda e, l=l: e.dma_start(out=st_in[:].rearrange("p c n -> p (c n)"), in_=stin_d[l]), "ldst",
                  writes=[d_stin])
        tail = mixer(l, g)
        tail[2]()
        tail[0]()
        merge(l, g, tail)
        if g == 1:
            S.dma("sp", lambda e, l=l: e.dma_start(out=sto_d[l], in_=st_out[:].rearrange("p c n -> p (c n)")), "st",
                  reads=[d_stout])
        wout_norm2(l, g)
        ffn(l, g, seq[i + 1] if i + 1 < len(seq) else None)

    for g in range(2):
        for t in range(3):
            norm_tile(0, g, t, 0, final=True)
    for c0 in range(0, NCH, 2):
        S.dma("sp", lambda e, c0=c0: e.dma_start(out=y_d[:, c0:c0 + 2, :], in_=x_sb[:, c0:c0 + 2, :]), "st",
              reads=[d_x[c][t] for c in (c0, c0 + 1) for t in range(6)])

    sem_keys = list(Sched.ENGS) + sorted(S.dma_cnt.keys())
    sems = {k: es.enter_context(nc.semaphore("s_" + k)) for k in sem_keys}

    def emit(name, e):
        for waits, fn, key, inc in S.streams[name]:
            for k, v in waits:
                e.wait_ge(sems[k], v)
            ins = fn(e)
            ins.then_inc(sems[key], inc)
        if name == "sp":
            for k, v in S.dma_cnt.items():
                e.wait_ge(sems[k], v)

    with nc.Block() as block:
        @block.tensor
        def _(e):
            emit("pe", e)

        @block.scalar
        def _(e):
            emit("act", e)

        @block.vector
        def _(e):
            emit("dve", e)

        @block.gpsimd
        def _(e):
            emit("pool", e)

        @block.sync
        def _(e):
            emit("sp", e)
    es.close()
    return nc


def _fm(a):
    T = a.shape[0]
    return np.ascontiguousarray(a.reshape(T, NCH, 128).transpose(2, 1, 0))


def _pack_weights(inp):
    ws = np.zeros((DEPTH, 128, WS_LAYER), np.float32)
    for l in range(DEPTH):
        off = 0
        gwl = np.zeros((128, 2, 8, 128), np.float32)
        for gi, name in enumerate(("gate_a_w", "gate_x_w")):
            w = inp[name][l]
            w2 = w.reshape(8, 2, 64, 64)
            gwl[0:64, gi, :, 0:64] = w2[:, 0].transpose(1, 0, 2)
            gwl[64:128, gi, :, 64:128] = w2[:, 1].transpose(1, 0, 2)
        ws[l, :, off:off + W_GATES] = gwl.reshape(128, W_GATES); off += W_GATES

        def item(W, col0):
            K = W.shape[0]
            blk = W[:, col0:col0 + 128].reshape(K // 128, 128, 128)
            return blk.transpose(1, 0, 2).reshape(128, K)

        w_in = inp["w_in"][l]
        for c in range(8):
            for s in ("xr", "gr", "cc", "hc", "bc"):
                ws[l, :, off:off + 1024] = item(w_in, OFF[s] + c * 128); off += 1024
        wa, wbm = inp["w_branch_a"][l], inp["w_branch_b"][l]
        for c in range(8):
            ws[l, :, off:off + 1024] = item(w_in, OFF["ga"] + c * 128); off += 1024
            ws[l, :, off:off + 1024] = item(w_in, OFF["gb"] + c * 128); off += 1024
            ws[l, :, off:off + 1024] = item(wbm, c * 128); off += 1024
            ws[l, :, off:off + 1024] = item(wa, c * 128); off += 1024
        wo = inp["w_out"][l]
        for c in range(8):
            ws[l, :, off:off + 1024] = item(wo, c * 128); off += 1024
        wg, wu = inp["w_ff_gate"][l], inp["w_ff_up"][l]
        for j in range(NFF):
            ws[l, :, off:off + 1024] = item(wg, j * 128); off += 1024
            ws[l, :, off:off + 1024] = item(wu, j * 128); off += 1024
        wd = inp["w_ff_down"][l]
        for c in range(8):
            ws[l, :, off:off + W_DN] = item(wd, c * 128); off += W_DN
        assert off == WS_LAYER
    return ws


def _pack_vecs(inp):
    v = np.zeros((128, NVEC * DEPTH * 8 + 8), np.float32)
    rows = []
    for l in range(DEPTH):
        rows.append([inp["norm1_g"][l]] + [inp["rnn_conv_w"][l, k] for k in range(4)] + [inp["rnn_conv_b"][l],
                    inp["gate_a_b"][l], inp["gate_x_b"][l], inp["lru_lambda"][l]] +
                    [inp["sc_conv_w"][l, k] for k in range(3)] + [inp["norm2_g"][l]])
    for i in range(NVEC):
        for l in range(DEPTH):
            o = (i * DEPTH + l) * 8
            v[:, o:o + 8] = np.asarray(rows[l][i]).reshape(8, 128).T
    v[:, NVEC * DEPTH * 8:] = np.asarray(inp["final_norm_g"]).reshape(8, 128).T
    return v


def kernel(**inputs):
    inp = {k: np.asarray(v) for k, v in inputs.items()}
    nc = build_program(DEPTH)
    ws = _pack_weights(inp)
    vecs = _pack_vecs(inp)
    meta = _fm(inp["meta_tokens"].astype(np.float32))
    in_maps = []
    for i in range(NCORES):
        xp = _fm(inp["x_prompt"][i])
        xs = _fm(inp["x_sample"][i * NS:(i + 1) * NS].reshape(TS, D))
        stin = np.zeros((DEPTH, 128, NCH, NSTI), np.float32)
        h0 = inp["state_rnn_h"][:, i * NS:(i + 1) * NS]
        rc = inp["state_rnn_conv"][:, i * NS:(i + 1) * NS]
        sc = inp["state_sc_conv"][:, i * NS:(i + 1) * NS]
        stin[:, :, :, 0:16] = h0.reshape(DEPTH, NS, NCH, 128).transpose(0, 3, 2, 1)
        stin[:, :, :, 16:64] = rc.reshape(DEPTH, NS, 3, NCH, 128).transpose(0, 4, 3, 1, 2).reshape(DEPTH, 128, NCH, 48)
        stin[:, :, :, 64:96] = sc.reshape(DEPTH, NS, 2, NCH, 128).transpose(0, 4, 3, 1, 2).reshape(DEPTH, 128, NCH, 32)
        in_maps.append(dict(xp=xp, meta=meta, xs=xs, stin=np.ascontiguousarray(stin.reshape(DEPTH, 128, NCH * NSTI)),
                            vecs=vecs, ws=ws))
    res = run_bass_kernel_spmd(nc, in_maps, core_ids=list(range(NCORES)))
    y_prompt = np.zeros((NCORES, SEQ, D), np.float32)
    y_sample = np.zeros((NCORES * NS, ST, D), np.float32)
    rnn_h_p = np.zeros((DEPTH, NCORES, D), np.float32)
    rnn_c_p = np.zeros((DEPTH, NCORES, 3, D), np.float32)
    sc_c_p = np.zeros((DEPTH, NCORES, 2, D), np.float32)
    rnn_h_s = np.zeros((DEPTH, NCORES * NS, D), np.float32)
    rnn_c_s = np.zeros((DEPTH, NCORES * NS, 3, D), np.float32)
    sc_c_s = np.zeros((DEPTH, NCORES * NS, 2, D), np.float32)
    for i in range(NCORES):
        r = res.results[i]
        y = np.asarray(r["y"]).reshape(128, NCH, TTOT)
        yt = y.transpose(2, 1, 0).reshape(TTOT, D)
        y_prompt[i] = yt[NMETA:TP]
        y_sample[i * NS:(i + 1) * NS] = yt[TP:].reshape(NS, ST, D)
        so = np.asarray(r["sto"]).reshape(DEPTH, 128, NCH, NSTO)
        so = so.transpose(0, 3, 2, 1).reshape(DEPTH, NSTO, D)
        rnn_h_p[:, i] = so[:, 0]
        rnn_c_p[:, i] = so[:, 1:4]
        sc_c_p[:, i] = so[:, 4:6]
        rnn_h_s[:, i * NS:(i + 1) * NS] = so[:, 6:22]
        rnn_c_s[:, i * NS:(i + 1) * NS] = so[:, 22:70].reshape(DEPTH, NS, 3, D)
        sc_c_s[:, i * NS:(i + 1) * NS] = so[:, 70:102].reshape(DEPTH, NS, 2, D)
    return (y_prompt, y_sample, rnn_h_p, rnn_c_p, sc_c_p, rnn_h_s, rnn_c_s, sc_c_s)
```

```python
import numpy as np
from contextlib import ExitStack
import concourse.bass as bass
import concourse.mybir as mybir
from concourse.bass_utils import run_bass_kernel_spmd

F32 = mybir.dt.float32
BF16 = mybir.dt.bfloat16
AF = mybir.ActivationFunctionType
ALU = mybir.AluOpType

D = 1024
NCH = 8
DFF = 2816
NFF = 22
DEPTH = 4
NMETA = 16
SEQ = 2048
TP = NMETA + SEQ
NS = 16
ST = 4
TS = NS * ST
TTOT = TP + TS
NCORES = 8
EPS = 1e-6
OFF = dict(xr=0, gr=1024, bc=2048, cc=3072, hc=4096, ga=5120, gb=6144)
TPG = TP // 2
TW = 344
GROUPS = [
    dict(off=0, n=TPG, tiles=[(0, TW), (TW, TW), (2 * TW, TW)], samp=False),
    dict(off=TPG, n=TPG + TS, tiles=[(0, TW), (TW, TW), (2 * TW, TW + TS)], samp=True),
]
GW = TPG + TS
NVEC = 13
NSTI = 96
NSTO = 102
W_GATES = 2 * 8 * 128
W_MIX = 5 * 1024
W_MRG = 4 * 1024
W_OUT = 4 * 1024
W_FFN = 4 * 1024
W_DN = NFF * 128
WS_LAYER = W_GATES + 8 * W_MIX + 8 * W_MRG + 2 * W_OUT + 11 * W_FFN + 8 * W_DN
WBUF = 5120
XRW = 1160
SB0 = 1040
NSQ = 3

SAME_SYNC_ALL = True


class Dep:
    __slots__ = ("w", "r")

    def __init__(self):
        self.w = None
        self.r = {}


class Sched:
    ENGS = ("pe", "act", "dve", "pool", "sp")

    def __init__(self):
        self.streams = {e: [] for e in self.ENGS}
        self.tick = {e: 0 for e in self.ENGS}
        self.known = {e: {} for e in self.ENGS}
        self.dma_cnt = {}

    def _waits(self, eng, reads, writes, small):
        waits = {}

        def need(k, v):
            if k == eng and (eng == "pe" or not (small or SAME_SYNC_ALL)):
                return
            if self.known[eng].get(k, 0) >= v:
                return
            if waits.get(k, 0) < v:
                waits[k] = v

        for t in reads:
            if t.w is not None:
                need(*t.w)
        for t in writes:
            if t.w is not None:
                need(*t.w)
            for k, v in t.r.items():
                need(k, v)
        for k, v in waits.items():
            self.known[eng][k] = v
        return list(waits.items())

    def _mark(self, tok, reads, writes):
        k, v = tok
        for t in reads:
            t.r[k] = v
        for t in writes:
            t.w = tok
            t.r = {}

    def op(self, eng, fn, reads=(), writes=(), small=False):
        waits = self._waits(eng, reads, writes, small)
        self.tick[eng] += 1
        tok = (eng, self.tick[eng])
        self.streams[eng].append((waits, fn, eng, 1))
        self._mark(tok, reads, writes)

    def dma(self, eng, fn, semkey, reads=(), writes=()):
        waits = self._waits(eng, reads, writes, False)
        self.dma_cnt[semkey] = self.dma_cnt.get(semkey, 0) + 16
        tok = (semkey, self.dma_cnt[semkey])
        self.streams[eng].append((waits, fn, semkey, 16))
        self._mark(tok, reads, writes)


def build_program(depth=DEPTH):
    nc = bass.Bass("TRN2", target_bir_lowering=False)
    xp_d = nc.dram_tensor("xp", [128, NCH, SEQ], F32, kind="ExternalInput").ap()
    meta_d = nc.dram_tensor("meta", [128, NCH, NMETA], F32, kind="ExternalInput").ap()
    xs_d = nc.dram_tensor("xs", [128, NCH, TS], F32, kind="ExternalInput").ap()
    stin_d = nc.dram_tensor("stin", [DEPTH, 128, NCH * NSTI], F32, kind="ExternalInput").ap()
    vecs_d = nc.dram_tensor("vecs", [128, NVEC * DEPTH * 8 + 8], F32, kind="ExternalInput").ap()
    ws_d = nc.dram_tensor("ws", [DEPTH, 128, WS_LAYER], F32, kind="ExternalInput").ap()
    y_d = nc.dram_tensor("y", [128, NCH, TTOT], F32, kind="ExternalOutput").ap()
    sto_d = nc.dram_tensor("sto", [DEPTH, 128, NCH * NSTO], F32, kind="ExternalOutput").ap()

    S = Sched()
    es = ExitStack()

    def sb(name, shape, dt):
        return es.enter_context(nc.sbuf_tensor(name, shape, dt))

    x_sb = sb("x_sb", [128, NCH, TTOT], F32)
    u_sb = sb("u_sb", [128, NCH, GW], BF16)
    R = sb("R", [128, 24, GW], BF16)
    wbuf = [sb("wbuf0", [128, WBUF], BF16), sb("wbuf1", [128, WBUF], BF16)]
    gw = sb("gw", [128, W_GATES], BF16)
    vec_sb = sb("vec_sb", [128, NVEC * DEPTH * 8 + 8], F32)
    der_sb = sb("der_sb", [128, 4 * DEPTH * 8], F32)
    dtmp = [sb("dtmp%d" % i, [128, DEPTH * 8], F32) for i in range(6)]
    ones_bf = sb("ones_bf", [128, 128], BF16)
    xr_sb = sb("xr_sb", [128, XRW], F32)
    xc = sb("xc", [128, GW], F32)
    xcb = sb("xcb", [128, GW], BF16)
    vc = sb("vc", [128, GW], F32)
    bA = sb("bA", [128, GW], F32)
    bB = sb("bB", [128, GW], F32)
    bC = sb("bC", [128, GW], F32)
    bD = sb("bD", [128, GW], F32)
    bD1 = sb("bD1", [128, GW], F32)
    sqb = [sb("sqb%d" % i, [128, 416], BF16) for i in range(NSQ)]
    st_in = sb("st_in", [128, NCH, NSTI], F32)
    st_out = sb("st_out", [128, NCH, NSTO], F32)
    hcar = sb("hcar", [128, NCH], F32)
    xhalo = sb("xhalo", [128, NCH, 3], F32)
    chhalo = sb("chhalo", [128, NCH, 2], F32)
    tmp16 = sb("tmp16", [128, NS], F32)
    banks = [es.enter_context(nc.psum_tensor("ps%d" % i, [128, 512], F32)) for i in range(8)]

    d_x = [[Dep() for _ in range(6)] for _ in range(NCH)]
    d_u = [[Dep() for _ in range(3)] for _ in range(NCH)]
    d_R = [[Dep() for _ in range(3)] for _ in range(24)]
    d_wbuf = [Dep(), Dep()]
    d_gw = Dep()
    d_vec = Dep()
    d_der = Dep()
    d_dtmp = [Dep() for _ in range(6)]
    d_ones = Dep()
    d_xr = [Dep() for _ in range(3)]
    d_xrh = Dep()
    d_xc = [Dep() for _ in range(3)]
    d_xcb = [Dep() for _ in range(3)]
    d_vc = [Dep() for _ in range(3)]
    d_bA = [Dep() for _ in range(3)]
    d_bB = [Dep() for _ in range(3)]
    d_bC = [Dep() for _ in range(3)]
    d_bD = [Dep() for _ in range(3)]
    d_bD1 = [Dep() for _ in range(3)]
    d_sqb = [Dep() for _ in range(NSQ)]
    d_stin = Dep()
    d_stout = Dep()
    d_hcar = Dep()
    d_xhalo = Dep()
    d_chhalo = Dep()
    d_tmp16 = Dep()
    d_bank = [Dep() for _ in range(8)]
    d_y = Dep()

    bank_ctr = [0]

    held = set()

    def next_bank():
        while True:
            b = bank_ctr[0] % 8
            bank_ctr[0] += 1
            if b not in held:
                return banks[b], d_bank[b]

    def hold_bank():
        bk = next_bank()
        held.add(banks.index(bk[0]))
        return bk

    def release_bank(bk):
        held.discard(banks.index(bk[0]))

    def V(l, i, c):
        o = (i * DEPTH + l) * 8 + c
        return vec_sb[:, o:o + 1]

    def DER(kind, l, c):
        o = kind * DEPTH * 8 + l * 8 + c
        return der_sb[:, o:o + 1]

    batches = []
    for l in range(depth):
        for g in range(2):
            off = W_GATES
            for _ in range(8):
                batches.append((l, off, W_MIX)); off += W_MIX
            for _ in range(8):
                batches.append((l, off, W_MRG)); off += W_MRG
            for _ in range(2):
                batches.append((l, off, W_OUT)); off += W_OUT
            for _ in range(11):
                batches.append((l, off, W_FFN)); off += W_FFN
            for _ in range(8):
                batches.append((l, off, W_DN)); off += W_DN
            assert off == WS_LAYER
    bstate = dict(issued=0, used=0)

    def issue_batch():
        i = bstate["issued"]
        if i >= len(batches):
            return
        l, off, n = batches[i]
        b = i % 2
        src = ws_d[l, :, off:off + n]
        dst = wbuf[b][:, 0:n]
        S.dma("pool", lambda e, src=src, dst=dst: e.dma_start(out=dst, in_=src, max_dma_last_dim=4096),
              "w%d" % b, reads=(), writes=(d_wbuf[b],))
        bstate["issued"] += 1

    def next_batch():
        i = bstate["used"]
        while bstate["issued"] <= min(i, len(batches) - 1):
            issue_batch()
        b = i % 2
        bstate["used"] += 1
        return wbuf[b], d_wbuf[b]

    def prefetch():
        if bstate["issued"] < bstate["used"] + 1:
            issue_batch()

    def mm_group(bank, d_b, n, pairs, reads):
        def fn(e, bank=bank, n=n, pairs=pairs):
            last = None
            for i, (lt, rh) in enumerate(pairs):
                last = e.matmul(out=bank[:, 0:n], lhsT=lt, rhs=rh, start=(i == 0), stop=(i == len(pairs) - 1))
            return last
        S.op("pe", fn, reads=reads, writes=(d_b,))

    def act(out, in_, func, reads, writes, bias=None, scale=None, small=False):
        kw = {}
        if bias is not None:
            kw["bias"] = bias
        if scale is not None:
            kw["scale"] = scale
        S.op("act", lambda e: e.activation(out=out, in_=in_, func=func, **kw), reads=reads, writes=writes, small=small)

    def tt(out, in0, in1, op, reads, writes, small=False):
        S.op("dve", lambda e: e.tensor_tensor(out=out, in0=in0, in1=in1, op=op), reads=reads, writes=writes, small=small)

    def ts(out, in0, s1, op0, reads, writes, s2=None, op1=None, small=False):
        if op1 is None:
            S.op("dve", lambda e: e.tensor_scalar(out=out, in0=in0, scalar1=s1, scalar2=None, op0=op0),
                 reads=reads, writes=writes, small=small)
        else:
            S.op("dve", lambda e: e.tensor_scalar(out=out, in0=in0, scalar1=s1, scalar2=s2, op0=op0, op1=op1),
                 reads=reads, writes=writes, small=small)

    def stt(out, in0, scalar, in1, op0, op1, reads, writes, small=False):
        S.op("dve", lambda e: e.scalar_tensor_tensor(out=out, in0=in0, scalar=scalar, in1=in1, op0=op0, op1=op1),
             reads=reads, writes=writes, small=small)

    def cp(out, in_, reads, writes, small=True):
        S.op("dve", lambda e: e.tensor_copy(out=out, in_=in_), reads=reads, writes=writes, small=small)

    S.dma("sp", lambda e: e.dma_start(out=vec_sb[:], in_=vecs_d), "ldv", writes=(d_vec,))
    for c0 in range(0, NCH, 2):
        S.dma("sp", lambda e, c0=c0: e.dma_start(out=x_sb[:, c0:c0 + 2, NMETA:TP], in_=xp_d[:, c0:c0 + 2, :]), "ldx%d" % c0,
              writes=[d_x[c][t] for c in (c0, c0 + 1) for t in range(6)])
    S.dma("sp", lambda e: e.dma_start(out=x_sb[:, :, 0:NMETA], in_=meta_d), "ldm",
          writes=[d_x[c][0] for c in range(NCH)])
    S.dma("sp", lambda e: e.dma_start(out=x_sb[:, :, TP:TTOT], in_=xs_d), "lds",
          writes=[d_x[c][5] for c in range(NCH)])
    S.op("dve", lambda e: e.memset(ones_bf[:], 1.0), writes=(d_ones,))

    NL = DEPTH * 8
    lam = vec_sb[:, 8 * NL:9 * NL]
    t0, t1, t2, t3, t4, t5 = [t[:] for t in dtmp]
    dd = d_dtmp
    ts(t0, lam, -1.0, ALU.mult, [d_vec], [dd[0]], small=True)
    tt(t0, t0, lam, ALU.min, [d_vec, dd[0]], [dd[0]], small=True)
    act(t1, t0, AF.Exp, [dd[0]], [dd[1]], small=True)
    ts(t2, t1, 2.0, ALU.add, [dd[1]], [dd[2]], small=True)
    S.op("dve", lambda e: e.reciprocal(out=t2, in_=t2), reads=[dd[2]], writes=[dd[2]], small=True)
    tt(t3, t1, t2, ALU.mult, [dd[1], dd[2]], [dd[3]], small=True)
    tt(t4, t3, t3, ALU.mult, [dd[3]], [dd[4]], small=True)
    S.op("dve", lambda e: e.memset(t5, 0.0), writes=[dd[5]], small=True)
    for k in range(9, 0, -1):
        stt(t5, t5, 1.0 / (2 * k + 1), t4, ALU.add, ALU.mult, [dd[5], dd[4]], [dd[5]], small=True)
    stt(t5, t5, 1.0, t3, ALU.add, ALU.mult, [dd[5], dd[3]], [dd[5]], small=True)
    ts(t0, lam, -1.0, ALU.mult, [d_vec, dd[0]], [dd[0]], s2=0.0, op1=ALU.max, small=True)
    stt(t1, t5, 2.0, t0, ALU.mult, ALU.add, [dd[5], dd[0], dd[1]], [dd[1]], small=True)
    ts(der_sb[:, 0 * NL:1 * NL], vec_sb[:, 6 * NL:7 * NL], 0.5, ALU.mult, [d_vec], [d_der], small=True)
    ts(der_sb[:, 1 * NL:2 * NL], vec_sb[:, 7 * NL:8 * NL], 0.5, ALU.mult, [d_vec], [d_der], small=True)
    ts(der_sb[:, 2 * NL:3 * NL], t1, -4.0, ALU.mult, [dd[1]], [d_der], small=True)
    ts(der_sb[:, 3 * NL:4 * NL], t1, 2.0, ALU.mult, [dd[1]], [d_der], small=True)

    sq_ctr = [0]

    def norm_sq_chunk(g, t, c, bank, d_b):
        G = GROUPS[g]
        o, n = G["tiles"][t]
        gt = g * 3 + t
        go = G["off"] + o
        q = sq_ctr[0] % NSQ
        sq_ctr[0] += 1
        act(sqb[q][:, 0:n], x_sb[:, c, go:go + n], AF.Square, [d_x[c][gt]], [d_sqb[q]])
        S.op("pe", lambda e: e.matmul(out=bank[:, 0:n], lhsT=ones_bf[:], rhs=sqb[q][:, 0:n],
                                      start=(c == 0), stop=(c == NCH - 1)),
             reads=[d_sqb[q], d_ones], writes=[d_b])

    def norm_finish(l, g, t, gi, bank, d_b, final=False):
        G = GROUPS[g]
        o, n = G["tiles"][t]
        gt = g * 3 + t
        go = G["off"] + o
        act(bA[:, o:o + n], bank[:, 0:n], AF.Ln, [d_b], [d_bA[t]], bias=EPS, scale=1.0 / D)
        act(bA[:, o:o + n], bA[:, o:o + n], AF.Exp, [d_bA[t]], [d_bA[t]], scale=-0.5)
        for c in range(NCH):
            if final:
                fo = NVEC * DEPTH * 8 + c
                stt(x_sb[:, c, go:go + n], x_sb[:, c, go:go + n], vec_sb[:, fo:fo + 1], bA[:, o:o + n], ALU.mult, ALU.mult,
                    [d_x[c][gt], d_bA[t], d_vec], [d_x[c][gt]])
            else:
                stt(u_sb[:, c, o:o + n], x_sb[:, c, go:go + n], V(l, gi, c), bA[:, o:o + n], ALU.mult, ALU.mult,
                    [d_x[c][gt], d_bA[t], d_vec], [d_u[c][t]])

    def norm_tile(l, g, t, gi, final=False):
        bank, d_b = next_bank()
        for c in range(NCH):
            norm_sq_chunk(g, t, c, bank, d_b)
        norm_finish(l, g, t, gi, bank, d_b, final)

    def conv_taps(G, width, wvec, l, c, k0, outbuf, d_out, ks=None):
        xs3 = xr_sb[:, SB0:SB0 + NS * 7].rearrange("p (s t) -> p s t", t=7)
        xo = outbuf[:, TPG:TPG + TS].rearrange("p (s t) -> p s t", t=ST)
        rd = list(d_xr) + [d_xrh, d_vec]
        for k in (range(k0, width) if ks is None else ks):
            wk = V(l, wvec + (width - 1 - k), c)
            if k == 0:
                ts(outbuf[:, 0:TPG], xr_sb[:, 3:3 + TPG], wk, ALU.mult, rd, list(d_out))
            else:
                stt(outbuf[:, 0:TPG], xr_sb[:, 3 - k:3 - k + TPG], wk, outbuf[:, 0:TPG], ALU.mult, ALU.add,
                    rd + list(d_out), list(d_out))
            if G["samp"]:
                if k == 0:
                    ts(xo, xs3[:, :, 3:7], wk, ALU.mult, rd, [d_out[2]])
                else:
                    stt(xo, xs3[:, :, 3 - k:7 - k], wk, xo, ALU.mult, ALU.add, rd + [d_out[2]], [d_out[2]])

    def mixer(l, g):
        G = GROUPS[g]
        tiles = G["tiles"]
        samp_g = G["samp"]
        xs3 = xr_sb[:, SB0:SB0 + NS * 7].rearrange("p (s t) -> p s t", t=7)
        part2b_prev = [None]
        NG = G["n"]
        ALLA, ALLB, ALLC, ALLD = list(d_bA), list(d_bB), list(d_bC), list(d_bD)

        for c in range(NCH):
            wb, d_wb = next_batch()
            prefetch()
            it = lambda i, wb=wb: wb[:, i * 1024:(i + 1) * 1024]
            gD, d_gD = (bD, d_bD) if c % 2 == 0 else (bD1, d_bD1)
            ALLG = list(d_gD)

            def mmw(item, t, d_wb=d_wb):
                o, n = tiles[t]
                bank, d_b = next_bank()
                mm_group(bank, d_b, n, [(item[:, k * 128:(k + 1) * 128], u_sb[:, k, o:o + n]) for k in range(NCH)],
                         [d_wb] + [d_u[k][t] for k in range(NCH)])
                return bank, d_b, o, n

            if g == 0:
                S.op("dve", lambda e: e.memset(xr_sb[:, 0:3], 0.0), reads=[], writes=[d_xrh], small=True)
            else:
                cp(xr_sb[:, 0:3], xhalo[:, c, :], [d_xhalo], [d_xrh])
            if samp_g:
                cp(xs3[:, :, 0:3], st_in[:, c, 16:64].rearrange("p (s t) -> p s t", t=3), [d_stin], [d_xrh])
            xrb = []
            for t in range(3):
                bank, d_b, o, n = mmw(it(0), t)
                xrb.append((bank, d_b, o, n))
                samp = samp_g and t == 2
                npz = TW if samp else n
                act(xr_sb[:, 3 + o:3 + o + npz], bank[:, 0:npz], AF.Copy, [d_b], [d_xr[t]])
                if samp:
                    act(xs3[:, :, 3:7], bank[:, TW:TW + TS].rearrange("p (s t) -> p s t", t=ST), AF.Copy, [d_b], [d_xr[t]])
            for t, (bank, d_b, o, n) in enumerate(xrb):
                act(xc[:, o:o + n], bank[:, 0:n], AF.Identity, [d_b, d_vec], [d_xc[t]], bias=V(l, 5, c), scale=V(l, 4, c))
            conv_taps(G, 4, 1, l, c, 1, xc, d_xc)
            if g == 0:
                cp(xhalo[:, c, :], xr_sb[:, 3 + TPG - 3:3 + TPG], [d_xr[2]], [d_xhalo])
            else:
                cp(st_out[:, c, 1:4], xr_sb[:, 3 + TPG - 3:3 + TPG], [d_xr[2]], [d_stout])
                cp(st_out[:, c, 22:70].rearrange("p (s t) -> p s t", t=3), xs3[:, :, 4:7], [d_xr[2]], [d_stout])

            if part2b_prev[0] is not None:
                part2b_prev[0][2]()
            for t in range(3):
                bank, d_b, o, n = mmw(it(1), t)
                act(gD[:, o:o + n], bank[:, 0:n], AF.Copy, [d_b], [d_gD[t]])
            act(xcb[:, 0:NG], xc[:, 0:NG], AF.Copy, list(d_xc), list(d_xcb))

            if g == 1:
                cp(xr_sb[:, 1:3], chhalo[:, c, :], [d_chhalo], [d_xrh])
            if samp_g:
                cp(xs3[:, :, 1:3], st_in[:, c, 64:96].rearrange("p (s t) -> p s t", t=2), [d_stin], [d_xrh])
            for t in range(3):
                bank, d_b, o, n = mmw(it(2), t)
                samp = samp_g and t == 2
                npz = TW if samp else n
                act(xr_sb[:, 3 + o:3 + o + npz], bank[:, 0:npz], AF.Copy, [d_b], [d_xr[t]])
                if samp:
                    act(xs3[:, :, 3:7], bank[:, TW:TW + TS].rearrange("p (s t) -> p s t", t=ST), AF.Copy, [d_b], [d_xr[t]])
            if part2b_prev[0] is not None:
                part2b_prev[0][0]()
            act(gD[:, 0:NG], gD[:, 0:NG], AF.Gelu_apprx_tanh, ALLG, ALLG)
            for t in range(3):
                bank, d_b, o, n = mmw(it(3), t)
                samp = samp_g and t == 2
                npz = TW if samp else n
                tt(xr_sb[:, 3 + o:3 + o + npz], xr_sb[:, 3 + o:3 + o + npz], bank[:, 0:npz], ALU.mult, [d_b, d_xr[t]], [d_xr[t]])
                if samp:
                    tt(xs3[:, :, 3:7], xs3[:, :, 3:7], bank[:, TW:TW + TS].rearrange("p (s t) -> p s t", t=ST), ALU.mult,
                       [d_b, d_xr[t]], [d_xr[t]])
            steps = list(part2b_prev[0][1]) if part2b_prev[0] is not None else []

            def step():
                if steps:
                    steps.pop(0)()
            conv_taps(G, 3, 9, l, c, 0, vc, d_vc)
            for t in range(3):
                bank, d_b, o, n = mmw(it(4), t)
                tt(R[:, 8 + c, o:o + n], bank[:, 0:n], vc[:, o:o + n], ALU.mult, [d_b, d_vc[t]], [d_R[8 + c][t]])
            if g == 0:
                cp(chhalo[:, c, :], xr_sb[:, 3 + TPG - 2:3 + TPG], [d_xr[2]], [d_chhalo])
            else:
                cp(st_out[:, c, 4:6], xr_sb[:, 3 + TPG - 2:3 + TPG], [d_xr[2]], [d_stout])
                cp(st_out[:, c, 70:102].rearrange("p (s t) -> p s t", t=2), xs3[:, :, 5:7], [d_xr[2]], [d_stout])
            while steps:
                step()
            part2b_prev[0] = None

            for t, (o, n) in enumerate(tiles):
                br, d_br = next_bank()
                S.op("pe", lambda e, br=br, n=n, o=o, c=c: e.matmul(out=br[:, 0:n], lhsT=gw[:, c * 128:(c + 1) * 128],
                                                                     rhs=xcb[:, o:o + n], start=True, stop=True),
                     reads=[d_gw, d_xcb[t]], writes=[d_br])
                bi, d_bi = next_bank()
                S.op("pe", lambda e, bi=bi, n=n, o=o, c=c: e.matmul(out=bi[:, 0:n], lhsT=gw[:, 1024 + c * 128:1024 + (c + 1) * 128],
                                                                     rhs=xcb[:, o:o + n], start=True, stop=True),
                     reads=[d_gw, d_xcb[t]], writes=[d_bi])
                act(bA[:, o:o + n], br[:, 0:n], AF.Tanh, [d_br, d_der], [d_bA[t]], bias=DER(0, l, c), scale=0.5)
                act(bC[:, o:o + n], bi[:, 0:n], AF.Tanh, [d_bi, d_der], [d_bC[t]], bias=DER(1, l, c), scale=0.5)
            stt(bC[:, 0:NG], bC[:, 0:NG], 1.0, xc[:, 0:NG], ALU.add, ALU.mult, ALLC + list(d_xc), ALLC)

            def part2a_tail(c=c):
                act(bB[:, 0:NG], bA[:, 0:NG], AF.Tanh, ALLA + [d_der], ALLB, bias=DER(3, l, c), scale=DER(3, l, c))
                act(bA[:, 0:NG], bA[:, 0:NG], AF.Exp, ALLA + ALLB + [d_der], ALLA, bias=DER(2, l, c), scale=DER(2, l, c))

            def part2b_act(c=c, gD=gD, ALLG=ALLG):
                act(bB[:, 0:NG], bB[:, 0:NG], AF.Sqrt, ALLB, ALLB, scale=0.25)

            def s_w(c=c):
                stt(bB[:, 0:NG], bA[:, 0:NG], 1.0, bB[:, 0:NG], ALU.add, ALU.mult, ALLA + ALLB, ALLB)

            def s_uu(c=c):
                stt(bC[:, 0:NG], bB[:, 0:NG], 0.5e-6, bC[:, 0:NG], ALU.max, ALU.mult, ALLB + ALLC, ALLC)
                if samp_g:
                    a3 = bA[:, TPG:TPG + TS].rearrange("p (s t) -> p s t", t=ST)
                    u3 = bC[:, TPG:TPG + TS].rearrange("p (s t) -> p s t", t=ST)
                    tt(tmp16[:], a3[:, :, 0], st_in[:, c, 0:NS], ALU.mult, [d_bA[2], d_stin], [d_tmp16], small=True)
                    tt(u3[:, :, 0], u3[:, :, 0], tmp16[:], ALU.add, [d_tmp16, d_bC[2]], [d_bC[2]], small=True)
                    S.op("dve", lambda e, a3=a3: e.memset(a3[:, :, 0], 0.0), reads=[d_tmp16], writes=[d_bA[2]], small=True)

            def s_scan(t, c=c):
                o, n = tiles[t]
                samp = samp_g and t == 2
                npz = TW if samp else n
                if t == 0:
                    init = 0.0 if g == 0 else hcar[:, c:c + 1]
                    rdi = [] if g == 0 else [d_hcar]
                else:
                    init = bB[:, o - 1:o]
                    rdi = [d_bB[t - 1]]
                S.op("dve", lambda e, o=o, npz=npz, init=init: e.tensor_tensor_scan(
                    out=bB[:, o:o + npz], data0=bA[:, o:o + npz], data1=bC[:, o:o + npz], initial=init,
                    op0=ALU.mult, op1=ALU.add), reads=[d_bA[t], d_bC[t], d_bB[t]] + rdi, writes=[d_bB[t]])
                if samp:
                    h3 = bB[:, TPG:TPG + TS].rearrange("p (s t) -> p s t", t=ST)
                    S.op("dve", lambda e: e.tensor_tensor_scan(
                        out=bB[:, TPG:TPG + TS], data0=bA[:, TPG:TPG + TS], data1=bC[:, TPG:TPG + TS], initial=0.0,
                        op0=ALU.mult, op1=ALU.add), reads=[d_bA[t], d_bC[t], d_bB[t]], writes=[d_bB[t]], small=True)
                    cp(st_out[:, c, 0:1], bB[:, TPG - 1:TPG], [d_bB[t]], [d_stout])
                    cp(st_out[:, c, 6:22], h3[:, :, ST - 1], [d_bB[t]], [d_stout])
                if g == 0 and t == 2:
                    cp(hcar[:, c:c + 1], bB[:, TPG - 1:TPG], [d_bB[t]], [d_hcar])

            def s_ya(c=c, gD=gD, ALLG=ALLG):
                tt(R[:, c, 0:NG], bB[:, 0:NG], gD[:, 0:NG], ALU.mult, ALLB + ALLG, list(d_R[c]))

            part2b_dve = [s_w, s_uu, lambda: s_scan(0), lambda: s_scan(1), lambda: s_scan(2), s_ya]

            part2b_prev[0] = (part2b_act, part2b_dve, part2a_tail)
        return part2b_prev[0]

    def merge(l, g, tail):
        G = GROUPS[g]
        tiles = G["tiles"]
        for c in range(NCH):
            wb, d_wb = next_batch()
            prefetch()
            it = lambda i, wb=wb: wb[:, i * 1024:(i + 1) * 1024]
            sA, dA, sC, dC = (xc, d_xc, vc, d_vc) if c == 0 else (bA, d_bA, bC, d_bC)
            for gi_item, dsig, sigbuf in ((0, dA, sA), (1, dC, sC)):
                if c == 0 and gi_item == 1 and tail is not None:
                    pass
                for t, (o, n) in enumerate(tiles):
                    bank, d_b = next_bank()
                    mm_group(bank, d_b, n, [(it(gi_item)[:, k * 128:(k + 1) * 128], u_sb[:, k, o:o + n]) for k in range(NCH)],
                             [d_wb] + [d_u[k][t] for k in range(NCH)])
                    if c == 0:
                        S.op("dve", lambda e, sigbuf=sigbuf, bank=bank, o=o, n=n: e.tensor_copy(out=sigbuf[:, o:o + n], in_=bank[:, 0:n]),
                             reads=[d_b], writes=[dsig[t]])
                        act(sigbuf[:, o:o + n], sigbuf[:, o:o + n], AF.Sigmoid, [dsig[t]], [dsig[t]])
                    else:
                        act(sigbuf[:, o:o + n], bank[:, 0:n], AF.Sigmoid, [d_b], [dsig[t]])
            if c == 0:
                for st_ in tail[1]:
                    st_()
            for t, (o, n) in enumerate(tiles):
                bank, d_b = next_bank()
                mm_group(bank, d_b, n, [(it(2)[:, k * 128:(k + 1) * 128], R[:, 8 + k, o:o + n]) for k in range(NCH)],
                         [d_wb] + [d_R[8 + k][t] for k in range(NCH)])
                tt(bD[:, o:o + n], bank[:, 0:n], sC[:, o:o + n], ALU.mult, [d_b, dC[t]], [d_bD[t]])
            for t, (o, n) in enumerate(tiles):
                bank, d_b = next_bank()
                mm_group(bank, d_b, n, [(it(3)[:, k * 128:(k + 1) * 128], R[:, k, o:o + n]) for k in range(NCH)],
                         [d_wb] + [d_R[k][t] for k in range(NCH)])
                tt(bB[:, o:o + n], bank[:, 0:n], sA[:, o:o + n], ALU.mult, [d_b, dA[t]], [d_bB[t]])
            for t, (o, n) in enumerate(tiles):
                tt(R[:, 16 + c, o:o + n], bB[:, o:o + n], bD[:, o:o + n], ALU.add, [d_bB[t], d_bD[t]], [d_R[16 + c][t]])

    def wout_norm2(l, g):
        G = GROUPS[g]
        tiles = G["tiles"]
        nb = [hold_bank() for _ in range(3)]
        for bi in range(2):
            wb, d_wb = next_batch()
            prefetch()
            for j in range(4):
                oc = bi * 4 + j
                item = wb[:, j * 1024:(j + 1) * 1024]
                for t, (o, n) in enumerate(tiles):
                    gt = g * 3 + t
                    go = G["off"] + o
                    bank, d_b = next_bank()
                    mm_group(bank, d_b, n, [(item[:, k * 128:(k + 1) * 128], R[:, 16 + k, o:o + n]) for k in range(NCH)],
                             [d_wb] + [d_R[16 + k][t] for k in range(NCH)])
                    tt(x_sb[:, oc, go:go + n], x_sb[:, oc, go:go + n], bank[:, 0:n], ALU.add, [d_b, d_x[oc][gt]], [d_x[oc][gt]])
                if oc > 0:
                    for t in range(3):
                        norm_sq_chunk(g, t, oc - 1, nb[t][0], nb[t][1])
        for t in range(3):
            norm_sq_chunk(g, t, NCH - 1, nb[t][0], nb[t][1])
        for t in range(3):
            norm_finish(l, g, t, 12, nb[t][0], nb[t][1])
            release_bank(nb[t])

    def ffn(l, g, nxt):
        G = GROUPS[g]
        tiles = G["tiles"]
        sbufs = [(bB, d_bB), (bC, d_bC), (bD, d_bD)]
        for bi in range(11):
            wb, d_wb = next_batch()
            prefetch()
            for jj in range(2):
                j = bi * 2 + jj
                gate = wb[:, (2 * jj) * 1024:(2 * jj + 1) * 1024]
                up = wb[:, (2 * jj + 1) * 1024:(2 * jj + 2) * 1024]
                sbuf_, dsb = sbufs[j % 3]
                for t, (o, n) in enumerate(tiles):
                    bank, d_b = next_bank()
                    mm_group(bank, d_b, n, [(gate[:, k * 128:(k + 1) * 128], u_sb[:, k, o:o + n]) for k in range(NCH)],
                             [d_wb] + [d_u[k][t] for k in range(NCH)])
                    act(sbuf_[:, o:o + n], bank[:, 0:n], AF.Silu, [d_b], [dsb[t]])
                for t, (o, n) in enumerate(tiles):
                    bank, d_b = next_bank()
                    mm_group(bank, d_b, n, [(up[:, k * 128:(k + 1) * 128], u_sb[:, k, o:o + n]) for k in range(NCH)],
                             [d_wb] + [d_u[k][t] for k in range(NCH)])
                    tt(R[:, j, o:o + n], bank[:, 0:n], sbuf_[:, o:o + n], ALU.mult, [d_b, dsb[t]], [d_R[j][t]])
        hoist = {1: 0, 3: 1, 5: 2}
        for oc in range(NCH):
            wb, d_wb = next_batch()
            prefetch()
            for t, (o, n) in enumerate(tiles):
                gt = g * 3 + t
                go = G["off"] + o
                bank, d_b = next_bank()
                mm_group(bank, d_b, n, [(wb[:, k * 128:(k + 1) * 128], R[:, k, o:o + n]) for k in range(NFF)],
                         [d_wb] + [d_R[k][t] for k in range(NFF)])
                tt(x_sb[:, oc, go:go + n], x_sb[:, oc, go:go + n], bank[:, 0:n], ALU.add, [d_b, d_x[oc][gt]], [d_x[oc][gt]])
            if nxt is not None and oc in hoist:
                norm_tile(nxt[0], nxt[1], hoist[oc], 0)

    seq = [(l, g) for l in range(depth) for g in range(2)]
    for t in range(3):
        norm_tile(0, 0, t, 0)
    for i, (l, g) in enumerate(seq):
        if g == 0:
            S.dma("pool", lambda e, l=l: e.dma_start(out=gw[:], in_=ws_d[l, :, 0:W_GATES], max_dma_last_dim=4096), "gwl",
                  writes=[d_gw])
            S.dma("sp", lambda e, l=l: e.dma_start(out=st_in[:].rearrange("p c n -> p (c n)"), in_=stin_d[l]), "ldst",
                  writes=[d_stin])
        tail = mixer(l, g)
        tail[2]()
        tail[0]()
        merge(l, g, tail)
        if g == 1:
            S.dma("sp", lambda e, l=l: e.dma_start(out=sto_d[l], in_=st_out[:].rearrange("p c n -> p (c n)")), "st",
                  reads=[d_stout])
        wout_norm2(l, g)
        ffn(l, g, seq[i + 1] if i + 1 < len(seq) else None)

    for g in range(2):
        for t in range(3):
            norm_tile(0, g, t, 0, final=True)
    for c0 in range(0, NCH, 2):
        S.dma("sp", lambda e, c0=c0: e.dma_start(out=y_d[:, c0:c0 + 2, :], in_=x_sb[:, c0:c0 + 2, :]), "st",
              reads=[d_x[c][t] for c in (c0, c0 + 1) for t in range(6)])

    sem_keys = list(Sched.ENGS) + sorted(S.dma_cnt.keys())
    sems = {k: es.enter_context(nc.semaphore("s_" + k)) for k in sem_keys}

    def emit(name, e):
        for waits, fn, key, inc in S.streams[name]:
            for k, v in waits:
                e.wait_ge(sems[k], v)
            ins = fn(e)
            ins.then_inc(sems[key], inc)
        if name == "sp":
            for k, v in S.dma_cnt.items():
                e.wait_ge(sems[k], v)

    with nc.Block() as block:
        @block.tensor
        def _(e):
            emit("pe", e)

        @block.scalar
        def _(e):
            emit("act", e)

        @block.vector
        def _(e):
            emit("dve", e)

        @block.gpsimd
        def _(e):
            emit("pool", e)

        @block.sync
        def _(e):
            emit("sp", e)
    es.close()
    return nc


def _fm(a):
    T = a.shape[0]
    return np.ascontiguousarray(a.reshape(T, NCH, 128).transpose(2, 1, 0))


def _pack_weights(inp):
    ws = np.zeros((DEPTH, 128, WS_LAYER), np.float32)
    for l in range(DEPTH):
        off = 0
        gwl = np.zeros((128, 2, 8, 128), np.float32)
        for gi, name in enumerate(("gate_a_w", "gate_x_w")):
            w = inp[name][l]
            w2 = w.reshape(8, 2, 64, 64)
            gwl[0:64, gi, :, 0:64] = w2[:, 0].transpose(1, 0, 2)
            gwl[64:128, gi, :, 64:128] = w2[:, 1].transpose(1, 0, 2)
        ws[l, :, off:off + W_GATES] = gwl.reshape(128, W_GATES); off += W_GATES

        def item(W, col0):
            K = W.shape[0]
            blk = W[:, col0:col0 + 128].reshape(K // 128, 128, 128)
            return blk.transpose(1, 0, 2).reshape(128, K)

        w_in = inp["w_in"][l]
        for c in range(8):
            for s in ("xr", "gr", "cc", "hc", "bc"):
                ws[l, :, off:off + 1024] = item(w_in, OFF[s] + c * 128); off += 1024
        wa, wbm = inp["w_branch_a"][l], inp["w_branch_b"][l]
        for c in range(8):
            ws[l, :, off:off + 1024] = item(w_in, OFF["ga"] + c * 128); off += 1024
            ws[l, :, off:off + 1024] = item(w_in, OFF["gb"] + c * 128); off += 1024
            ws[l, :, off:off + 1024] = item(wbm, c * 128); off += 1024
            ws[l, :, off:off + 1024] = item(wa, c * 128); off += 1024
        wo = inp["w_out"][l]
        for c in range(8):
            ws[l, :, off:off + 1024] = item(wo, c * 128); off += 1024
        wg, wu = inp["w_ff_gate"][l], inp["w_ff_up"][l]
        for j in range(NFF):
            ws[l, :, off:off + 1024] = item(wg, j * 128); off += 1024
            ws[l, :, off:off + 1024] = item(wu, j * 128); off += 1024
        wd = inp["w_ff_down"][l]
        for c in range(8):
            ws[l, :, off:off + W_DN] = item(wd, c * 128); off += W_DN
        assert off == WS_LAYER
    return ws


def _pack_vecs(inp):
    v = np.zeros((128, NVEC * DEPTH * 8 + 8), np.float32)
    rows = []
    for l in range(DEPTH):
        rows.append([inp["norm1_g"][l]] + [inp["rnn_conv_w"][l, k] for k in range(4)] + [inp["rnn_conv_b"][l],
                    inp["gate_a_b"][l], inp["gate_x_b"][l], inp["lru_lambda"][l]] +
                    [inp["sc_conv_w"][l, k] for k in range(3)] + [inp["norm2_g"][l]])
    for i in range(NVEC):
        for l in range(DEPTH):
            o = (i * DEPTH + l) * 8
            v[:, o:o + 8] = np.asarray(rows[l][i]).reshape(8, 128).T
    v[:, NVEC * DEPTH * 8:] = np.asarray(inp["final_norm_g"]).reshape(8, 128).T
    return v


def kernel(**inputs):
    inp = {k: np.asarray(v) for k, v in inputs.items()}
    nc = build_program(DEPTH)
    ws = _pack_weights(inp)
    vecs = _pack_vecs(inp)
    meta = _fm(inp["meta_tokens"].astype(np.float32))
    in_maps = []
    for i in range(NCORES):
        xp = _fm(inp["x_prompt"][i])
        xs = _fm(inp["x_sample"][i * NS:(i + 1) * NS].reshape(TS, D))
        stin = np.zeros((DEPTH, 128, NCH, NSTI), np.float32)
        h0 = inp["state_rnn_h"][:, i * NS:(i + 1) * NS]
        rc = inp["state_rnn_conv"][:, i * NS:(i + 1) * NS]
        sc = inp["state_sc_conv"][:, i * NS:(i + 1) * NS]
        stin[:, :, :, 0:16] = h0.reshape(DEPTH, NS, NCH, 128).transpose(0, 3, 2, 1)
        stin[:, :, :, 16:64] = rc.reshape(DEPTH, NS, 3, NCH, 128).transpose(0, 4, 3, 1, 2).reshape(DEPTH, 128, NCH, 48)
        stin[:, :, :, 64:96] = sc.reshape(DEPTH, NS, 2, NCH, 128).transpose(0, 4, 3, 1, 2).reshape(DEPTH, 128, NCH, 32)
        in_maps.append(dict(xp=xp, meta=meta, xs=xs, stin=np.ascontiguousarray(stin.reshape(DEPTH, 128, NCH * NSTI)),
                            vecs=vecs, ws=ws))
    res = run_bass_kernel_spmd(nc, in_maps, core_ids=list(range(NCORES)))
    y_prompt = np.zeros((NCORES, SEQ, D), np.float32)
    y_sample = np.zeros((NCORES * NS, ST, D), np.float32)
    rnn_h_p = np.zeros((DEPTH, NCORES, D), np.float32)
    rnn_c_p = np.zeros((DEPTH, NCORES, 3, D), np.float32)
    sc_c_p = np.zeros((DEPTH, NCORES, 2, D), np.float32)
    rnn_h_s = np.zeros((DEPTH, NCORES * NS, D), np.float32)
    rnn_c_s = np.zeros((DEPTH, NCORES * NS, 3, D), np.float32)
    sc_c_s = np.zeros((DEPTH, NCORES * NS, 2, D), np.float32)
    for i in range(NCORES):
        r = res.results[i]
        y = np.asarray(r["y"]).reshape(128, NCH, TTOT)
        yt = y.transpose(2, 1, 0).reshape(TTOT, D)
        y_prompt[i] = yt[NMETA:TP]
        y_sample[i * NS:(i + 1) * NS] = yt[TP:].reshape(NS, ST, D)
        so = np.asarray(r["sto"]).reshape(DEPTH, 128, NCH, NSTO)
        so = so.transpose(0, 3, 2, 1).reshape(DEPTH, NSTO, D)
        rnn_h_p[:, i] = so[:, 0]
        rnn_c_p[:, i] = so[:, 1:4]
        sc_c_p[:, i] = so[:, 4:6]
        rnn_h_s[:, i * NS:(i + 1) * NS] = so[:, 6:22]
        rnn_c_s[:, i * NS:(i + 1) * NS] = so[:, 22:70].reshape(DEPTH, NS, 3, D)
        sc_c_s[:, i * NS:(i + 1) * NS] = so[:, 70:102].reshape(DEPTH, NS, 2, D)
    return (y_prompt, y_sample, rnn_h_p, rnn_c_p, sc_c_p, rnn_h_s, rnn_c_s, sc_c_s)
```

```python
import numpy as np
from contextlib import ExitStack
import concourse.bass as bass
import concourse.mybir as mybir
from concourse.bass_utils import run_bass_kernel_spmd

F32 = mybir.dt.float32
BF16 = mybir.dt.bfloat16
AF = mybir.ActivationFunctionType
ALU = mybir.AluOpType

D = 1024
NCH = 8
DFF = 2816
NFF = 22
DEPTH = 4
NMETA = 16
SEQ = 2048
TP = NMETA + SEQ
NS = 16
ST = 4
TS = NS * ST
TTOT = TP + TS
NCORES = 8
EPS = 1e-6
OFF = dict(xr=0, gr=1024, bc=2048, cc=3072, hc=4096, ga=5120, gb=6144)
TPG = TP // 2
TW = 344
GROUPS = [
    dict(off=0, n=TPG, tiles=[(0, TW), (TW, TW), (2 * TW, TW)], samp=False),
    dict(off=TPG, n=TPG + TS, tiles=[(0, TW), (TW, TW), (2 * TW, TW + TS)], samp=True),
]
GW = TPG + TS
NVEC = 13
NSTI = 96
NSTO = 102
W_GATES = 2 * 8 * 128
W_MIX = 5 * 1024
W_MRG = 4 * 1024
W_OUT = 4 * 1024
W_FFN = 4 * 1024
W_DN = NFF * 128
WS_LAYER = W_GATES + 8 * W_MIX + 8 * W_MRG + 2 * W_OUT + 11 * W_FFN + 8 * W_DN
WBUF = 5120
XRW = 1160
SB0 = 1040
NSQ = 3

SAME_SYNC_ALL = True


class Dep:
    __slots__ = ("w", "r")

    def __init__(self):
        self.w = None
        self.r = {}


class Sched:
    ENGS = ("pe", "act", "dve", "pool", "sp")

    def __init__(self):
        self.streams = {e: [] for e in self.ENGS}
        self.tick = {e: 0 for e in self.ENGS}
        self.known = {e: {} for e in self.ENGS}
        self.dma_cnt = {}

    def _waits(self, eng, reads, writes, small):
        waits = {}

        def need(k, v):
            if k == eng and (eng == "pe" or not (small or SAME_SYNC_ALL)):
                return
            if self.known[eng].get(k, 0) >= v:
                return
            if waits.get(k, 0) < v:
                waits[k] = v

        for t in reads:
            if t.w is not None:
                need(*t.w)
        for t in writes:
            if t.w is not None:
                need(*t.w)
            for k, v in t.r.items():
                need(k, v)
        for k, v in waits.items():
            self.known[eng][k] = v
        return list(waits.items())

    def _mark(self, tok, reads, writes):
        k, v = tok
        for t in reads:
            t.r[k] = v
        for t in writes:
            t.w = tok
            t.r = {}

    def op(self, eng, fn, reads=(), writes=(), small=False):
        waits = self._waits(eng, reads, writes, small)
        self.tick[eng] += 1
        tok = (eng, self.tick[eng])
        self.streams[eng].append((waits, fn, eng, 1))
        self._mark(tok, reads, writes)

    def dma(self, eng, fn, semkey, reads=(), writes=()):
        waits = self._waits(eng, reads, writes, False)
        self.dma_cnt[semkey] = self.dma_cnt.get(semkey, 0) + 16
        tok = (semkey, self.dma_cnt[semkey])
        self.streams[eng].append((waits, fn, semkey, 16))
        self._mark(tok, reads, writes)


def build_program(depth=DEPTH):
    nc = bass.Bass("TRN2", target_bir_lowering=False)
    xp_d = nc.dram_tensor("xp", [128, NCH, SEQ], F32, kind="ExternalInput").ap()
    meta_d = nc.dram_tensor("meta", [128, NCH, NMETA], F32, kind="ExternalInput").ap()
    xs_d = nc.dram_tensor("xs", [128, NCH, TS], F32, kind="ExternalInput").ap()
    stin_d = nc.dram_tensor("stin", [DEPTH, 128, NCH * NSTI], F32, kind="ExternalInput").ap()
    vecs_d = nc.dram_tensor("vecs", [128, NVEC * DEPTH * 8 + 8], F32, kind="ExternalInput").ap()
    ws_d = nc.dram_tensor("ws", [DEPTH, 128, WS_LAYER], F32, kind="ExternalInput").ap()
    y_d = nc.dram_tensor("y", [128, NCH, TTOT], F32, kind="ExternalOutput").ap()
    sto_d = nc.dram_tensor("sto", [DEPTH, 128, NCH * NSTO], F32, kind="ExternalOutput").ap()

    S = Sched()
    es = ExitStack()

    def sb(name, shape, dt):
        return es.enter_context(nc.sbuf_tensor(name, shape, dt))

    x_sb = sb("x_sb", [128, NCH, TTOT], F32)
    u_sb = sb("u_sb", [128, NCH, GW], BF16)
    R = sb("R", [128, 24, GW], BF16)
    wbuf = [sb("wbuf0", [128, WBUF], BF16), sb("wbuf1", [128, WBUF], BF16)]
    gw = sb("gw", [128, W_GATES], BF16)
    vec_sb = sb("vec_sb", [128, NVEC * DEPTH * 8 + 8], F32)
    der_sb = sb("der_sb", [128, 4 * DEPTH * 8], F32)
    dtmp = [sb("dtmp%d" % i, [128, DEPTH * 8], F32) for i in range(6)]
    ones_bf = sb("ones_bf", [128, 128], BF16)
    xr_sb = sb("xr_sb", [128, XRW], F32)
    xc = sb("xc", [128, GW], F32)
    xcb = sb("xcb", [128, GW], BF16)
    vc = sb("vc", [128, GW], F32)
    bA = sb("bA", [128, GW], F32)
    bB = sb("bB", [128, GW], F32)
    bC = sb("bC", [128, GW], F32)
    bD = sb("bD", [128, GW], F32)
    bD1 = sb("bD1", [128, GW], F32)
    sqb = [sb("sqb%d" % i, [128, 416], BF16) for i in range(NSQ)]
    st_in = sb("st_in", [128, NCH, NSTI], F32)
    st_out = sb("st_out", [128, NCH, NSTO], F32)
    hcar = sb("hcar", [128, NCH], F32)
    xhalo = sb("xhalo", [128, NCH, 3], F32)
    chhalo = sb("chhalo", [128, NCH, 2], F32)
    tmp16 = sb("tmp16", [128, NS], F32)
    banks = [es.enter_context(nc.psum_tensor("ps%d" % i, [128, 512], F32)) for i in range(8)]

    d_x = [[Dep() for _ in range(6)] for _ in range(NCH)]
    d_u = [[Dep() for _ in range(3)] for _ in range(NCH)]
    d_R = [[Dep() for _ in range(3)] for _ in range(24)]
    d_wbuf = [Dep(), Dep()]
    d_gw = Dep()
    d_vec = Dep()
    d_der = Dep()
    d_dtmp = [Dep() for _ in range(6)]
    d_ones = Dep()
    d_xr = [Dep() for _ in range(3)]
    d_xrh = Dep()
    d_xc = [Dep() for _ in range(3)]
    d_xcb = [Dep() for _ in range(3)]
    d_vc = [Dep() for _ in range(3)]
    d_bA = [Dep() for _ in range(3)]
    d_bB = [Dep() for _ in range(3)]
    d_bC = [Dep() for _ in range(3)]
    d_bD = [Dep() for _ in range(3)]
    d_bD1 = [Dep() for _ in range(3)]
    d_sqb = [Dep() for _ in range(NSQ)]
    d_stin = Dep()
    d_stout = Dep()
    d_hcar = Dep()
    d_xhalo = Dep()
    d_chhalo = Dep()
    d_tmp16 = Dep()
    d_bank = [Dep() for _ in range(8)]
    d_y = Dep()

    bank_ctr = [0]

    held = set()

    def next_bank():
        while True:
            b = bank_ctr[0] % 8
            bank_ctr[0] += 1
            if b not in held:
                return banks[b], d_bank[b]

    def hold_bank():
        bk = next_bank()
        held.add(banks.index(bk[0]))
        return bk

    def release_bank(bk):
        held.discard(banks.index(bk[0]))

    def V(l, i, c):
        o = (i * DEPTH + l) * 8 + c
        return vec_sb[:, o:o + 1]

    def DER(kind, l, c):
        o = kind * DEPTH * 8 + l * 8 + c
        return der_sb[:, o:o + 1]

    batches = []
    for l in range(depth):
        for g in range(2):
            off = W_GATES
            for _ in range(8):
                batches.append((l, off, W_MIX)); off += W_MIX
            for _ in range(8):
                batches.append((l, off, W_MRG)); off += W_MRG
            for _ in range(2):
                batches.append((l, off, W_OUT)); off += W_OUT
            for _ in range(11):
                batches.append((l, off, W_FFN)); off += W_FFN
            for _ in range(8):
                batches.append((l, off, W_DN)); off += W_DN
            assert off == WS_LAYER
    bstate = dict(issued=0, used=0)

    def issue_batch():
        i = bstate["issued"]
        if i >= len(batches):
            return
        l, off, n = batches[i]
        b = i % 2
        src = ws_d[l, :, off:off + n]
        dst = wbuf[b][:, 0:n]
        S.dma("pool", lambda e, src=src, dst=dst: e.dma_start(out=dst, in_=src, max_dma_last_dim=4096),
              "w%d" % b, reads=(), writes=(d_wbuf[b],))
        bstate["issued"] += 1

    def next_batch():
        i = bstate["used"]
        while bstate["issued"] <= min(i, len(batches) - 1):
            issue_batch()
        b = i % 2
        bstate["used"] += 1
        return wbuf[b], d_wbuf[b]

    def prefetch():
        if bstate["issued"] < bstate["used"] + 1:
            issue_batch()

    def mm_group(bank, d_b, n, pairs, reads):
        def fn(e, bank=bank, n=n, pairs=pairs):
            last = None
            for i, (lt, rh) in enumerate(pairs):
                last = e.matmul(out=bank[:, 0:n], lhsT=lt, rhs=rh, start=(i == 0), stop=(i == len(pairs) - 1))
            return last
        S.op("pe", fn, reads=reads, writes=(d_b,))

    def act(out, in_, func, reads, writes, bias=None, scale=None, small=False):
        kw = {}
        if bias is not None:
            kw["bias"] = bias
        if scale is not None:
            kw["scale"] = scale
        S.op("act", lambda e: e.activation(out=out, in_=in_, func=func, **kw), reads=reads, writes=writes, small=small)

    def tt(out, in0, in1, op, reads, writes, small=False):
        S.op("dve", lambda e: e.tensor_tensor(out=out, in0=in0, in1=in1, op=op), reads=reads, writes=writes, small=small)

    def ts(out, in0, s1, op0, reads, writes, s2=None, op1=None, small=False):
        if op1 is None:
            S.op("dve", lambda e: e.tensor_scalar(out=out, in0=in0, scalar1=s1, scalar2=None, op0=op0),
                 reads=reads, writes=writes, small=small)
        else:
            S.op("dve", lambda e: e.tensor_scalar(out=out, in0=in0, scalar1=s1, scalar2=s2, op0=op0, op1=op1),
                 reads=reads, writes=writes, small=small)

    def stt(out, in0, scalar, in1, op0, op1, reads, writes, small=False):
        S.op("dve", lambda e: e.scalar_tensor_tensor(out=out, in0=in0, scalar=scalar, in1=in1, op0=op0, op1=op1),
             reads=reads, writes=writes, small=small)

    def cp(out, in_, reads, writes, small=True):
        S.op("dve", lambda e: e.tensor_copy(out=out, in_=in_), reads=reads, writes=writes, small=small)

    S.dma("sp", lambda e: e.dma_start(out=vec_sb[:], in_=vecs_d), "ldv", writes=(d_vec,))
    for c0 in range(0, NCH, 2):
        S.dma("sp", lambda e, c0=c0: e.dma_start(out=x_sb[:, c0:c0 + 2, NMETA:TP], in_=xp_d[:, c0:c0 + 2, :]), "ldx%d" % c0,
              writes=[d_x[c][t] for c in (c0, c0 + 1) for t in range(6)])
    S.dma("sp", lambda e: e.dma_start(out=x_sb[:, :, 0:NMETA], in_=meta_d), "ldm",
          writes=[d_x[c][0] for c in range(NCH)])
    S.dma("sp", lambda e: e.dma_start(out=x_sb[:, :, TP:TTOT], in_=xs_d), "lds",
          writes=[d_x[c][5] for c in range(NCH)])
    S.op("dve", lambda e: e.memset(ones_bf[:], 1.0), writes=(d_ones,))

    NL = DEPTH * 8
    lam = vec_sb[:, 8 * NL:9 * NL]
    t0, t1, t2, t3, t4, t5 = [t[:] for t in dtmp]
    dd = d_dtmp
    ts(t0, lam, -1.0, ALU.mult, [d_vec], [dd[0]], small=True)
    tt(t0, t0, lam, ALU.min, [d_vec, dd[0]], [dd[0]], small=True)
    act(t1, t0, AF.Exp, [dd[0]], [dd[1]], small=True)
    ts(t2, t1, 2.0, ALU.add, [dd[1]], [dd[2]], small=True)
    S.op("dve", lambda e: e.reciprocal(out=t2, in_=t2), reads=[dd[2]], writes=[dd[2]], small=True)
    tt(t3, t1, t2, ALU.mult, [dd[1], dd[2]], [dd[3]], small=True)
    tt(t4, t3, t3, ALU.mult, [dd[3]], [dd[4]], small=True)
    S.op("dve", lambda e: e.memset(t5, 0.0), writes=[dd[5]], small=True)
    for k in range(9, 0, -1):
        stt(t5, t5, 1.0 / (2 * k + 1), t4, ALU.add, ALU.mult, [dd[5], dd[4]], [dd[5]], small=True)
    stt(t5, t5, 1.0, t3, ALU.add, ALU.mult, [dd[5], dd[3]], [dd[5]], small=True)
    ts(t0, lam, -1.0, ALU.mult, [d_vec, dd[0]], [dd[0]], s2=0.0, op1=ALU.max, small=True)
    stt(t1, t5, 2.0, t0, ALU.mult, ALU.add, [dd[5], dd[0], dd[1]], [dd[1]], small=True)
    ts(der_sb[:, 0 * NL:1 * NL], vec_sb[:, 6 * NL:7 * NL], 0.5, ALU.mult, [d_vec], [d_der], small=True)
    ts(der_sb[:, 1 * NL:2 * NL], vec_sb[:, 7 * NL:8 * NL], 0.5, ALU.mult, [d_vec], [d_der], small=True)
    ts(der_sb[:, 2 * NL:3 * NL], t1, -4.0, ALU.mult, [dd[1]], [d_der], small=True)
    ts(der_sb[:, 3 * NL:4 * NL], t1, 2.0, ALU.mult, [dd[1]], [d_der], small=True)

    sq_ctr = [0]

    def norm_sq_chunk(g, t, c, bank, d_b):
        G = GROUPS[g]
        o, n = G["tiles"][t]
        gt = g * 3 + t
        go = G["off"] + o
        q = sq_ctr[0] % NSQ
        sq_ctr[0] += 1
        act(sqb[q][:, 0:n], x_sb[:, c, go:go + n], AF.Square, [d_x[c][gt]], [d_sqb[q]])
        S.op("pe", lambda e: e.matmul(out=bank[:, 0:n], lhsT=ones_bf[:], rhs=sqb[q][:, 0:n],
                                      start=(c == 0), stop=(c == NCH - 1)),
             reads=[d_sqb[q], d_ones], writes=[d_b])

    def norm_finish(l, g, t, gi, bank, d_b, final=False):
        G = GROUPS[g]
        o, n = G["tiles"][t]
        gt = g * 3 + t
        go = G["off"] + o
        act(bA[:, o:o + n], bank[:, 0:n], AF.Ln, [d_b], [d_bA[t]], bias=EPS, scale=1.0 / D)
        act(bA[:, o:o + n], bA[:, o:o + n], AF.Exp, [d_bA[t]], [d_bA[t]], scale=-0.5)
        for c in range(NCH):
            if final:
                fo = NVEC * DEPTH * 8 + c
                stt(x_sb[:, c, go:go + n], x_sb[:, c, go:go + n], vec_sb[:, fo:fo + 1], bA[:, o:o + n], ALU.mult, ALU.mult,
                    [d_x[c][gt], d_bA[t], d_vec], [d_x[c][gt]])
            else:
                stt(u_sb[:, c, o:o + n], x_sb[:, c, go:go + n], V(l, gi, c), bA[:, o:o + n], ALU.mult, ALU.mult,
                    [d_x[c][gt], d_bA[t], d_vec], [d_u[c][t]])

    def norm_tile(l, g, t, gi, final=False):
        bank, d_b = next_bank()
        for c in range(NCH):
            norm_sq_chunk(g, t, c, bank, d_b)
        norm_finish(l, g, t, gi, bank, d_b, final)

    def conv_taps(G, width, wvec, l, c, k0, outbuf, d_out, ks=None):
        xs3 = xr_sb[:, SB0:SB0 + NS * 7].rearrange("p (s t) -> p s t", t=7)
        xo = outbuf[:, TPG:TPG + TS].rearrange("p (s t) -> p s t", t=ST)
        rd = list(d_xr) + [d_xrh, d_vec]
        for k in (range(k0, width) if ks is None else ks):
            wk = V(l, wvec + (width - 1 - k), c)
            if k == 0:
                ts(outbuf[:, 0:TPG], xr_sb[:, 3:3 + TPG], wk, ALU.mult, rd, list(d_out))
            else:
                stt(outbuf[:, 0:TPG], xr_sb[:, 3 - k:3 - k + TPG], wk, outbuf[:, 0:TPG], ALU.mult, ALU.add,
                    rd + list(d_out), list(d_out))
            if G["samp"]:
                if k == 0:
                    ts(xo, xs3[:, :, 3:7], wk, ALU.mult, rd, [d_out[2]])
                else:
                    stt(xo, xs3[:, :, 3 - k:7 - k], wk, xo, ALU.mult, ALU.add, rd + [d_out[2]], [d_out[2]])

    def mixer(l, g):
        G = GROUPS[g]
        tiles = G["tiles"]
        samp_g = G["samp"]
        xs3 = xr_sb[:, SB0:SB0 + NS * 7].rearrange("p (s t) -> p s t", t=7)
        part2b_prev = [None]
        NG = G["n"]
        ALLA, ALLB, ALLC, ALLD = list(d_bA), list(d_bB), list(d_bC), list(d_bD)

        for c in range(NCH):
            wb, d_wb = next_batch()
            prefetch()
            it = lambda i, wb=wb: wb[:, i * 1024:(i + 1) * 1024]
            gD, d_gD = (bD, d_bD) if c % 2 == 0 else (bD1, d_bD1)
            ALLG = list(d_gD)

            def mmw(item, t, d_wb=d_wb):
                o, n = tiles[t]
                bank, d_b = next_bank()
                mm_group(bank, d_b, n, [(item[:, k * 128:(k + 1) * 128], u_sb[:, k, o:o + n]) for k in range(NCH)],
                         [d_wb] + [d_u[k][t] for k in range(NCH)])
                return bank, d_b, o, n

            if g == 0:
                S.op("dve", lambda e: e.memset(xr_sb[:, 0:3], 0.0), reads=[], writes=[d_xrh], small=True)
            else:
                cp(xr_sb[:, 0:3], xhalo[:, c, :], [d_xhalo], [d_xrh])
            if samp_g:
                cp(xs3[:, :, 0:3], st_in[:, c, 16:64].rearrange("p (s t) -> p s t", t=3), [d_stin], [d_xrh])
            xrb = []
            for t in range(3):
                bank, d_b, o, n = mmw(it(0), t)
                xrb.append((bank, d_b, o, n))
                samp = samp_g and t == 2
                npz = TW if samp else n
                act(xr_sb[:, 3 + o:3 + o + npz], bank[:, 0:npz], AF.Copy, [d_b], [d_xr[t]])
                if samp:
                    act(xs3[:, :, 3:7], bank[:, TW:TW + TS].rearrange("p (s t) -> p s t", t=ST), AF.Copy, [d_b], [d_xr[t]])
            for t, (bank, d_b, o, n) in enumerate(xrb):
                act(xc[:, o:o + n], bank[:, 0:n], AF.Identity, [d_b, d_vec], [d_xc[t]], bias=V(l, 5, c), scale=V(l, 4, c))
            conv_taps(G, 4, 1, l, c, 1, xc, d_xc)
            if g == 0:
                cp(xhalo[:, c, :], xr_sb[:, 3 + TPG - 3:3 + TPG], [d_xr[2]], [d_xhalo])
            else:
                cp(st_out[:, c, 1:4], xr_sb[:, 3 + TPG - 3:3 + TPG], [d_xr[2]], [d_stout])
                cp(st_out[:, c, 22:70].rearrange("p (s t) -> p s t", t=3), xs3[:, :, 4:7], [d_xr[2]], [d_stout])

            if part2b_prev[0] is not None:
                part2b_prev[0][2]()
            for t in range(3):
                bank, d_b, o, n = mmw(it(1), t)
                act(gD[:, o:o + n], bank[:, 0:n], AF.Copy, [d_b], [d_gD[t]])
            act(xcb[:, 0:NG], xc[:, 0:NG], AF.Copy, list(d_xc), list(d_xcb))

            if g == 1:
                cp(xr_sb[:, 1:3], chhalo[:, c, :], [d_chhalo], [d_xrh])
            if samp_g:
                cp(xs3[:, :, 1:3], st_in[:, c, 64:96].rearrange("p (s t) -> p s t", t=2), [d_stin], [d_xrh])
            for t in range(3):
                bank, d_b, o, n = mmw(it(2), t)
                samp = samp_g and t == 2
                npz = TW if samp else n
                act(xr_sb[:, 3 + o:3 + o + npz], bank[:, 0:npz], AF.Copy, [d_b], [d_xr[t]])
                if samp:
                    act(xs3[:, :, 3:7], bank[:, TW:TW + TS].rearrange("p (s t) -> p s t", t=ST), AF.Copy, [d_b], [d_xr[t]])
            if part2b_prev[0] is not None:
                part2b_prev[0][0]()
            act(gD[:, 0:NG], gD[:, 0:NG], AF.Gelu_apprx_tanh, ALLG, ALLG)
            for t in range(3):
                bank, d_b, o, n = mmw(it(3), t)
                samp = samp_g and t == 2
                npz = TW if samp else n
                tt(xr_sb[:, 3 + o:3 + o + npz], xr_sb[:, 3 + o:3 + o + npz], bank[:, 0:npz], ALU.mult, [d_b, d_xr[t]], [d_xr[t]])
                if samp:
                    tt(xs3[:, :, 3:7], xs3[:, :, 3:7], bank[:, TW:TW + TS].rearrange("p (s t) -> p s t", t=ST), ALU.mult,
                       [d_b, d_xr[t]], [d_xr[t]])
            steps = list(part2b_prev[0][1]) if part2b_prev[0] is not None else []

            def step():
                if steps:
                    steps.pop(0)()
            conv_taps(G, 3, 9, l, c, 0, vc, d_vc)
            for t in range(3):
                bank, d_b, o, n = mmw(it(4), t)
                tt(R[:, 8 + c, o:o + n], bank[:, 0:n], vc[:, o:o + n], ALU.mult, [d_b, d_vc[t]], [d_R[8 + c][t]])
            if g == 0:
                cp(chhalo[:, c, :], xr_sb[:, 3 + TPG - 2:3 + TPG], [d_xr[2]], [d_chhalo])
            else:
                cp(st_out[:, c, 4:6], xr_sb[:, 3 + TPG - 2:3 + TPG], [d_xr[2]], [d_stout])
                cp(st_out[:, c, 70:102].rearrange("p (s t) -> p s t", t=2), xs3[:, :, 5:7], [d_xr[2]], [d_stout])
            while steps:
                step()
            part2b_prev[0] = None

            for t, (o, n) in enumerate(tiles):
                br, d_br = next_bank()
                S.op("pe", lambda e, br=br, n=n, o=o, c=c: e.matmul(out=br[:, 0:n], lhsT=gw[:, c * 128:(c + 1) * 128],
                                                                     rhs=xcb[:, o:o + n], start=True, stop=True),
                     reads=[d_gw, d_xcb[t]], writes=[d_br])
                bi, d_bi = next_bank()
                S.op("pe", lambda e, bi=bi, n=n, o=o, c=c: e.matmul(out=bi[:, 0:n], lhsT=gw[:, 1024 + c * 128:1024 + (c + 1) * 128],
                                                                     rhs=xcb[:, o:o + n], start=True, stop=True),
                     reads=[d_gw, d_xcb[t]], writes=[d_bi])
                act(bA[:, o:o + n], br[:, 0:n], AF.Tanh, [d_br, d_der], [d_bA[t]], bias=DER(0, l, c), scale=0.5)
                act(bC[:, o:o + n], bi[:, 0:n], AF.Tanh, [d_bi, d_der], [d_bC[t]], bias=DER(1, l, c), scale=0.5)
            stt(bC[:, 0:NG], bC[:, 0:NG], 1.0, xc[:, 0:NG], ALU.add, ALU.mult, ALLC + list(d_xc), ALLC)

            def part2a_tail(c=c):
                act(bB[:, 0:NG], bA[:, 0:NG], AF.Tanh, ALLA + [d_der], ALLB, bias=DER(3, l, c), scale=DER(3, l, c))
                act(bA[:, 0:NG], bA[:, 0:NG], AF.Exp, ALLA + ALLB + [d_der], ALLA, bias=DER(2, l, c), scale=DER(2, l, c))

            def part2b_act(c=c, gD=gD, ALLG=ALLG):
                act(bB[:, 0:NG], bB[:, 0:NG], AF.Sqrt, ALLB, ALLB, scale=0.25)

            def s_w(c=c, t=None):
                if t is None:
                    stt(bB[:, 0:NG], bA[:, 0:NG], 1.0, bB[:, 0:NG], ALU.add, ALU.mult, ALLA + ALLB, ALLB)
                else:
                    o, n = tiles[t]
                    stt(bB[:, o:o + n], bA[:, o:o + n], 1.0, bB[:, o:o + n], ALU.add, ALU.mult, [d_bA[t], d_bB[t]], [d_bB[t]])

            def s_uu(c=c, t=None):
                if t is None:
                    stt(bC[:, 0:NG], bB[:, 0:NG], 0.5e-6, bC[:, 0:NG], ALU.max, ALU.mult, ALLB + ALLC, ALLC)
                else:
                    o, n = tiles[t]
                    stt(bC[:, o:o + n], bB[:, o:o + n], 0.5e-6, bC[:, o:o + n], ALU.max, ALU.mult, [d_bB[t], d_bC[t]], [d_bC[t]])
                if samp_g and (t is None or t == 2):
                    a3 = bA[:, TPG:TPG + TS].rearrange("p (s t) -> p s t", t=ST)
                    u3 = bC[:, TPG:TPG + TS].rearrange("p (s t) -> p s t", t=ST)
                    tt(tmp16[:], a3[:, :, 0], st_in[:, c, 0:NS], ALU.mult, [d_bA[2], d_stin], [d_tmp16], small=True)
                    tt(u3[:, :, 0], u3[:, :, 0], tmp16[:], ALU.add, [d_tmp16, d_bC[2]], [d_bC[2]], small=True)
                    S.op("dve", lambda e, a3=a3: e.memset(a3[:, :, 0], 0.0), reads=[d_tmp16], writes=[d_bA[2]], small=True)

            def s_scan(t, c=c):
                o, n = tiles[t]
                samp = samp_g and t == 2
                npz = TW if samp else n
                if t == 0:
                    init = 0.0 if g == 0 else hcar[:, c:c + 1]
                    rdi = [] if g == 0 else [d_hcar]
                else:
                    init = bB[:, o - 1:o]
                    rdi = [d_bB[t - 1]]
                S.op("dve", lambda e, o=o, npz=npz, init=init: e.tensor_tensor_scan(
                    out=bB[:, o:o + npz], data0=bA[:, o:o + npz], data1=bC[:, o:o + npz], initial=init,
                    op0=ALU.mult, op1=ALU.add), reads=[d_bA[t], d_bC[t], d_bB[t]] + rdi, writes=[d_bB[t]])
                if samp:
                    h3 = bB[:, TPG:TPG + TS].rearrange("p (s t) -> p s t", t=ST)
                    S.op("dve", lambda e: e.tensor_tensor_scan(
                        out=bB[:, TPG:TPG + TS], data0=bA[:, TPG:TPG + TS], data1=bC[:, TPG:TPG + TS], initial=0.0,
                        op0=ALU.mult, op1=ALU.add), reads=[d_bA[t], d_bC[t], d_bB[t]], writes=[d_bB[t]], small=True)
                    cp(st_out[:, c, 0:1], bB[:, TPG - 1:TPG], [d_bB[t]], [d_stout])
                    cp(st_out[:, c, 6:22], h3[:, :, ST - 1], [d_bB[t]], [d_stout])
                if g == 0 and t == 2:
                    cp(hcar[:, c:c + 1], bB[:, TPG - 1:TPG], [d_bB[t]], [d_hcar])

            def s_ya(c=c, gD=gD, ALLG=ALLG, d_gD=d_gD, t=None):
                if t is None:
                    tt(R[:, c, 0:NG], bB[:, 0:NG], gD[:, 0:NG], ALU.mult, ALLB + ALLG, list(d_R[c]))
                else:
                    o, n = tiles[t]
                    tt(R[:, c, o:o + n], bB[:, o:o + n], gD[:, o:o + n], ALU.mult, [d_bB[t], d_gD[t]], [d_R[c][t]])

            if c < NCH - 1:
                part2b_dve = [s_w, s_uu, lambda: s_scan(0), lambda: s_scan(1), lambda: s_scan(2), s_ya]
            else:
                part2b_dve = []
                for tt_ in range(3):
                    part2b_dve += [lambda tt_=tt_: s_w(t=tt_), lambda tt_=tt_: s_uu(t=tt_),
                                   lambda tt_=tt_: s_scan(tt_), lambda tt_=tt_: s_ya(t=tt_)]

            part2b_prev[0] = (part2b_act, part2b_dve, part2a_tail)
        return part2b_prev[0]

    def merge(l, g, tail):
        G = GROUPS[g]
        tiles = G["tiles"]
        for c in range(NCH):
            wb, d_wb = next_batch()
            prefetch()
            it = lambda i, wb=wb: wb[:, i * 1024:(i + 1) * 1024]
            sA, dA, sC, dC = (xc, d_xc, vc, d_vc) if c == 0 else (bA, d_bA, bC, d_bC)
            for gi_item, dsig, sigbuf in ((0, dA, sA), (1, dC, sC)):
                if c == 0 and gi_item == 1 and tail is not None:
                    pass
                for t, (o, n) in enumerate(tiles):
                    bank, d_b = next_bank()
                    mm_group(bank, d_b, n, [(it(gi_item)[:, k * 128:(k + 1) * 128], u_sb[:, k, o:o + n]) for k in range(NCH)],
                             [d_wb] + [d_u[k][t] for k in range(NCH)])
                    if c == 0:
                        S.op("dve", lambda e, sigbuf=sigbuf, bank=bank, o=o, n=n: e.tensor_copy(out=sigbuf[:, o:o + n], in_=bank[:, 0:n]),
                             reads=[d_b], writes=[dsig[t]])
                        act(sigbuf[:, o:o + n], sigbuf[:, o:o + n], AF.Sigmoid, [dsig[t]], [dsig[t]])
                    else:
                        act(sigbuf[:, o:o + n], bank[:, 0:n], AF.Sigmoid, [d_b], [dsig[t]])
            if c == 0:
                for st_ in tail[1]:
                    st_()
            for t, (o, n) in enumerate(tiles):
                bank, d_b = next_bank()
                mm_group(bank, d_b, n, [(it(2)[:, k * 128:(k + 1) * 128], R[:, 8 + k, o:o + n]) for k in range(NCH)],
                         [d_wb] + [d_R[8 + k][t] for k in range(NCH)])
                tt(bD[:, o:o + n], bank[:, 0:n], sC[:, o:o + n], ALU.mult, [d_b, dC[t]], [d_bD[t]])
            for t, (o, n) in enumerate(tiles):
                bank, d_b = next_bank()
                mm_group(bank, d_b, n, [(it(3)[:, k * 128:(k + 1) * 128], R[:, k, o:o + n]) for k in range(NCH)],
                         [d_wb] + [d_R[k][t] for k in range(NCH)])
                tt(bB[:, o:o + n], bank[:, 0:n], sA[:, o:o + n], ALU.mult, [d_b, dA[t]], [d_bB[t]])
            for t, (o, n) in enumerate(tiles):
                tt(R[:, 16 + c, o:o + n], bB[:, o:o + n], bD[:, o:o + n], ALU.add, [d_bB[t], d_bD[t]], [d_R[16 + c][t]])

    def wout_norm2(l, g):
        G = GROUPS[g]
        tiles = G["tiles"]
        nb = [hold_bank() for _ in range(3)]
        act(dtmp[4][:], dtmp[2][:], AF.Ln, [d_dtmp[2]], [d_dtmp[4]], small=True)
        for bi in range(2):
            wb, d_wb = next_batch()
            prefetch()
            for j in range(4):
                oc = bi * 4 + j
                item = wb[:, j * 1024:(j + 1) * 1024]
                for t, (o, n) in enumerate(tiles):
                    gt = g * 3 + t
                    go = G["off"] + o
                    bank, d_b = next_bank()
                    mm_group(bank, d_b, n, [(item[:, k * 128:(k + 1) * 128], R[:, 16 + k, o:o + n]) for k in range(NCH)],
                             [d_wb] + [d_R[16 + k][t] for k in range(NCH)])
                    tt(x_sb[:, oc, go:go + n], x_sb[:, oc, go:go + n], bank[:, 0:n], ALU.add, [d_b, d_x[oc][gt]], [d_x[oc][gt]])
                if oc > 0:
                    for t in range(3):
                        norm_sq_chunk(g, t, oc - 1, nb[t][0], nb[t][1])
        for t in range(3):
            norm_sq_chunk(g, t, NCH - 1, nb[t][0], nb[t][1])
        for t in range(3):
            norm_finish(l, g, t, 12, nb[t][0], nb[t][1])
            release_bank(nb[t])

    def ffn(l, g, nxt):
        G = GROUPS[g]
        tiles = G["tiles"]
        sbufs = [(bB, d_bB), (bC, d_bC), (bD, d_bD)]
        for bi in range(11):
            wb, d_wb = next_batch()
            prefetch()
            if bi == 0:
                for t, (o, n) in enumerate(tiles):
                    for jj in range(2):
                        gate = wb[:, (2 * jj) * 1024:(2 * jj + 1) * 1024]
                        sbuf_, dsb = sbufs[jj % 3]
                        bank, d_b = next_bank()
                        mm_group(bank, d_b, n, [(gate[:, k * 128:(k + 1) * 128], u_sb[:, k, o:o + n]) for k in range(NCH)],
                                 [d_wb] + [d_u[k][t] for k in range(NCH)])
                        act(sbuf_[:, o:o + n], bank[:, 0:n], AF.Silu, [d_b], [dsb[t]])
                    for jj in range(2):
                        up = wb[:, (2 * jj + 1) * 1024:(2 * jj + 2) * 1024]
                        sbuf_, dsb = sbufs[jj % 3]
                        bank, d_b = next_bank()
                        mm_group(bank, d_b, n, [(up[:, k * 128:(k + 1) * 128], u_sb[:, k, o:o + n]) for k in range(NCH)],
                                 [d_wb] + [d_u[k][t] for k in range(NCH)])
                        tt(R[:, jj, o:o + n], bank[:, 0:n], sbuf_[:, o:o + n], ALU.mult, [d_b, dsb[t]], [d_R[jj][t]])
                continue
            for jj in range(2):
                j = bi * 2 + jj
                gate = wb[:, (2 * jj) * 1024:(2 * jj + 1) * 1024]
                up = wb[:, (2 * jj + 1) * 1024:(2 * jj + 2) * 1024]
                sbuf_, dsb = sbufs[j % 3]
                for t, (o, n) in enumerate(tiles):
                    bank, d_b = next_bank()
                    mm_group(bank, d_b, n, [(gate[:, k * 128:(k + 1) * 128], u_sb[:, k, o:o + n]) for k in range(NCH)],
                             [d_wb] + [d_u[k][t] for k in range(NCH)])
                    act(sbuf_[:, o:o + n], bank[:, 0:n], AF.Silu, [d_b], [dsb[t]])
                for t, (o, n) in enumerate(tiles):
                    bank, d_b = next_bank()
                    mm_group(bank, d_b, n, [(up[:, k * 128:(k + 1) * 128], u_sb[:, k, o:o + n]) for k in range(NCH)],
                             [d_wb] + [d_u[k][t] for k in range(NCH)])
                    tt(R[:, j, o:o + n], bank[:, 0:n], sbuf_[:, o:o + n], ALU.mult, [d_b, dsb[t]], [d_R[j][t]])
        hoist = {1: 0, 3: 1, 5: 2}
        for oc in range(NCH):
            wb, d_wb = next_batch()
            prefetch()
            for t, (o, n) in enumerate(tiles):
                gt = g * 3 + t
                go = G["off"] + o
                bank, d_b = next_bank()
                mm_group(bank, d_b, n, [(wb[:, k * 128:(k + 1) * 128], R[:, k, o:o + n]) for k in range(NFF)],
                         [d_wb] + [d_R[k][t] for k in range(NFF)])
                tt(x_sb[:, oc, go:go + n], x_sb[:, oc, go:go + n], bank[:, 0:n], ALU.add, [d_b, d_x[oc][gt]], [d_x[oc][gt]])
            if nxt is not None and oc in hoist:
                norm_tile(nxt[0], nxt[1], hoist[oc], 0)

    seq = [(l, g) for l in range(depth) for g in range(2)]
    for t in range(3):
        norm_tile(0, 0, t, 0)
    for i, (l, g) in enumerate(seq):
        if g == 0:
            S.dma("pool", lambda e, l=l: e.dma_start(out=gw[:], in_=ws_d[l, :, 0:W_GATES], max_dma_last_dim=4096), "gwl",
                  writes=[d_gw])
            S.dma("sp", lambda e, l=l: e.dma_start(out=st_in[:].rearrange("p c n -> p (c n)"), in_=stin_d[l]), "ldst",
                  writes=[d_stin])
        tail = mixer(l, g)
        tail[2]()
        tail[0]()
        merge(l, g, tail)
        if g == 1:
            S.dma("sp", lambda e, l=l: e.dma_start(out=sto_d[l], in_=st_out[:].rearrange("p c n -> p (c n)")), "st",
                  reads=[d_stout])
        wout_norm2(l, g)
        ffn(l, g, seq[i + 1] if i + 1 < len(seq) else None)

    for g in range(2):
        for t in range(3):
            norm_tile(0, g, t, 0, final=True)
    for c0 in range(0, NCH, 2):
        S.dma("sp", lambda e, c0=c0: e.dma_start(out=y_d[:, c0:c0 + 2, :], in_=x_sb[:, c0:c0 + 2, :]), "st",
              reads=[d_x[c][t] for c in (c0, c0 + 1) for t in range(6)])

    sem_keys = list(Sched.ENGS) + sorted(S.dma_cnt.keys())
    sems = {k: es.enter_context(nc.semaphore("s_" + k)) for k in sem_keys}

    def emit(name, e):
        for waits, fn, key, inc in S.streams[name]:
            for k, v in waits:
                e.wait_ge(sems[k], v)
            ins = fn(e)
            ins.then_inc(sems[key], inc)
        if name == "sp":
            for k, v in S.dma_cnt.items():
                e.wait_ge(sems[k], v)

    with nc.Block() as block:
        @block.tensor
        def _(e):
            emit("pe", e)

        @block.scalar
        def _(e):
            emit("act", e)

        @block.vector
        def _(e):
            emit("dve", e)

        @block.gpsimd
        def _(e):
            emit("pool", e)

        @block.sync
        def _(e):
            emit("sp", e)
    es.close()
    return nc


def _fm(a):
    T = a.shape[0]
    return np.ascontiguousarray(a.reshape(T, NCH, 128).transpose(2, 1, 0))


def _pack_weights(inp):
    ws = np.zeros((DEPTH, 128, WS_LAYER), np.float32)
    for l in range(DEPTH):
        off = 0
        gwl = np.zeros((128, 2, 8, 128), np.float32)
        for gi, name in enumerate(("gate_a_w", "gate_x_w")):
            w = inp[name][l]
            w2 = w.reshape(8, 2, 64, 64)
            gwl[0:64, gi, :, 0:64] = w2[:, 0].transpose(1, 0, 2)
            gwl[64:128, gi, :, 64:128] = w2[:, 1].transpose(1, 0, 2)
        ws[l, :, off:off + W_GATES] = gwl.reshape(128, W_GATES); off += W_GATES

        def item(W, col0):
            K = W.shape[0]
            blk = W[:, col0:col0 + 128].reshape(K // 128, 128, 128)
            return blk.transpose(1, 0, 2).reshape(128, K)

        w_in = inp["w_in"][l]
        for c in range(8):
            for s in ("xr", "gr", "cc", "hc", "bc"):
                ws[l, :, off:off + 1024] = item(w_in, OFF[s] + c * 128); off += 1024
        wa, wbm = inp["w_branch_a"][l], inp["w_branch_b"][l]
        for c in range(8):
            ws[l, :, off:off + 1024] = item(w_in, OFF["ga"] + c * 128); off += 1024
            ws[l, :, off:off + 1024] = item(w_in, OFF["gb"] + c * 128); off += 1024
            ws[l, :, off:off + 1024] = item(wbm, c * 128); off += 1024
            ws[l, :, off:off + 1024] = item(wa, c * 128); off += 1024
        wo = inp["w_out"][l]
        for c in range(8):
            ws[l, :, off:off + 1024] = item(wo, c * 128); off += 1024
        wg, wu = inp["w_ff_gate"][l], inp["w_ff_up"][l]
        for j in range(NFF):
            ws[l, :, off:off + 1024] = item(wg, j * 128); off += 1024
            ws[l, :, off:off + 1024] = item(wu, j * 128); off += 1024
        wd = inp["w_ff_down"][l]
        for c in range(8):
            ws[l, :, off:off + W_DN] = item(wd, c * 128); off += W_DN
        assert off == WS_LAYER
    return ws


def _pack_vecs(inp):
    v = np.zeros((128, NVEC * DEPTH * 8 + 8), np.float32)
    rows = []
    for l in range(DEPTH):
        rows.append([inp["norm1_g"][l]] + [inp["rnn_conv_w"][l, k] for k in range(4)] + [inp["rnn_conv_b"][l],
                    inp["gate_a_b"][l], inp["gate_x_b"][l], inp["lru_lambda"][l]] +
                    [inp["sc_conv_w"][l, k] for k in range(3)] + [inp["norm2_g"][l]])
    for i in range(NVEC):
        for l in range(DEPTH):
            o = (i * DEPTH + l) * 8
            v[:, o:o + 8] = np.asarray(rows[l][i]).reshape(8, 128).T
    v[:, NVEC * DEPTH * 8:] = np.asarray(inp["final_norm_g"]).reshape(8, 128).T
    return v


def kernel(**inputs):
    inp = {k: np.asarray(v) for k, v in inputs.items()}
    nc = build_program(DEPTH)
    ws = _pack_weights(inp)
    vecs = _pack_vecs(inp)
    meta = _fm(inp["meta_tokens"].astype(np.float32))
    in_maps = []
    for i in range(NCORES):
        xp = _fm(inp["x_prompt"][i])
        xs = _fm(inp["x_sample"][i * NS:(i + 1) * NS].reshape(TS, D))
        stin = np.zeros((DEPTH, 128, NCH, NSTI), np.float32)
        h0 = inp["state_rnn_h"][:, i * NS:(i + 1) * NS]
        rc = inp["state_rnn_conv"][:, i * NS:(i + 1) * NS]
        sc = inp["state_sc_conv"][:, i * NS:(i + 1) * NS]
        stin[:, :, :, 0:16] = h0.reshape(DEPTH, NS, NCH, 128).transpose(0, 3, 2, 1)
        stin[:, :, :, 16:64] = rc.reshape(DEPTH, NS, 3, NCH, 128).transpose(0, 4, 3, 1, 2).reshape(DEPTH, 128, NCH, 48)
        stin[:, :, :, 64:96] = sc.reshape(DEPTH, NS, 2, NCH, 128).transpose(0, 4, 3, 1, 2).reshape(DEPTH, 128, NCH, 32)
        in_maps.append(dict(xp=xp, meta=meta, xs=xs, stin=np.ascontiguousarray(stin.reshape(DEPTH, 128, NCH * NSTI)),
                            vecs=vecs, ws=ws))
    res = run_bass_kernel_spmd(nc, in_maps, core_ids=list(range(NCORES)))
    y_prompt = np.zeros((NCORES, SEQ, D), np.float32)
    y_sample = np.zeros((NCORES * NS, ST, D), np.float32)
    rnn_h_p = np.zeros((DEPTH, NCORES, D), np.float32)
    rnn_c_p = np.zeros((DEPTH, NCORES, 3, D), np.float32)
    sc_c_p = np.zeros((DEPTH, NCORES, 2, D), np.float32)
    rnn_h_s = np.zeros((DEPTH, NCORES * NS, D), np.float32)
    rnn_c_s = np.zeros((DEPTH, NCORES * NS, 3, D), np.float32)
    sc_c_s = np.zeros((DEPTH, NCORES * NS, 2, D), np.float32)
    for i in range(NCORES):
        r = res.results[i]
        y = np.asarray(r["y"]).reshape(128, NCH, TTOT)
        yt = y.transpose(2, 1, 0).reshape(TTOT, D)
        y_prompt[i] = yt[NMETA:TP]
        y_sample[i * NS:(i + 1) * NS] = yt[TP:].reshape(NS, ST, D)
        so = np.asarray(r["sto"]).reshape(DEPTH, 128, NCH, NSTO)
        so = so.transpose(0, 3, 2, 1).reshape(DEPTH, NSTO, D)
        rnn_h_p[:, i] = so[:, 0]
        rnn_c_p[:, i] = so[:, 1:4]
        sc_c_p[:, i] = so[:, 4:6]
        rnn_h_s[:, i * NS:(i + 1) * NS] = so[:, 6:22]
        rnn_c_s[:, i * NS:(i + 1) * NS] = so[:, 22:70].reshape(DEPTH, NS, 3, D)
        sc_c_s[:, i * NS:(i + 1) * NS] = so[:, 70:102].reshape(DEPTH, NS, 2, D)
    return (y_prompt, y_sample, rnn_h_p, rnn_c_p, sc_c_p, rnn_h_s, rnn_c_s, sc_c_s)
```

```python
import numpy as np
from contextlib import ExitStack
import concourse.bass as bass
import concourse.mybir as mybir
from concourse.bass_utils import run_bass_kernel_spmd

F32 = mybir.dt.float32
BF16 = mybir.dt.bfloat16
AF = mybir.ActivationFunctionType
ALU = mybir.AluOpType

D = 1024
NCH = 8
DFF = 2816
NFF = 22
DEPTH = 4
NMETA = 16
SEQ = 2048
TP = NMETA + SEQ
NS = 16
ST = 4
TS = NS * ST
TTOT = TP + TS
NCORES = 8
EPS = 1e-6
OFF = dict(xr=0, gr=1024, bc=2048, cc=3072, hc=4096, ga=5120, gb=6144)
TPG = TP // 2
TW = 344
GROUPS = [
    dict(off=0, n=TPG, tiles=[(0, TW), (TW, TW), (2 * TW, TW)], samp=False),
    dict(off=TPG, n=TPG + TS, tiles=[(0, TW), (TW, TW), (2 * TW, TW + TS)], samp=True),
]
GW = TPG + TS
NVEC = 13
NSTI = 96
NSTO = 102
W_GATES = 2 * 8 * 128
W_MIX = 5 * 1024
W_MRG = 4 * 1024
W_OUT = 4 * 1024
W_FFN = 4 * 1024
W_DN = NFF * 128
WS_LAYER = W_GATES + 8 * W_MIX + 8 * W_MRG + 2 * W_OUT + 11 * W_FFN + 8 * W_DN
WBUF = 5120
XRW = 1160
SB0 = 1040
NSQ = 3

SAME_SYNC_ALL = True


class Dep:
    __slots__ = ("w", "r")

    def __init__(self):
        self.w = None
        self.r = {}


class Sched:
    ENGS = ("pe", "act", "dve", "pool", "sp")

    def __init__(self):
        self.streams = {e: [] for e in self.ENGS}
        self.tick = {e: 0 for e in self.ENGS}
        self.known = {e: {} for e in self.ENGS}
        self.dma_cnt = {}

    def _waits(self, eng, reads, writes, small):
        waits = {}

        def need(k, v):
            if k == eng and (eng == "pe" or not (small or SAME_SYNC_ALL)):
                return
            if self.known[eng].get(k, 0) >= v:
                return
            if waits.get(k, 0) < v:
                waits[k] = v

        for t in reads:
            if t.w is not None:
                need(*t.w)
        for t in writes:
            if t.w is not None:
                need(*t.w)
            for k, v in t.r.items():
                need(k, v)
        for k, v in waits.items():
            self.known[eng][k] = v
        return list(waits.items())

    def _mark(self, tok, reads, writes):
        k, v = tok
        for t in reads:
            t.r[k] = v
        for t in writes:
            t.w = tok
            t.r = {}

    def op(self, eng, fn, reads=(), writes=(), small=False):
        waits = self._waits(eng, reads, writes, small)
        self.tick[eng] += 1
        tok = (eng, self.tick[eng])
        self.streams[eng].append((waits, fn, eng, 1))
        self._mark(tok, reads, writes)

    def dma(self, eng, fn, semkey, reads=(), writes=()):
        waits = self._waits(eng, reads, writes, False)
        self.dma_cnt[semkey] = self.dma_cnt.get(semkey, 0) + 16
        tok = (semkey, self.dma_cnt[semkey])
        self.streams[eng].append((waits, fn, semkey, 16))
        self._mark(tok, reads, writes)


def build_program(depth=DEPTH):
    nc = bass.Bass("TRN2", target_bir_lowering=False)
    xp_d = nc.dram_tensor("xp", [128, NCH, SEQ], F32, kind="ExternalInput").ap()
    meta_d = nc.dram_tensor("meta", [128, NCH, NMETA], F32, kind="ExternalInput").ap()
    xs_d = nc.dram_tensor("xs", [128, NCH, TS], F32, kind="ExternalInput").ap()
    stin_d = nc.dram_tensor("stin", [DEPTH, 128, NCH * NSTI], F32, kind="ExternalInput").ap()
    vecs_d = nc.dram_tensor("vecs", [128, NVEC * DEPTH * 8 + 8], F32, kind="ExternalInput").ap()
    ws_d = nc.dram_tensor("ws", [DEPTH, 128, WS_LAYER], F32, kind="ExternalInput").ap()
    y_d = nc.dram_tensor("y", [128, NCH, TTOT], F32, kind="ExternalOutput").ap()
    sto_d = nc.dram_tensor("sto", [DEPTH, 128, NCH * NSTO], F32, kind="ExternalOutput").ap()

    S = Sched()
    es = ExitStack()

    def sb(name, shape, dt):
        return es.enter_context(nc.sbuf_tensor(name, shape, dt))

    x_sb = sb("x_sb", [128, NCH, TTOT], F32)
    u_sb = sb("u_sb", [128, NCH, GW], BF16)
    R = sb("R", [128, 24, GW], BF16)
    wbuf = [sb("wbuf0", [128, WBUF], BF16), sb("wbuf1", [128, WBUF], BF16)]
    gw = sb("gw", [128, W_GATES], BF16)
    vec_sb = sb("vec_sb", [128, NVEC * DEPTH * 8 + 8], F32)
    der_sb = sb("der_sb", [128, 4 * DEPTH * 8], F32)
    dtmp = [sb("dtmp%d" % i, [128, DEPTH * 8], F32) for i in range(6)]
    ones_bf = sb("ones_bf", [128, 128], BF16)
    xr_sb = sb("xr_sb", [128, XRW], F32)
    xc = sb("xc", [128, GW], F32)
    xcb = sb("xcb", [128, GW], BF16)
    vc = sb("vc", [128, GW], F32)
    bA = sb("bA", [128, GW], F32)
    bB = sb("bB", [128, GW], F32)
    bC = sb("bC", [128, GW], F32)
    bD = sb("bD", [128, GW], F32)
    bD1 = sb("bD1", [128, GW], F32)
    sqb = [sb("sqb%d" % i, [128, 416], BF16) for i in range(NSQ)]
    st_in = sb("st_in", [128, NCH, NSTI], F32)
    st_out = sb("st_out", [128, NCH, NSTO], F32)
    hcar = sb("hcar", [128, NCH], F32)
    xhalo = sb("xhalo", [128, NCH, 3], F32)
    chhalo = sb("chhalo", [128, NCH, 2], F32)
    tmp16 = sb("tmp16", [128, NS], F32)
    banks = [es.enter_context(nc.psum_tensor("ps%d" % i, [128, 512], F32)) for i in range(8)]

    d_x = [[Dep() for _ in range(6)] for _ in range(NCH)]
    d_u = [[Dep() for _ in range(3)] for _ in range(NCH)]
    d_R = [[Dep() for _ in range(3)] for _ in range(24)]
    d_wbuf = [Dep(), Dep()]
    d_gw = Dep()
    d_vec = Dep()
    d_der = Dep()
    d_dtmp = [Dep() for _ in range(6)]
    d_ones = Dep()
    d_xr = [Dep() for _ in range(3)]
    d_xrh = Dep()
    d_xc = [Dep() for _ in range(3)]
    d_xcb = [Dep() for _ in range(3)]
    d_vc = [Dep() for _ in range(3)]
    d_bA = [Dep() for _ in range(3)]
    d_bB = [Dep() for _ in range(3)]
    d_bC = [Dep() for _ in range(3)]
    d_bD = [Dep() for _ in range(3)]
    d_bD1 = [Dep() for _ in range(3)]
    d_sqb = [Dep() for _ in range(NSQ)]
    d_stin = Dep()
    d_stout = Dep()
    d_hcar = Dep()
    d_xhalo = Dep()
    d_chhalo = Dep()
    d_tmp16 = Dep()
    d_bank = [Dep() for _ in range(8)]
    d_y = Dep()

    bank_ctr = [0]

    held = set()

    def next_bank():
        while True:
            b = bank_ctr[0] % 8
            bank_ctr[0] += 1
            if b not in held:
                return banks[b], d_bank[b]

    def hold_bank():
        bk = next_bank()
        held.add(banks.index(bk[0]))
        return bk

    def release_bank(bk):
        held.discard(banks.index(bk[0]))

    def V(l, i, c):
        o = (i * DEPTH + l) * 8 + c
        return vec_sb[:, o:o + 1]

    def DER(kind, l, c):
        o = kind * DEPTH * 8 + l * 8 + c
        return der_sb[:, o:o + 1]

    batches = []
    for l in range(depth):
        for g in range(2):
            off = W_GATES
            for _ in range(8):
                batches.append((l, off, W_MIX)); off += W_MIX
            for _ in range(8):
                batches.append((l, off, W_MRG)); off += W_MRG
            for _ in range(2):
                batches.append((l, off, W_OUT)); off += W_OUT
            for _ in range(11):
                batches.append((l, off, W_FFN)); off += W_FFN
            for _ in range(8):
                batches.append((l, off, W_DN)); off += W_DN
            assert off == WS_LAYER
    bstate = dict(issued=0, used=0)

    def issue_batch():
        i = bstate["issued"]
        if i >= len(batches):
            return
        l, off, n = batches[i]
        b = i % 2
        src = ws_d[l, :, off:off + n]
        dst = wbuf[b][:, 0:n]
        S.dma("pool", lambda e, src=src, dst=dst: e.dma_start(out=dst, in_=src, max_dma_last_dim=4096),
              "w%d" % b, reads=(), writes=(d_wbuf[b],))
        bstate["issued"] += 1

    def next_batch():
        i = bstate["used"]
        while bstate["issued"] <= min(i, len(batches) - 1):
            issue_batch()
        b = i % 2
        bstate["used"] += 1
        return wbuf[b], d_wbuf[b]

    def prefetch():
        if bstate["issued"] < bstate["used"] + 1:
            issue_batch()

    def mm_group(bank, d_b, n, pairs, reads):
        def fn(e, bank=bank, n=n, pairs=pairs):
            last = None
            for i, (lt, rh) in enumerate(pairs):
                last = e.matmul(out=bank[:, 0:n], lhsT=lt, rhs=rh, start=(i == 0), stop=(i == len(pairs) - 1))
            return last
        S.op("pe", fn, reads=reads, writes=(d_b,))

    def act(out, in_, func, reads, writes, bias=None, scale=None, small=False):
        kw = {}
        if bias is not None:
            kw["bias"] = bias
        if scale is not None:
            kw["scale"] = scale
        S.op("act", lambda e: e.activation(out=out, in_=in_, func=func, **kw), reads=reads, writes=writes, small=small)

    def tt(out, in0, in1, op, reads, writes, small=False):
        S.op("dve", lambda e: e.tensor_tensor(out=out, in0=in0, in1=in1, op=op), reads=reads, writes=writes, small=small)

    def ts(out, in0, s1, op0, reads, writes, s2=None, op1=None, small=False):
        if op1 is None:
            S.op("dve", lambda e: e.tensor_scalar(out=out, in0=in0, scalar1=s1, scalar2=None, op0=op0),
                 reads=reads, writes=writes, small=small)
        else:
            S.op("dve", lambda e: e.tensor_scalar(out=out, in0=in0, scalar1=s1, scalar2=s2, op0=op0, op1=op1),
                 reads=reads, writes=writes, small=small)

    def stt(out, in0, scalar, in1, op0, op1, reads, writes, small=False):
        S.op("dve", lambda e: e.scalar_tensor_tensor(out=out, in0=in0, scalar=scalar, in1=in1, op0=op0, op1=op1),
             reads=reads, writes=writes, small=small)

    def cp(out, in_, reads, writes, small=True):
        S.op("dve", lambda e: e.tensor_copy(out=out, in_=in_), reads=reads, writes=writes, small=small)

    S.dma("sp", lambda e: e.dma_start(out=vec_sb[:], in_=vecs_d), "ldv", writes=(d_vec,))
    NP0 = TPG - NMETA
    for c0 in range(0, NCH, 4):
        S.dma("sp", lambda e, c0=c0: e.dma_start(out=x_sb[:, c0:c0 + 4, NMETA:TPG], in_=xp_d[:, c0:c0 + 4, 0:NP0]), "ldx%d" % c0,
              writes=[d_x[c][t] for c in range(c0, c0 + 4) for t in range(3)])
    S.dma("sp", lambda e: e.dma_start(out=x_sb[:, :, 0:NMETA], in_=meta_d), "ldm",
          writes=[d_x[c][0] for c in range(NCH)])
    S.dma("sp", lambda e: e.dma_start(out=x_sb[:, :, TP:TTOT], in_=xs_d), "lds",
          writes=[d_x[c][5] for c in range(NCH)])
    for c0 in range(0, NCH, 4):
        S.dma("sp", lambda e, c0=c0: e.dma_start(out=x_sb[:, c0:c0 + 4, TPG:TP], in_=xp_d[:, c0:c0 + 4, NP0:SEQ]), "ldy%d" % c0,
              writes=[d_x[c][t] for c in range(c0, c0 + 4) for t in range(3, 6)])
    S.op("dve", lambda e: e.memset(ones_bf[:], 1.0), writes=(d_ones,))

    NL = DEPTH * 8
    lam = vec_sb[:, 8 * NL:9 * NL]
    t0, t1, t2, t3, t4, t5 = [t[:] for t in dtmp]
    dd = d_dtmp
    ts(t0, lam, -1.0, ALU.mult, [d_vec], [dd[0]], small=True)
    tt(t0, t0, lam, ALU.min, [d_vec, dd[0]], [dd[0]], small=True)
    act(t1, t0, AF.Exp, [dd[0]], [dd[1]], small=True)
    ts(t2, t1, 2.0, ALU.add, [dd[1]], [dd[2]], small=True)
    S.op("dve", lambda e: e.reciprocal(out=t2, in_=t2), reads=[dd[2]], writes=[dd[2]], small=True)
    tt(t3, t1, t2, ALU.mult, [dd[1], dd[2]], [dd[3]], small=True)
    tt(t4, t3, t3, ALU.mult, [dd[3]], [dd[4]], small=True)
    S.op("dve", lambda e: e.memset(t5, 0.0), writes=[dd[5]], small=True)
    for k in range(9, 0, -1):
        stt(t5, t5, 1.0 / (2 * k + 1), t4, ALU.add, ALU.mult, [dd[5], dd[4]], [dd[5]], small=True)
    stt(t5, t5, 1.0, t3, ALU.add, ALU.mult, [dd[5], dd[3]], [dd[5]], small=True)
    ts(t0, lam, -1.0, ALU.mult, [d_vec, dd[0]], [dd[0]], s2=0.0, op1=ALU.max, small=True)
    stt(t1, t5, 2.0, t0, ALU.mult, ALU.add, [dd[5], dd[0], dd[1]], [dd[1]], small=True)
    ts(der_sb[:, 0 * NL:1 * NL], vec_sb[:, 6 * NL:7 * NL], 0.5, ALU.mult, [d_vec], [d_der], small=True)
    ts(der_sb[:, 1 * NL:2 * NL], vec_sb[:, 7 * NL:8 * NL], 0.5, ALU.mult, [d_vec], [d_der], small=True)
    ts(der_sb[:, 2 * NL:3 * NL], t1, -4.0, ALU.mult, [dd[1]], [d_der], small=True)
    ts(der_sb[:, 3 * NL:4 * NL], t1, 2.0, ALU.mult, [dd[1]], [d_der], small=True)

    sq_ctr = [0]

    def norm_sq_chunk(g, t, c, bank, d_b):
        G = GROUPS[g]
        o, n = G["tiles"][t]
        gt = g * 3 + t
        go = G["off"] + o
        q = sq_ctr[0] % NSQ
        sq_ctr[0] += 1
        act(sqb[q][:, 0:n], x_sb[:, c, go:go + n], AF.Square, [d_x[c][gt]], [d_sqb[q]])
        S.op("pe", lambda e: e.matmul(out=bank[:, 0:n], lhsT=ones_bf[:], rhs=sqb[q][:, 0:n],
                                      start=(c == 0), stop=(c == NCH - 1)),
             reads=[d_sqb[q], d_ones], writes=[d_b])

    def norm_finish(l, g, t, gi, bank, d_b, final=False):
        G = GROUPS[g]
        o, n = G["tiles"][t]
        gt = g * 3 + t
        go = G["off"] + o
        act(bA[:, o:o + n], bank[:, 0:n], AF.Ln, [d_b], [d_bA[t]], bias=EPS, scale=1.0 / D)
        act(bA[:, o:o + n], bA[:, o:o + n], AF.Exp, [d_bA[t]], [d_bA[t]], scale=-0.5)
        for c in range(NCH):
            if final:
                fo = NVEC * DEPTH * 8 + c
                stt(x_sb[:, c, go:go + n], x_sb[:, c, go:go + n], vec_sb[:, fo:fo + 1], bA[:, o:o + n], ALU.mult, ALU.mult,
                    [d_x[c][gt], d_bA[t], d_vec], [d_x[c][gt]])
            else:
                stt(u_sb[:, c, o:o + n], x_sb[:, c, go:go + n], V(l, gi, c), bA[:, o:o + n], ALU.mult, ALU.mult,
                    [d_x[c][gt], d_bA[t], d_vec], [d_u[c][t]])

    def norm_tile(l, g, t, gi, final=False):
        bank, d_b = next_bank()
        for c in range(NCH):
            norm_sq_chunk(g, t, c, bank, d_b)
        norm_finish(l, g, t, gi, bank, d_b, final)

    def conv_taps(G, width, wvec, l, c, k0, outbuf, d_out, ks=None):
        xs3 = xr_sb[:, SB0:SB0 + NS * 7].rearrange("p (s t) -> p s t", t=7)
        xo = outbuf[:, TPG:TPG + TS].rearrange("p (s t) -> p s t", t=ST)
        rd = list(d_xr) + [d_xrh, d_vec]
        for k in (range(k0, width) if ks is None else ks):
            wk = V(l, wvec + (width - 1 - k), c)
            if k == 0:
                ts(outbuf[:, 0:TPG], xr_sb[:, 3:3 + TPG], wk, ALU.mult, rd, list(d_out))
            else:
                stt(outbuf[:, 0:TPG], xr_sb[:, 3 - k:3 - k + TPG], wk, outbuf[:, 0:TPG], ALU.mult, ALU.add,
                    rd + list(d_out), list(d_out))
            if G["samp"]:
                if k == 0:
                    ts(xo, xs3[:, :, 3:7], wk, ALU.mult, rd, [d_out[2]])
                else:
                    stt(xo, xs3[:, :, 3 - k:7 - k], wk, xo, ALU.mult, ALU.add, rd + [d_out[2]], [d_out[2]])

    def mixer(l, g):
        G = GROUPS[g]
        tiles = G["tiles"]
        samp_g = G["samp"]
        xs3 = xr_sb[:, SB0:SB0 + NS * 7].rearrange("p (s t) -> p s t", t=7)
        part2b_prev = [None]
        NG = G["n"]
        ALLA, ALLB, ALLC, ALLD = list(d_bA), list(d_bB), list(d_bC), list(d_bD)

        for c in range(NCH):
            wb, d_wb = next_batch()
            prefetch()
            it = lambda i, wb=wb: wb[:, i * 1024:(i + 1) * 1024]
            gD, d_gD = (bD, d_bD) if c % 2 == 0 else (bD1, d_bD1)
            ALLG = list(d_gD)

            def mmw(item, t, d_wb=d_wb):
                o, n = tiles[t]
                bank, d_b = next_bank()
                mm_group(bank, d_b, n, [(item[:, k * 128:(k + 1) * 128], u_sb[:, k, o:o + n]) for k in range(NCH)],
                         [d_wb] + [d_u[k][t] for k in range(NCH)])
                return bank, d_b, o, n

            if g == 0:
                S.op("dve", lambda e: e.memset(xr_sb[:, 0:3], 0.0), reads=[], writes=[d_xrh], small=True)
            else:
                cp(xr_sb[:, 0:3], xhalo[:, c, :], [d_xhalo], [d_xrh])
            if samp_g:
                cp(xs3[:, :, 0:3], st_in[:, c, 16:64].rearrange("p (s t) -> p s t", t=3), [d_stin], [d_xrh])
            xrb = []
            for t in range(3):
                bank, d_b, o, n = mmw(it(0), t)
                xrb.append((bank, d_b, o, n))
                samp = samp_g and t == 2
                npz = TW if samp else n
                act(xr_sb[:, 3 + o:3 + o + npz], bank[:, 0:npz], AF.Copy, [d_b], [d_xr[t]])
                if samp:
                    act(xs3[:, :, 3:7], bank[:, TW:TW + TS].rearrange("p (s t) -> p s t", t=ST), AF.Copy, [d_b], [d_xr[t]])
            for t, (bank, d_b, o, n) in enumerate(xrb):
                act(xc[:, o:o + n], bank[:, 0:n], AF.Identity, [d_b, d_vec], [d_xc[t]], bias=V(l, 5, c), scale=V(l, 4, c))
            conv_taps(G, 4, 1, l, c, 1, xc, d_xc)
            if g == 0:
                cp(xhalo[:, c, :], xr_sb[:, 3 + TPG - 3:3 + TPG], [d_xr[2]], [d_xhalo])
            else:
                cp(st_out[:, c, 1:4], xr_sb[:, 3 + TPG - 3:3 + TPG], [d_xr[2]], [d_stout])
                cp(st_out[:, c, 22:70].rearrange("p (s t) -> p s t", t=3), xs3[:, :, 4:7], [d_xr[2]], [d_stout])

            if part2b_prev[0] is not None:
                part2b_prev[0][2]()
            for t in range(3):
                bank, d_b, o, n = mmw(it(1), t)
                act(gD[:, o:o + n], bank[:, 0:n], AF.Copy, [d_b], [d_gD[t]])
            act(xcb[:, 0:NG], xc[:, 0:NG], AF.Copy, list(d_xc), list(d_xcb))

            if g == 1:
                cp(xr_sb[:, 1:3], chhalo[:, c, :], [d_chhalo], [d_xrh])
            if samp_g:
                cp(xs3[:, :, 1:3], st_in[:, c, 64:96].rearrange("p (s t) -> p s t", t=2), [d_stin], [d_xrh])
            for t in range(3):
                bank, d_b, o, n = mmw(it(2), t)
                samp = samp_g and t == 2
                npz = TW if samp else n
                act(xr_sb[:, 3 + o:3 + o + npz], bank[:, 0:npz], AF.Copy, [d_b], [d_xr[t]])
                if samp:
                    act(xs3[:, :, 3:7], bank[:, TW:TW + TS].rearrange("p (s t) -> p s t", t=ST), AF.Copy, [d_b], [d_xr[t]])
            if part2b_prev[0] is not None:
                part2b_prev[0][0]()
            act(gD[:, 0:NG], gD[:, 0:NG], AF.Gelu_apprx_tanh, ALLG, ALLG)
            for t in range(3):
                bank, d_b, o, n = mmw(it(3), t)
                samp = samp_g and t == 2
                npz = TW if samp else n
                tt(xr_sb[:, 3 + o:3 + o + npz], xr_sb[:, 3 + o:3 + o + npz], bank[:, 0:npz], ALU.mult, [d_b, d_xr[t]], [d_xr[t]])
                if samp:
                    tt(xs3[:, :, 3:7], xs3[:, :, 3:7], bank[:, TW:TW + TS].rearrange("p (s t) -> p s t", t=ST), ALU.mult,
                       [d_b, d_xr[t]], [d_xr[t]])
            steps = list(part2b_prev[0][1]) if part2b_prev[0] is not None else []

            def step():
                if steps:
                    steps.pop(0)()
            conv_taps(G, 3, 9, l, c, 0, vc, d_vc)
            for t in range(3):
                bank, d_b, o, n = mmw(it(4), t)
                tt(R[:, 8 + c, o:o + n], bank[:, 0:n], vc[:, o:o + n], ALU.mult, [d_b, d_vc[t]], [d_R[8 + c][t]])
            if g == 0:
                cp(chhalo[:, c, :], xr_sb[:, 3 + TPG - 2:3 + TPG], [d_xr[2]], [d_chhalo])
            else:
                cp(st_out[:, c, 4:6], xr_sb[:, 3 + TPG - 2:3 + TPG], [d_xr[2]], [d_stout])
                cp(st_out[:, c, 70:102].rearrange("p (s t) -> p s t", t=2), xs3[:, :, 5:7], [d_xr[2]], [d_stout])
            while steps:
                step()
            part2b_prev[0] = None

            for t, (o, n) in enumerate(tiles):
                br, d_br = next_bank()
                S.op("pe", lambda e, br=br, n=n, o=o, c=c: e.matmul(out=br[:, 0:n], lhsT=gw[:, c * 128:(c + 1) * 128],
                                                                     rhs=xcb[:, o:o + n], start=True, stop=True),
                     reads=[d_gw, d_xcb[t]], writes=[d_br])
                bi, d_bi = next_bank()
                S.op("pe", lambda e, bi=bi, n=n, o=o, c=c: e.matmul(out=bi[:, 0:n], lhsT=gw[:, 1024 + c * 128:1024 + (c + 1) * 128],
                                                                     rhs=xcb[:, o:o + n], start=True, stop=True),
                     reads=[d_gw, d_xcb[t]], writes=[d_bi])
                act(bA[:, o:o + n], br[:, 0:n], AF.Tanh, [d_br, d_der], [d_bA[t]], bias=DER(0, l, c), scale=0.5)
                act(bC[:, o:o + n], bi[:, 0:n], AF.Tanh, [d_bi, d_der], [d_bC[t]], bias=DER(1, l, c), scale=0.5)
            stt(bC[:, 0:NG], bC[:, 0:NG], 1.0, xc[:, 0:NG], ALU.add, ALU.mult, ALLC + list(d_xc), ALLC)

            def part2a_tail(c=c):
                act(bB[:, 0:NG], bA[:, 0:NG], AF.Tanh, ALLA + [d_der], ALLB, bias=DER(3, l, c), scale=DER(3, l, c))
                act(bA[:, 0:NG], bA[:, 0:NG], AF.Exp, ALLA + ALLB + [d_der], ALLA, bias=DER(2, l, c), scale=DER(2, l, c))

            def part2b_act(c=c, gD=gD, ALLG=ALLG):
                act(bB[:, 0:NG], bB[:, 0:NG], AF.Sqrt, ALLB, ALLB, scale=0.25)

            def s_w(c=c, t=None):
                if t is None:
                    stt(bB[:, 0:NG], bA[:, 0:NG], 1.0, bB[:, 0:NG], ALU.add, ALU.mult, ALLA + ALLB, ALLB)
                else:
                    o, n = tiles[t]
                    stt(bB[:, o:o + n], bA[:, o:o + n], 1.0, bB[:, o:o + n], ALU.add, ALU.mult, [d_bA[t], d_bB[t]], [d_bB[t]])

            def s_uu(c=c, t=None):
                if t is None:
                    stt(bC[:, 0:NG], bB[:, 0:NG], 0.5e-6, bC[:, 0:NG], ALU.max, ALU.mult, ALLB + ALLC, ALLC)
                else:
                    o, n = tiles[t]
                    stt(bC[:, o:o + n], bB[:, o:o + n], 0.5e-6, bC[:, o:o + n], ALU.max, ALU.mult, [d_bB[t], d_bC[t]], [d_bC[t]])
                if samp_g and (t is None or t == 2):
                    a3 = bA[:, TPG:TPG + TS].rearrange("p (s t) -> p s t", t=ST)
                    u3 = bC[:, TPG:TPG + TS].rearrange("p (s t) -> p s t", t=ST)
                    tt(tmp16[:], a3[:, :, 0], st_in[:, c, 0:NS], ALU.mult, [d_bA[2], d_stin], [d_tmp16], small=True)
                    tt(u3[:, :, 0], u3[:, :, 0], tmp16[:], ALU.add, [d_tmp16, d_bC[2]], [d_bC[2]], small=True)
                    S.op("dve", lambda e, a3=a3: e.memset(a3[:, :, 0], 0.0), reads=[d_tmp16], writes=[d_bA[2]], small=True)

            def s_scan(t, c=c):
                o, n = tiles[t]
                samp = samp_g and t == 2
                npz = TW if samp else n
                if t == 0:
                    init = 0.0 if g == 0 else hcar[:, c:c + 1]
                    rdi = [] if g == 0 else [d_hcar]
                else:
                    init = bB[:, o - 1:o]
                    rdi = [d_bB[t - 1]]
                S.op("dve", lambda e, o=o, npz=npz, init=init: e.tensor_tensor_scan(
                    out=bB[:, o:o + npz], data0=bA[:, o:o + npz], data1=bC[:, o:o + npz], initial=init,
                    op0=ALU.mult, op1=ALU.add), reads=[d_bA[t], d_bC[t], d_bB[t]] + rdi, writes=[d_bB[t]])
                if samp:
                    h3 = bB[:, TPG:TPG + TS].rearrange("p (s t) -> p s t", t=ST)
                    S.op("dve", lambda e: e.tensor_tensor_scan(
                        out=bB[:, TPG:TPG + TS], data0=bA[:, TPG:TPG + TS], data1=bC[:, TPG:TPG + TS], initial=0.0,
                        op0=ALU.mult, op1=ALU.add), reads=[d_bA[t], d_bC[t], d_bB[t]], writes=[d_bB[t]], small=True)
                    cp(st_out[:, c, 0:1], bB[:, TPG - 1:TPG], [d_bB[t]], [d_stout])
                    cp(st_out[:, c, 6:22], h3[:, :, ST - 1], [d_bB[t]], [d_stout])
                if g == 0 and t == 2:
                    cp(hcar[:, c:c + 1], bB[:, TPG - 1:TPG], [d_bB[t]], [d_hcar])

            def s_ya(c=c, gD=gD, ALLG=ALLG, d_gD=d_gD, t=None):
                if t is None:
                    tt(R[:, c, 0:NG], bB[:, 0:NG], gD[:, 0:NG], ALU.mult, ALLB + ALLG, list(d_R[c]))
                else:
                    o, n = tiles[t]
                    tt(R[:, c, o:o + n], bB[:, o:o + n], gD[:, o:o + n], ALU.mult, [d_bB[t], d_gD[t]], [d_R[c][t]])

            if c < NCH - 1:
                part2b_dve = [s_w, s_uu, lambda: s_scan(0), lambda: s_scan(1), lambda: s_scan(2), s_ya]
            else:
                part2b_dve = []
                for tt_ in range(3):
                    part2b_dve += [lambda tt_=tt_: s_w(t=tt_), lambda tt_=tt_: s_uu(t=tt_),
                                   lambda tt_=tt_: s_scan(tt_), lambda tt_=tt_: s_ya(t=tt_)]

            part2b_prev[0] = (part2b_act, part2b_dve, part2a_tail)
        return part2b_prev[0]

    def merge(l, g, tail):
        G = GROUPS[g]
        tiles = G["tiles"]
        for c in range(NCH):
            wb, d_wb = next_batch()
            prefetch()
            it = lambda i, wb=wb: wb[:, i * 1024:(i + 1) * 1024]
            sA, dA, sC, dC = (xc, d_xc, vc, d_vc) if c == 0 else (bA, d_bA, bC, d_bC)
            for gi_item, dsig, sigbuf in ((0, dA, sA), (1, dC, sC)):
                if c == 0 and gi_item == 1 and tail is not None:
                    pass
                for t, (o, n) in enumerate(tiles):
                    bank, d_b = next_bank()
                    mm_group(bank, d_b, n, [(it(gi_item)[:, k * 128:(k + 1) * 128], u_sb[:, k, o:o + n]) for k in range(NCH)],
                             [d_wb] + [d_u[k][t] for k in range(NCH)])
                    if c == 0:
                        S.op("dve", lambda e, sigbuf=sigbuf, bank=bank, o=o, n=n: e.tensor_copy(out=sigbuf[:, o:o + n], in_=bank[:, 0:n]),
                             reads=[d_b], writes=[dsig[t]])
                        act(sigbuf[:, o:o + n], sigbuf[:, o:o + n], AF.Sigmoid, [dsig[t]], [dsig[t]])
                    else:
                        act(sigbuf[:, o:o + n], bank[:, 0:n], AF.Sigmoid, [d_b], [dsig[t]])
            if c == 0:
                for st_ in tail[1]:
                    st_()
            for t, (o, n) in enumerate(tiles):
                bank, d_b = next_bank()
                mm_group(bank, d_b, n, [(it(2)[:, k * 128:(k + 1) * 128], R[:, 8 + k, o:o + n]) for k in range(NCH)],
                         [d_wb] + [d_R[8 + k][t] for k in range(NCH)])
                tt(bD[:, o:o + n], bank[:, 0:n], sC[:, o:o + n], ALU.mult, [d_b, dC[t]], [d_bD[t]])
            for t, (o, n) in enumerate(tiles):
                bank, d_b = next_bank()
                mm_group(bank, d_b, n, [(it(3)[:, k * 128:(k + 1) * 128], R[:, k, o:o + n]) for k in range(NCH)],
                         [d_wb] + [d_R[k][t] for k in range(NCH)])
                tt(bB[:, o:o + n], bank[:, 0:n], sA[:, o:o + n], ALU.mult, [d_b, dA[t]], [d_bB[t]])
            for t, (o, n) in enumerate(tiles):
                tt(R[:, 16 + c, o:o + n], bB[:, o:o + n], bD[:, o:o + n], ALU.add, [d_bB[t], d_bD[t]], [d_R[16 + c][t]])

    def wout_norm2(l, g):
        G = GROUPS[g]
        tiles = G["tiles"]
        nb = [hold_bank() for _ in range(3)]
        act(dtmp[4][:], dtmp[2][:], AF.Ln, [d_dtmp[2]], [d_dtmp[4]], small=True)
        for bi in range(2):
            wb, d_wb = next_batch()
            prefetch()
            for j in range(4):
                oc = bi * 4 + j
                item = wb[:, j * 1024:(j + 1) * 1024]
                for t, (o, n) in enumerate(tiles):
                    gt = g * 3 + t
                    go = G["off"] + o
                    bank, d_b = next_bank()
                    mm_group(bank, d_b, n, [(item[:, k * 128:(k + 1) * 128], R[:, 16 + k, o:o + n]) for k in range(NCH)],
                             [d_wb] + [d_R[16 + k][t] for k in range(NCH)])
                    tt(x_sb[:, oc, go:go + n], x_sb[:, oc, go:go + n], bank[:, 0:n], ALU.add, [d_b, d_x[oc][gt]], [d_x[oc][gt]])
                if oc > 0:
                    for t in range(3):
                        norm_sq_chunk(g, t, oc - 1, nb[t][0], nb[t][1])
        for t in range(3):
            norm_sq_chunk(g, t, NCH - 1, nb[t][0], nb[t][1])
        for t in range(3):
            norm_finish(l, g, t, 12, nb[t][0], nb[t][1])
            release_bank(nb[t])

    def ffn(l, g, nxt):
        G = GROUPS[g]
        tiles = G["tiles"]
        sbufs = [(bB, d_bB), (bC, d_bC), (bD, d_bD)]
        for bi in range(11):
            wb, d_wb = next_batch()
            prefetch()
            if bi == 0:
                for t, (o, n) in enumerate(tiles):
                    for jj in range(2):
                        gate = wb[:, (2 * jj) * 1024:(2 * jj + 1) * 1024]
                        sbuf_, dsb = sbufs[jj % 3]
                        bank, d_b = next_bank()
                        mm_group(bank, d_b, n, [(gate[:, k * 128:(k + 1) * 128], u_sb[:, k, o:o + n]) for k in range(NCH)],
                                 [d_wb] + [d_u[k][t] for k in range(NCH)])
                        act(sbuf_[:, o:o + n], bank[:, 0:n], AF.Silu, [d_b], [dsb[t]])
                    for jj in range(2):
                        up = wb[:, (2 * jj + 1) * 1024:(2 * jj + 2) * 1024]
                        sbuf_, dsb = sbufs[jj % 3]
                        bank, d_b = next_bank()
                        mm_group(bank, d_b, n, [(up[:, k * 128:(k + 1) * 128], u_sb[:, k, o:o + n]) for k in range(NCH)],
                                 [d_wb] + [d_u[k][t] for k in range(NCH)])
                        tt(R[:, jj, o:o + n], bank[:, 0:n], sbuf_[:, o:o + n], ALU.mult, [d_b, dsb[t]], [d_R[jj][t]])
                continue
            for jj in range(2):
                j = bi * 2 + jj
                gate = wb[:, (2 * jj) * 1024:(2 * jj + 1) * 1024]
                up = wb[:, (2 * jj + 1) * 1024:(2 * jj + 2) * 1024]
                sbuf_, dsb = sbufs[j % 3]
                for t, (o, n) in enumerate(tiles):
                    bank, d_b = next_bank()
                    mm_group(bank, d_b, n, [(gate[:, k * 128:(k + 1) * 128], u_sb[:, k, o:o + n]) for k in range(NCH)],
                             [d_wb] + [d_u[k][t] for k in range(NCH)])
                    act(sbuf_[:, o:o + n], bank[:, 0:n], AF.Silu, [d_b], [dsb[t]])
                for t, (o, n) in enumerate(tiles):
                    bank, d_b = next_bank()
                    mm_group(bank, d_b, n, [(up[:, k * 128:(k + 1) * 128], u_sb[:, k, o:o + n]) for k in range(NCH)],
                             [d_wb] + [d_u[k][t] for k in range(NCH)])
                    tt(R[:, j, o:o + n], bank[:, 0:n], sbuf_[:, o:o + n], ALU.mult, [d_b, dsb[t]], [d_R[j][t]])
        hoist = {1: 0, 3: 1, 5: 2}
        for oc in range(NCH):
            wb, d_wb = next_batch()
            prefetch()
            for t, (o, n) in enumerate(tiles):
                gt = g * 3 + t
                go = G["off"] + o
                bank, d_b = next_bank()
                mm_group(bank, d_b, n, [(wb[:, k * 128:(k + 1) * 128], R[:, k, o:o + n]) for k in range(NFF)],
                         [d_wb] + [d_R[k][t] for k in range(NFF)])
                tt(x_sb[:, oc, go:go + n], x_sb[:, oc, go:go + n], bank[:, 0:n], ALU.add, [d_b, d_x[oc][gt]], [d_x[oc][gt]])
            if nxt is not None and oc in hoist:
                norm_tile(nxt[0], nxt[1], hoist[oc], 0)
            if nxt is None and oc in hoist:
                norm_tile(0, 0, hoist[oc], 0, final=True)
                if hoist[oc] == 2:
                    for c0 in range(0, NCH, 4):
                        S.dma("sp", lambda e, c0=c0: e.dma_start(out=y_d[:, c0:c0 + 4, 0:TPG], in_=x_sb[:, c0:c0 + 4, 0:TPG]), "st",
                              reads=[d_x[c][t] for c in range(c0, c0 + 4) for t in range(3)])

    seq = [(l, g) for l in range(depth) for g in range(2)]
    for t in range(3):
        norm_tile(0, 0, t, 0)
    for i, (l, g) in enumerate(seq):
        if g == 0:
            S.dma("pool", lambda e, l=l: e.dma_start(out=gw[:], in_=ws_d[l, :, 0:W_GATES], max_dma_last_dim=4096), "gwl",
                  writes=[d_gw])
            S.dma("sp", lambda e, l=l: e.dma_start(out=st_in[:].rearrange("p c n -> p (c n)"), in_=stin_d[l]), "ldst",
                  writes=[d_stin])
        tail = mixer(l, g)
        tail[2]()
        tail[0]()
        merge(l, g, tail)
        if g == 1:
            S.dma("sp", lambda e, l=l: e.dma_start(out=sto_d[l], in_=st_out[:].rearrange("p c n -> p (c n)")), "st",
                  reads=[d_stout])
        wout_norm2(l, g)
        ffn(l, g, seq[i + 1] if i + 1 < len(seq) else None)

    for t in range(3):
        norm_tile(0, 1, t, 0, final=True)
    for c0 in range(0, NCH, 4):
        S.dma("sp", lambda e, c0=c0: e.dma_start(out=y_d[:, c0:c0 + 4, TPG:TTOT], in_=x_sb[:, c0:c0 + 4, TPG:TTOT]), "st",
              reads=[d_x[c][t] for c in range(c0, c0 + 4) for t in range(3, 6)])

    sem_keys = list(Sched.ENGS) + sorted(S.dma_cnt.keys())
    sems = {k: es.enter_context(nc.semaphore("s_" + k)) for k in sem_keys}

    def emit(name, e):
        for waits, fn, key, inc in S.streams[name]:
            for k, v in waits:
                e.wait_ge(sems[k], v)
            ins = fn(e)
            ins.then_inc(sems[key], inc)
        if name == "sp":
            for k, v in S.dma_cnt.items():
                e.wait_ge(sems[k], v)

    with nc.Block() as block:
        @block.tensor
        def _(e):
            emit("pe", e)

        @block.scalar
        def _(e):
            emit("act", e)

        @block.vector
        def _(e):
            emit("dve", e)

        @block.gpsimd
        def _(e):
            emit("pool", e)

        @block.sync
        def _(e):
            emit("sp", e)
    es.close()
    return nc


def _fm(a):
    T = a.shape[0]
    return np.ascontiguousarray(a.reshape(T, NCH, 128).transpose(2, 1, 0))


def _pack_weights(inp):
    ws = np.zeros((DEPTH, 128, WS_LAYER), np.float32)
    for l in range(DEPTH):
        off = 0
        gwl = np.zeros((128, 2, 8, 128), np.float32)
        for gi, name in enumerate(("gate_a_w", "gate_x_w")):
            w = inp[name][l]
            w2 = w.reshape(8, 2, 64, 64)
            gwl[0:64, gi, :, 0:64] = w2[:, 0].transpose(1, 0, 2)
            gwl[64:128, gi, :, 64:128] = w2[:, 1].transpose(1, 0, 2)
        ws[l, :, off:off + W_GATES] = gwl.reshape(128, W_GATES); off += W_GATES

        def item(W, col0):
            K = W.shape[0]
            blk = W[:, col0:col0 + 128].reshape(K // 128, 128, 128)
            return blk.transpose(1, 0, 2).reshape(128, K)

        w_in = inp["w_in"][l]
        for c in range(8):
            for s in ("xr", "gr", "cc", "hc", "bc"):
                ws[l, :, off:off + 1024] = item(w_in, OFF[s] + c * 128); off += 1024
        wa, wbm = inp["w_branch_a"][l], inp["w_branch_b"][l]
        for c in range(8):
            ws[l, :, off:off + 1024] = item(w_in, OFF["ga"] + c * 128); off += 1024
            ws[l, :, off:off + 1024] = item(w_in, OFF["gb"] + c * 128); off += 1024
            ws[l, :, off:off + 1024] = item(wbm, c * 128); off += 1024
            ws[l, :, off:off + 1024] = item(wa, c * 128); off += 1024
        wo = inp["w_out"][l]
        for c in range(8):
            ws[l, :, off:off + 1024] = item(wo, c * 128); off += 1024
        wg, wu = inp["w_ff_gate"][l], inp["w_ff_up"][l]
        for j in range(NFF):
            ws[l, :, off:off + 1024] = item(wg, j * 128); off += 1024
            ws[l, :, off:off + 1024] = item(wu, j * 128); off += 1024
        wd = inp["w_ff_down"][l]
        for c in range(8):
            ws[l, :, off:off + W_DN] = item(wd, c * 128); off += W_DN
        assert off == WS_LAYER
    return ws


def _pack_vecs(inp):
    v = np.zeros((128, NVEC * DEPTH * 8 + 8), np.float32)
    rows = []
    for l in range(DEPTH):
        rows.append([inp["norm1_g"][l]] + [inp["rnn_conv_w"][l, k] for k in range(4)] + [inp["rnn_conv_b"][l],
                    inp["gate_a_b"][l], inp["gate_x_b"][l], inp["lru_lambda"][l]] +
                    [inp["sc_conv_w"][l, k] for k in range(3)] + [inp["norm2_g"][l]])
    for i in range(NVEC):
        for l in range(DEPTH):
            o = (i * DEPTH + l) * 8
            v[:, o:o + 8] = np.asarray(rows[l][i]).reshape(8, 128).T
    v[:, NVEC * DEPTH * 8:] = np.asarray(inp["final_norm_g"]).reshape(8, 128).T
    return v


def kernel(**inputs):
    inp = {k: np.asarray(v) for k, v in inputs.items()}
    nc = build_program(DEPTH)
    ws = _pack_weights(inp)
    vecs = _pack_vecs(inp)
    meta = _fm(inp["meta_tokens"].astype(np.float32))
    in_maps = []
    for i in range(NCORES):
        xp = _fm(inp["x_prompt"][i])
        xs = _fm(inp["x_sample"][i * NS:(i + 1) * NS].reshape(TS, D))
        stin = np.zeros((DEPTH, 128, NCH, NSTI), np.float32)
        h0 = inp["state_rnn_h"][:, i * NS:(i + 1) * NS]
        rc = inp["state_rnn_conv"][:, i * NS:(i + 1) * NS]
        sc = inp["state_sc_conv"][:, i * NS:(i + 1) * NS]
        stin[:, :, :, 0:16] = h0.reshape(DEPTH, NS, NCH, 128).transpose(0, 3, 2, 1)
        stin[:, :, :, 16:64] = rc.reshape(DEPTH, NS, 3, NCH, 128).transpose(0, 4, 3, 1, 2).reshape(DEPTH, 128, NCH, 48)
        stin[:, :, :, 64:96] = sc.reshape(DEPTH, NS, 2, NCH, 128).transpose(0, 4, 3, 1, 2).reshape(DEPTH, 128, NCH, 32)
        in_maps.append(dict(xp=xp, meta=meta, xs=xs, stin=np.ascontiguousarray(stin.reshape(DEPTH, 128, NCH * NSTI)),
                            vecs=vecs, ws=ws))
    res = run_bass_kernel_spmd(nc, in_maps, core_ids=list(range(NCORES)))
    y_prompt = np.zeros((NCORES, SEQ, D), np.float32)
    y_sample = np.zeros((NCORES * NS, ST, D), np.float32)
    rnn_h_p = np.zeros((DEPTH, NCORES, D), np.float32)
    rnn_c_p = np.zeros((DEPTH, NCORES, 3, D), np.float32)
    sc_c_p = np.zeros((DEPTH, NCORES, 2, D), np.float32)
    rnn_h_s = np.zeros((DEPTH, NCORES * NS, D), np.float32)
    rnn_c_s = np.zeros((DEPTH, NCORES * NS, 3, D), np.float32)
    sc_c_s = np.zeros((DEPTH, NCORES * NS, 2, D), np.float32)
    for i in range(NCORES):
        r = res.results[i]
        y = np.asarray(r["y"]).reshape(128, NCH, TTOT)
        yt = y.transpose(2, 1, 0).reshape(TTOT, D)
        y_prompt[i] = yt[NMETA:TP]
        y_sample[i * NS:(i + 1) * NS] = yt[TP:].reshape(NS, ST, D)
        so = np.asarray(r["sto"]).reshape(DEPTH, 128, NCH, NSTO)
        so = so.transpose(0, 3, 2, 1).reshape(DEPTH, NSTO, D)
        rnn_h_p[:, i] = so[:, 0]
        rnn_c_p[:, i] = so[:, 1:4]
        sc_c_p[:, i] = so[:, 4:6]
        rnn_h_s[:, i * NS:(i + 1) * NS] = so[:, 6:22]
        rnn_c_s[:, i * NS:(i + 1) * NS] = so[:, 22:70].reshape(DEPTH, NS, 3, D)
        sc_c_s[:, i * NS:(i + 1) * NS] = so[:, 70:102].reshape(DEPTH, NS, 2, D)
    return (y_prompt, y_sample, rnn_h_p, rnn_c_p, sc_c_p, rnn_h_s, rnn_c_s, sc_c_s)
```

```python
import numpy as np
from contextlib import ExitStack
import concourse.bass as bass
import concourse.mybir as mybir
from concourse.bass_utils import run_bass_kernel_spmd

F32 = mybir.dt.float32
BF16 = mybir.dt.bfloat16
AF = mybir.ActivationFunctionType
ALU = mybir.AluOpType

D = 1024
NCH = 8
DFF = 2816
NFF = 22
DEPTH = 4
NMETA = 16
SEQ = 2048
TP = NMETA + SEQ
NS = 16
ST = 4
TS = NS * ST
TTOT = TP + TS
NCORES = 8
EPS = 1e-6
OFF = dict(xr=0, gr=1024, bc=2048, cc=3072, hc=4096, ga=5120, gb=6144)
TPG = TP // 2
TW = 344
GROUPS = [
    dict(off=0, n=TPG, tiles=[(0, TW), (TW, TW), (2 * TW, TW)], samp=False),
    dict(off=TPG, n=TPG + TS, tiles=[(0, TW), (TW, TW), (2 * TW, TW + TS)], samp=True),
]
GW = TPG + TS
NVEC = 13
NSTI = 96
NSTO = 102
W_GATES = 2 * 8 * 128
W_MIX = 5 * 1024
W_MRG = 4 * 1024
W_OUT = 4 * 1024
W_FFN = 4 * 1024
W_DN = NFF * 128
WS_LAYER = W_GATES + 8 * W_MIX + 8 * W_MRG + 2 * W_OUT + 11 * W_FFN + 8 * W_DN
WBUF = 5120
XRW = 1160
SB0 = 1040
NSQ = 3

SAME_SYNC_ALL = True


class Dep:
    __slots__ = ("w", "r")

    def __init__(self):
        self.w = None
        self.r = {}


class Sched:
    ENGS = ("pe", "act", "dve", "pool", "sp")

    def __init__(self):
        self.streams = {e: [] for e in self.ENGS}
        self.tick = {e: 0 for e in self.ENGS}
        self.known = {e: {} for e in self.ENGS}
        self.dma_cnt = {}

    def _waits(self, eng, reads, writes, small):
        waits = {}

        def need(k, v):
            if k == eng and (eng == "pe" or not (small or SAME_SYNC_ALL)):
                return
            if self.known[eng].get(k, 0) >= v:
                return
            if waits.get(k, 0) < v:
                waits[k] = v

        for t in reads:
            if t.w is not None:
                need(*t.w)
        for t in writes:
            if t.w is not None:
                need(*t.w)
            for k, v in t.r.items():
                need(k, v)
        for k, v in waits.items():
            self.known[eng][k] = v
        return list(waits.items())

    def _mark(self, tok, reads, writes):
        k, v = tok
        for t in reads:
            t.r[k] = v
        for t in writes:
            t.w = tok
            t.r = {}

    def op(self, eng, fn, reads=(), writes=(), small=False):
        waits = self._waits(eng, reads, writes, small)
        self.tick[eng] += 1
        tok = (eng, self.tick[eng])
        self.streams[eng].append((waits, fn, eng, 1))
        self._mark(tok, reads, writes)

    def dma(self, eng, fn, semkey, reads=(), writes=()):
        waits = self._waits(eng, reads, writes, False)
        self.dma_cnt[semkey] = self.dma_cnt.get(semkey, 0) + 16
        tok = (semkey, self.dma_cnt[semkey])
        self.streams[eng].append((waits, fn, semkey, 16))
        self._mark(tok, reads, writes)


def build_program(depth=DEPTH):
    nc = bass.Bass("TRN2", target_bir_lowering=False)
    xp_d = nc.dram_tensor("xp", [128, NCH, SEQ], F32, kind="ExternalInput").ap()
    meta_d = nc.dram_tensor("meta", [128, NCH, NMETA], F32, kind="ExternalInput").ap()
    xs_d = nc.dram_tensor("xs", [128, NCH, TS], F32, kind="ExternalInput").ap()
    stin_d = nc.dram_tensor("stin", [DEPTH, 128, NCH * NSTI], F32, kind="ExternalInput").ap()
    vecs_d = nc.dram_tensor("vecs", [128, NVEC * DEPTH * 8 + 8], F32, kind="ExternalInput").ap()
    ws_d = nc.dram_tensor("ws", [DEPTH, 128, WS_LAYER], F32, kind="ExternalInput").ap()
    y_d = nc.dram_tensor("y", [128, NCH, TTOT], F32, kind="ExternalOutput").ap()
    sto_d = nc.dram_tensor("sto", [DEPTH, 128, NCH * NSTO], F32, kind="ExternalOutput").ap()

    S = Sched()
    es = ExitStack()

    def sb(name, shape, dt):
        return es.enter_context(nc.sbuf_tensor(name, shape, dt))

    x_sb = sb("x_sb", [128, NCH, TTOT], F32)
    u_sb = sb("u_sb", [128, NCH, GW], BF16)
    R = sb("R", [128, 24, GW], BF16)
    wbuf = [sb("wbuf0", [128, WBUF], BF16), sb("wbuf1", [128, WBUF], BF16)]
    gw = sb("gw", [128, W_GATES], BF16)
    vec_sb = sb("vec_sb", [128, NVEC * DEPTH * 8 + 8], F32)
    der_sb = sb("der_sb", [128, 4 * DEPTH * 8], F32)
    dtmp = [sb("dtmp%d" % i, [128, DEPTH * 8], F32) for i in range(6)]
    ones_bf = sb("ones_bf", [128, 128], BF16)
    xr_sb = sb("xr_sb", [128, XRW], F32)
    xc = sb("xc", [128, GW], F32)
    xcb = sb("xcb", [128, GW], BF16)
    vc = sb("vc", [128, GW], F32)
    bA = sb("bA", [128, GW], F32)
    bB = sb("bB", [128, GW], F32)
    bC = sb("bC", [128, GW], F32)
    bD = sb("bD", [128, GW], F32)
    bD1 = sb("bD1", [128, GW], F32)
    sqb = [sb("sqb%d" % i, [128, 416], BF16) for i in range(NSQ)]
    st_in = sb("st_in", [128, NCH, NSTI], F32)
    st_out = sb("st_out", [128, NCH, NSTO], F32)
    hcar = sb("hcar", [128, NCH], F32)
    xhalo = sb("xhalo", [128, NCH, 3], F32)
    chhalo = sb("chhalo", [128, NCH, 2], F32)
    tmp16 = sb("tmp16", [128, NS], F32)
    banks = [es.enter_context(nc.psum_tensor("ps%d" % i, [128, 512], F32)) for i in range(8)]

    d_x = [[Dep() for _ in range(6)] for _ in range(NCH)]
    d_u = [[Dep() for _ in range(3)] for _ in range(NCH)]
    d_R = [[Dep() for _ in range(3)] for _ in range(24)]
    d_wbuf = [Dep(), Dep()]
    d_gw = Dep()
    d_vec = Dep()
    d_der = Dep()
    d_dtmp = [Dep() for _ in range(6)]
    d_ones = Dep()
    d_xr = [Dep() for _ in range(3)]
    d_xrh = Dep()
    d_xc = [Dep() for _ in range(3)]
    d_xcb = [Dep() for _ in range(3)]
    d_vc = [Dep() for _ in range(3)]
    d_bA = [Dep() for _ in range(3)]
    d_bB = [Dep() for _ in range(3)]
    d_bC = [Dep() for _ in range(3)]
    d_bD = [Dep() for _ in range(3)]
    d_bD1 = [Dep() for _ in range(3)]
    d_sqb = [Dep() for _ in range(NSQ)]
    d_stin = Dep()
    d_stout = Dep()
    d_hcar = Dep()
    d_xhalo = Dep()
    d_chhalo = Dep()
    d_tmp16 = Dep()
    d_bank = [Dep() for _ in range(8)]
    d_y = Dep()

    bank_ctr = [0]

    held = set()

    def next_bank():
        while True:
            b = bank_ctr[0] % 8
            bank_ctr[0] += 1
            if b not in held:
                return banks[b], d_bank[b]

    def hold_bank():
        bk = next_bank()
        held.add(banks.index(bk[0]))
        return bk

    def release_bank(bk):
        held.discard(banks.index(bk[0]))

    def V(l, i, c):
        o = (i * DEPTH + l) * 8 + c
        return vec_sb[:, o:o + 1]

    def DER(kind, l, c):
        o = kind * DEPTH * 8 + l * 8 + c
        return der_sb[:, o:o + 1]

    batches = []
    for l in range(depth):
        for g in range(2):
            off = W_GATES
            for _ in range(8):
                batches.append((l, off, W_MIX)); off += W_MIX
            for _ in range(8):
                batches.append((l, off, W_MRG)); off += W_MRG
            for _ in range(2):
                batches.append((l, off, W_OUT)); off += W_OUT
            for _ in range(11):
                batches.append((l, off, W_FFN)); off += W_FFN
            for _ in range(8):
                batches.append((l, off, W_DN)); off += W_DN
            assert off == WS_LAYER
    bstate = dict(issued=0, used=0)

    def issue_batch():
        i = bstate["issued"]
        if i >= len(batches):
            return
        l, off, n = batches[i]
        b = i % 2
        src = ws_d[l, :, off:off + n]
        dst = wbuf[b][:, 0:n]
        S.dma("pool", lambda e, src=src, dst=dst: e.dma_start(out=dst, in_=src, max_dma_last_dim=4096),
              "w%d" % b, reads=(), writes=(d_wbuf[b],))
        bstate["issued"] += 1

    def next_batch():
        i = bstate["used"]
        while bstate["issued"] <= min(i, len(batches) - 1):
            issue_batch()
        b = i % 2
        bstate["used"] += 1
        return wbuf[b], d_wbuf[b]

    def prefetch():
        if bstate["issued"] < bstate["used"] + 1:
            issue_batch()

    def mm_group(bank, d_b, n, pairs, reads):
        def fn(e, bank=bank, n=n, pairs=pairs):
            last = None
            for i, (lt, rh) in enumerate(pairs):
                last = e.matmul(out=bank[:, 0:n], lhsT=lt, rhs=rh, start=(i == 0), stop=(i == len(pairs) - 1))
            return last
        S.op("pe", fn, reads=reads, writes=(d_b,))

    def act(out, in_, func, reads, writes, bias=None, scale=None, small=False):
        kw = {}
        if bias is not None:
            kw["bias"] = bias
        if scale is not None:
            kw["scale"] = scale
        S.op("act", lambda e: e.activation(out=out, in_=in_, func=func, **kw), reads=reads, writes=writes, small=small)

    def tt(out, in0, in1, op, reads, writes, small=False):
        S.op("dve", lambda e: e.tensor_tensor(out=out, in0=in0, in1=in1, op=op), reads=reads, writes=writes, small=small)

    def ts(out, in0, s1, op0, reads, writes, s2=None, op1=None, small=False):
        if op1 is None:
            S.op("dve", lambda e: e.tensor_scalar(out=out, in0=in0, scalar1=s1, scalar2=None, op0=op0),
                 reads=reads, writes=writes, small=small)
        else:
            S.op("dve", lambda e: e.tensor_scalar(out=out, in0=in0, scalar1=s1, scalar2=s2, op0=op0, op1=op1),
                 reads=reads, writes=writes, small=small)

    def stt(out, in0, scalar, in1, op0, op1, reads, writes, small=False):
        S.op("dve", lambda e: e.scalar_tensor_tensor(out=out, in0=in0, scalar=scalar, in1=in1, op0=op0, op1=op1),
             reads=reads, writes=writes, small=small)

    def cp(out, in_, reads, writes, small=True):
        S.op("dve", lambda e: e.tensor_copy(out=out, in_=in_), reads=reads, writes=writes, small=small)

    S.dma("sp", lambda e: e.dma_start(out=vec_sb[:], in_=vecs_d), "ldv", writes=(d_vec,))
    NP0 = TPG - NMETA
    S.dma("sp", lambda e: e.dma_start(out=x_sb[:, :, 0:NMETA], in_=meta_d), "ldm",
          writes=[d_x[c][0] for c in range(NCH)])
    for t, (o, n) in enumerate(GROUPS[0]["tiles"]):
        lo = max(o, NMETA)
        S.dma("sp", lambda e, lo=lo, o=o, n=n: e.dma_start(out=x_sb[:, :, lo:o + n], in_=xp_d[:, :, lo - NMETA:o + n - NMETA]),
              "ldx%d" % t, writes=[d_x[c][t] for c in range(NCH)])

    def deferred_loads():
        S.dma("sp", lambda e: e.dma_start(out=x_sb[:, :, TP:TTOT], in_=xs_d), "lds",
              reads=[d_u[NCH - 1][2]], writes=[d_x[c][5] for c in range(NCH)])
        for c0 in range(0, NCH, 4):
            S.dma("sp", lambda e, c0=c0: e.dma_start(out=x_sb[:, c0:c0 + 4, TPG:TP], in_=xp_d[:, c0:c0 + 4, NP0:SEQ]), "ldy%d" % c0,
                  reads=[d_u[NCH - 1][2]], writes=[d_x[c][t] for c in range(c0, c0 + 4) for t in range(3, 6)])
    S.op("dve", lambda e: e.memset(ones_bf[:], 1.0), writes=(d_ones,))

    def derived_constants():
        NL = DEPTH * 8
        lam = vec_sb[:, 8 * NL:9 * NL]
        t0, t1, t2, t3, t4, t5 = [t[:] for t in dtmp]
        dd = d_dtmp
        ts(t0, lam, -1.0, ALU.mult, [d_vec], [dd[0]], small=True)
        tt(t0, t0, lam, ALU.min, [d_vec, dd[0]], [dd[0]], small=True)
        act(t1, t0, AF.Exp, [dd[0]], [dd[1]], small=True)
        ts(t2, t1, 2.0, ALU.add, [dd[1]], [dd[2]], small=True)
        S.op("dve", lambda e: e.reciprocal(out=t2, in_=t2), reads=[dd[2]], writes=[dd[2]], small=True)
        tt(t3, t1, t2, ALU.mult, [dd[1], dd[2]], [dd[3]], small=True)
        tt(t4, t3, t3, ALU.mult, [dd[3]], [dd[4]], small=True)
        S.op("dve", lambda e: e.memset(t5, 0.0), writes=[dd[5]], small=True)
        for k in range(9, 0, -1):
            stt(t5, t5, 1.0 / (2 * k + 1), t4, ALU.add, ALU.mult, [dd[5], dd[4]], [dd[5]], small=True)
        stt(t5, t5, 1.0, t3, ALU.add, ALU.mult, [dd[5], dd[3]], [dd[5]], small=True)
        ts(t0, lam, -1.0, ALU.mult, [d_vec, dd[0]], [dd[0]], s2=0.0, op1=ALU.max, small=True)
        stt(t1, t5, 2.0, t0, ALU.mult, ALU.add, [dd[5], dd[0], dd[1]], [dd[1]], small=True)
        ts(der_sb[:, 0 * NL:1 * NL], vec_sb[:, 6 * NL:7 * NL], 0.5, ALU.mult, [d_vec], [d_der], small=True)
        ts(der_sb[:, 1 * NL:2 * NL], vec_sb[:, 7 * NL:8 * NL], 0.5, ALU.mult, [d_vec], [d_der], small=True)
        ts(der_sb[:, 2 * NL:3 * NL], t1, -4.0, ALU.mult, [dd[1]], [d_der], small=True)
        ts(der_sb[:, 3 * NL:4 * NL], t1, 2.0, ALU.mult, [dd[1]], [d_der], small=True)


    sq_ctr = [0]

    def norm_sq_chunk(g, t, c, bank, d_b):
        G = GROUPS[g]
        o, n = G["tiles"][t]
        gt = g * 3 + t
        go = G["off"] + o
        q = sq_ctr[0] % NSQ
        sq_ctr[0] += 1
        act(sqb[q][:, 0:n], x_sb[:, c, go:go + n], AF.Square, [d_x[c][gt]], [d_sqb[q]])
        S.op("pe", lambda e: e.matmul(out=bank[:, 0:n], lhsT=ones_bf[:], rhs=sqb[q][:, 0:n],
                                      start=(c == 0), stop=(c == NCH - 1)),
             reads=[d_sqb[q], d_ones], writes=[d_b])

    def norm_finish(l, g, t, gi, bank, d_b, final=False):
        G = GROUPS[g]
        o, n = G["tiles"][t]
        gt = g * 3 + t
        go = G["off"] + o
        act(bA[:, o:o + n], bank[:, 0:n], AF.Ln, [d_b], [d_bA[t]], bias=EPS, scale=1.0 / D)
        act(bA[:, o:o + n], bA[:, o:o + n], AF.Exp, [d_bA[t]], [d_bA[t]], scale=-0.5)
        for c in range(NCH):
            if final:
                fo = NVEC * DEPTH * 8 + c
                stt(x_sb[:, c, go:go + n], x_sb[:, c, go:go + n], vec_sb[:, fo:fo + 1], bA[:, o:o + n], ALU.mult, ALU.mult,
                    [d_x[c][gt], d_bA[t], d_vec], [d_x[c][gt]])
            else:
                stt(u_sb[:, c, o:o + n], x_sb[:, c, go:go + n], V(l, gi, c), bA[:, o:o + n], ALU.mult, ALU.mult,
                    [d_x[c][gt], d_bA[t], d_vec], [d_u[c][t]])

    def norm_tile(l, g, t, gi, final=False):
        bank, d_b = next_bank()
        for c in range(NCH):
            norm_sq_chunk(g, t, c, bank, d_b)
        norm_finish(l, g, t, gi, bank, d_b, final)

    def conv_taps(G, width, wvec, l, c, k0, outbuf, d_out, ks=None):
        xs3 = xr_sb[:, SB0:SB0 + NS * 7].rearrange("p (s t) -> p s t", t=7)
        xo = outbuf[:, TPG:TPG + TS].rearrange("p (s t) -> p s t", t=ST)
        rd = list(d_xr) + [d_xrh, d_vec]
        for k in (range(k0, width) if ks is None else ks):
            wk = V(l, wvec + (width - 1 - k), c)
            if k == 0:
                ts(outbuf[:, 0:TPG], xr_sb[:, 3:3 + TPG], wk, ALU.mult, rd, list(d_out))
            else:
                stt(outbuf[:, 0:TPG], xr_sb[:, 3 - k:3 - k + TPG], wk, outbuf[:, 0:TPG], ALU.mult, ALU.add,
                    rd + list(d_out), list(d_out))
            if G["samp"]:
                if k == 0:
                    ts(xo, xs3[:, :, 3:7], wk, ALU.mult, rd, [d_out[2]])
                else:
                    stt(xo, xs3[:, :, 3 - k:7 - k], wk, xo, ALU.mult, ALU.add, rd + [d_out[2]], [d_out[2]])

    def mixer(l, g):
        G = GROUPS[g]
        tiles = G["tiles"]
        samp_g = G["samp"]
        xs3 = xr_sb[:, SB0:SB0 + NS * 7].rearrange("p (s t) -> p s t", t=7)
        part2b_prev = [None]
        NG = G["n"]
        ALLA, ALLB, ALLC, ALLD = list(d_bA), list(d_bB), list(d_bC), list(d_bD)

        for c in range(NCH):
            wb, d_wb = next_batch()
            prefetch()
            it = lambda i, wb=wb: wb[:, i * 1024:(i + 1) * 1024]
            gD, d_gD = (bD, d_bD) if c % 2 == 0 else (bD1, d_bD1)
            ALLG = list(d_gD)

            def mmw(item, t, d_wb=d_wb):
                o, n = tiles[t]
                bank, d_b = next_bank()
                mm_group(bank, d_b, n, [(item[:, k * 128:(k + 1) * 128], u_sb[:, k, o:o + n]) for k in range(NCH)],
                         [d_wb] + [d_u[k][t] for k in range(NCH)])
                return bank, d_b, o, n

            if g == 0:
                S.op("dve", lambda e: e.memset(xr_sb[:, 0:3], 0.0), reads=[], writes=[d_xrh], small=True)
            else:
                cp(xr_sb[:, 0:3], xhalo[:, c, :], [d_xhalo], [d_xrh])
            if samp_g:
                cp(xs3[:, :, 0:3], st_in[:, c, 16:64].rearrange("p (s t) -> p s t", t=3), [d_stin], [d_xrh])
            xrb = []
            for t in range(3):
                bank, d_b, o, n = mmw(it(0), t)
                xrb.append((bank, d_b, o, n))
                samp = samp_g and t == 2
                npz = TW if samp else n
                act(xr_sb[:, 3 + o:3 + o + npz], bank[:, 0:npz], AF.Copy, [d_b], [d_xr[t]])
                if samp:
                    act(xs3[:, :, 3:7], bank[:, TW:TW + TS].rearrange("p (s t) -> p s t", t=ST), AF.Copy, [d_b], [d_xr[t]])
            for t, (bank, d_b, o, n) in enumerate(xrb):
                act(xc[:, o:o + n], bank[:, 0:n], AF.Identity, [d_b, d_vec], [d_xc[t]], bias=V(l, 5, c), scale=V(l, 4, c))
            conv_taps(G, 4, 1, l, c, 1, xc, d_xc)
            if g == 0:
                cp(xhalo[:, c, :], xr_sb[:, 3 + TPG - 3:3 + TPG], [d_xr[2]], [d_xhalo])
            else:
                cp(st_out[:, c, 1:4], xr_sb[:, 3 + TPG - 3:3 + TPG], [d_xr[2]], [d_stout])
                cp(st_out[:, c, 22:70].rearrange("p (s t) -> p s t", t=3), xs3[:, :, 4:7], [d_xr[2]], [d_stout])

            if part2b_prev[0] is not None:
                part2b_prev[0][2]()
            for t in range(3):
                bank, d_b, o, n = mmw(it(1), t)
                act(gD[:, o:o + n], bank[:, 0:n], AF.Copy, [d_b], [d_gD[t]])
            act(xcb[:, 0:NG], xc[:, 0:NG], AF.Copy, list(d_xc), list(d_xcb))

            if g == 1:
                cp(xr_sb[:, 1:3], chhalo[:, c, :], [d_chhalo], [d_xrh])
            if samp_g:
                cp(xs3[:, :, 1:3], st_in[:, c, 64:96].rearrange("p (s t) -> p s t", t=2), [d_stin], [d_xrh])
            for t in range(3):
                bank, d_b, o, n = mmw(it(2), t)
                samp = samp_g and t == 2
                npz = TW if samp else n
                act(xr_sb[:, 3 + o:3 + o + npz], bank[:, 0:npz], AF.Copy, [d_b], [d_xr[t]])
                if samp:
                    act(xs3[:, :, 3:7], bank[:, TW:TW + TS].rearrange("p (s t) -> p s t", t=ST), AF.Copy, [d_b], [d_xr[t]])
            if part2b_prev[0] is not None:
                part2b_prev[0][0]()
            act(gD[:, 0:NG], gD[:, 0:NG], AF.Gelu_apprx_tanh, ALLG, ALLG)
            for t in range(3):
                bank, d_b, o, n = mmw(it(3), t)
                samp = samp_g and t == 2
                npz = TW if samp else n
                tt(xr_sb[:, 3 + o:3 + o + npz], xr_sb[:, 3 + o:3 + o + npz], bank[:, 0:npz], ALU.mult, [d_b, d_xr[t]], [d_xr[t]])
                if samp:
                    tt(xs3[:, :, 3:7], xs3[:, :, 3:7], bank[:, TW:TW + TS].rearrange("p (s t) -> p s t", t=ST), ALU.mult,
                       [d_b, d_xr[t]], [d_xr[t]])
            steps = list(part2b_prev[0][1]) if part2b_prev[0] is not None else []

            def step():
                if steps:
                    steps.pop(0)()
            conv_taps(G, 3, 9, l, c, 0, vc, d_vc)
            for t in range(3):
                bank, d_b, o, n = mmw(it(4), t)
                tt(R[:, 8 + c, o:o + n], bank[:, 0:n], vc[:, o:o + n], ALU.mult, [d_b, d_vc[t]], [d_R[8 + c][t]])
            if g == 0:
                cp(chhalo[:, c, :], xr_sb[:, 3 + TPG - 2:3 + TPG], [d_xr[2]], [d_chhalo])
            else:
                cp(st_out[:, c, 4:6], xr_sb[:, 3 + TPG - 2:3 + TPG], [d_xr[2]], [d_stout])
                cp(st_out[:, c, 70:102].rearrange("p (s t) -> p s t", t=2), xs3[:, :, 5:7], [d_xr[2]], [d_stout])
            while steps:
                step()
            part2b_prev[0] = None

            for t, (o, n) in enumerate(tiles):
                br, d_br = next_bank()
                S.op("pe", lambda e, br=br, n=n, o=o, c=c: e.matmul(out=br[:, 0:n], lhsT=gw[:, c * 128:(c + 1) * 128],
                                                                     rhs=xcb[:, o:o + n], start=True, stop=True),
                     reads=[d_gw, d_xcb[t]], writes=[d_br])
                bi, d_bi = next_bank()
                S.op("pe", lambda e, bi=bi, n=n, o=o, c=c: e.matmul(out=bi[:, 0:n], lhsT=gw[:, 1024 + c * 128:1024 + (c + 1) * 128],
                                                                     rhs=xcb[:, o:o + n], start=True, stop=True),
                     reads=[d_gw, d_xcb[t]], writes=[d_bi])
                act(bA[:, o:o + n], br[:, 0:n], AF.Tanh, [d_br, d_der], [d_bA[t]], bias=DER(0, l, c), scale=0.5)
                act(bC[:, o:o + n], bi[:, 0:n], AF.Tanh, [d_bi, d_der], [d_bC[t]], bias=DER(1, l, c), scale=0.5)
            stt(bC[:, 0:NG], bC[:, 0:NG], 1.0, xc[:, 0:NG], ALU.add, ALU.mult, ALLC + list(d_xc), ALLC)

            def part2a_tail(c=c):
                act(bB[:, 0:NG], bA[:, 0:NG], AF.Tanh, ALLA + [d_der], ALLB, bias=DER(3, l, c), scale=DER(3, l, c))
                act(bA[:, 0:NG], bA[:, 0:NG], AF.Exp, ALLA + ALLB + [d_der], ALLA, bias=DER(2, l, c), scale=DER(2, l, c))

            def part2b_act(c=c, gD=gD, ALLG=ALLG):
                act(bB[:, 0:NG], bB[:, 0:NG], AF.Sqrt, ALLB, ALLB, scale=0.25)

            def s_w(c=c, t=None):
                if t is None:
                    stt(bB[:, 0:NG], bA[:, 0:NG], 1.0, bB[:, 0:NG], ALU.add, ALU.mult, ALLA + ALLB, ALLB)
                else:
                    o, n = tiles[t]
                    stt(bB[:, o:o + n], bA[:, o:o + n], 1.0, bB[:, o:o + n], ALU.add, ALU.mult, [d_bA[t], d_bB[t]], [d_bB[t]])

            def s_uu(c=c, t=None):
                if t is None:
                    stt(bC[:, 0:NG], bB[:, 0:NG], 0.5e-6, bC[:, 0:NG], ALU.max, ALU.mult, ALLB + ALLC, ALLC)
                else:
                    o, n = tiles[t]
                    stt(bC[:, o:o + n], bB[:, o:o + n], 0.5e-6, bC[:, o:o + n], ALU.max, ALU.mult, [d_bB[t], d_bC[t]], [d_bC[t]])
                if samp_g and (t is None or t == 2):
                    a3 = bA[:, TPG:TPG + TS].rearrange("p (s t) -> p s t", t=ST)
                    u3 = bC[:, TPG:TPG + TS].rearrange("p (s t) -> p s t", t=ST)
                    tt(tmp16[:], a3[:, :, 0], st_in[:, c, 0:NS], ALU.mult, [d_bA[2], d_stin], [d_tmp16], small=True)
                    tt(u3[:, :, 0], u3[:, :, 0], tmp16[:], ALU.add, [d_tmp16, d_bC[2]], [d_bC[2]], small=True)
                    S.op("dve", lambda e, a3=a3: e.memset(a3[:, :, 0], 0.0), reads=[d_tmp16], writes=[d_bA[2]], small=True)

            def s_scan(t, c=c):
                o, n = tiles[t]
                samp = samp_g and t == 2
                npz = TW if samp else n
                if t == 0:
                    init = 0.0 if g == 0 else hcar[:, c:c + 1]
                    rdi = [] if g == 0 else [d_hcar]
                else:
                    init = bB[:, o - 1:o]
                    rdi = [d_bB[t - 1]]
                S.op("dve", lambda e, o=o, npz=npz, init=init: e.tensor_tensor_scan(
                    out=bB[:, o:o + npz], data0=bA[:, o:o + npz], data1=bC[:, o:o + npz], initial=init,
                    op0=ALU.mult, op1=ALU.add), reads=[d_bA[t], d_bC[t], d_bB[t]] + rdi, writes=[d_bB[t]])
                if samp:
                    h3 = bB[:, TPG:TPG + TS].rearrange("p (s t) -> p s t", t=ST)
                    S.op("dve", lambda e: e.tensor_tensor_scan(
                        out=bB[:, TPG:TPG + TS], data0=bA[:, TPG:TPG + TS], data1=bC[:, TPG:TPG + TS], initial=0.0,
                        op0=ALU.mult, op1=ALU.add), reads=[d_bA[t], d_bC[t], d_bB[t]], writes=[d_bB[t]], small=True)
                    cp(st_out[:, c, 0:1], bB[:, TPG - 1:TPG], [d_bB[t]], [d_stout])
                    cp(st_out[:, c, 6:22], h3[:, :, ST - 1], [d_bB[t]], [d_stout])
                if g == 0 and t == 2:
                    cp(hcar[:, c:c + 1], bB[:, TPG - 1:TPG], [d_bB[t]], [d_hcar])

            def s_ya(c=c, gD=gD, ALLG=ALLG, d_gD=d_gD, t=None):
                if t is None:
                    tt(R[:, c, 0:NG], bB[:, 0:NG], gD[:, 0:NG], ALU.mult, ALLB + ALLG, list(d_R[c]))
                else:
                    o, n = tiles[t]
                    tt(R[:, c, o:o + n], bB[:, o:o + n], gD[:, o:o + n], ALU.mult, [d_bB[t], d_gD[t]], [d_R[c][t]])

            if c < NCH - 1:
                part2b_dve = [s_w, s_uu, lambda: s_scan(0), lambda: s_scan(1), lambda: s_scan(2), s_ya]
            else:
                part2b_dve = []
                for tt_ in range(3):
                    part2b_dve += [lambda tt_=tt_: s_w(t=tt_), lambda tt_=tt_: s_uu(t=tt_),
                                   lambda tt_=tt_: s_scan(tt_), lambda tt_=tt_: s_ya(t=tt_)]

            part2b_prev[0] = (part2b_act, part2b_dve, part2a_tail)
        return part2b_prev[0]

    def merge(l, g, tail):
        G = GROUPS[g]
        tiles = G["tiles"]
        for c in range(NCH):
            wb, d_wb = next_batch()
            prefetch()
            it = lambda i, wb=wb: wb[:, i * 1024:(i + 1) * 1024]
            sA, dA, sC, dC = (xc, d_xc, vc, d_vc) if c == 0 else (bA, d_bA, bC, d_bC)
            for gi_item, dsig, sigbuf in ((0, dA, sA), (1, dC, sC)):
                if c == 0 and gi_item == 1 and tail is not None:
                    pass
                for t, (o, n) in enumerate(tiles):
                    bank, d_b = next_bank()
                    mm_group(bank, d_b, n, [(it(gi_item)[:, k * 128:(k + 1) * 128], u_sb[:, k, o:o + n]) for k in range(NCH)],
                             [d_wb] + [d_u[k][t] for k in range(NCH)])
                    if c == 0:
                        S.op("dve", lambda e, sigbuf=sigbuf, bank=bank, o=o, n=n: e.tensor_copy(out=sigbuf[:, o:o + n], in_=bank[:, 0:n]),
                             reads=[d_b], writes=[dsig[t]])
                        act(sigbuf[:, o:o + n], sigbuf[:, o:o + n], AF.Sigmoid, [dsig[t]], [dsig[t]])
                    else:
                        act(sigbuf[:, o:o + n], bank[:, 0:n], AF.Sigmoid, [d_b], [dsig[t]])
            if c == 0:
                for st_ in tail[1]:
                    st_()
            for t, (o, n) in enumerate(tiles):
                bank, d_b = next_bank()
                mm_group(bank, d_b, n, [(it(2)[:, k * 128:(k + 1) * 128], R[:, 8 + k, o:o + n]) for k in range(NCH)],
                         [d_wb] + [d_R[8 + k][t] for k in range(NCH)])
                tt(bD[:, o:o + n], bank[:, 0:n], sC[:, o:o + n], ALU.mult, [d_b, dC[t]], [d_bD[t]])
            for t, (o, n) in enumerate(tiles):
                bank, d_b = next_bank()
                mm_group(bank, d_b, n, [(it(3)[:, k * 128:(k + 1) * 128], R[:, k, o:o + n]) for k in range(NCH)],
                         [d_wb] + [d_R[k][t] for k in range(NCH)])
                tt(bB[:, o:o + n], bank[:, 0:n], sA[:, o:o + n], ALU.mult, [d_b, dA[t]], [d_bB[t]])
            for t, (o, n) in enumerate(tiles):
                tt(R[:, 16 + c, o:o + n], bB[:, o:o + n], bD[:, o:o + n], ALU.add, [d_bB[t], d_bD[t]], [d_R[16 + c][t]])

    def wout_norm2(l, g):
        G = GROUPS[g]
        tiles = G["tiles"]
        nb = [hold_bank() for _ in range(3)]
        act(dtmp[4][:], dtmp[2][:], AF.Ln, [d_dtmp[2]], [d_dtmp[4]], small=True)
        for bi in range(2):
            wb, d_wb = next_batch()
            prefetch()
            for j in range(4):
                oc = bi * 4 + j
                item = wb[:, j * 1024:(j + 1) * 1024]
                for t, (o, n) in enumerate(tiles):
                    gt = g * 3 + t
                    go = G["off"] + o
                    bank, d_b = next_bank()
                    mm_group(bank, d_b, n, [(item[:, k * 128:(k + 1) * 128], R[:, 16 + k, o:o + n]) for k in range(NCH)],
                             [d_wb] + [d_R[16 + k][t] for k in range(NCH)])
                    tt(x_sb[:, oc, go:go + n], x_sb[:, oc, go:go + n], bank[:, 0:n], ALU.add, [d_b, d_x[oc][gt]], [d_x[oc][gt]])
                if oc > 0:
                    for t in range(3):
                        norm_sq_chunk(g, t, oc - 1, nb[t][0], nb[t][1])
        for t in range(3):
            norm_sq_chunk(g, t, NCH - 1, nb[t][0], nb[t][1])
        for t in range(3):
            norm_finish(l, g, t, 12, nb[t][0], nb[t][1])
            release_bank(nb[t])

    def store_tile(g, t):
        G = GROUPS[g]
        o, n = G["tiles"][t]
        go = G["off"] + o
        gt = g * 3 + t
        for c0 in range(0, NCH, 4):
            S.dma("sp", lambda e, c0=c0: e.dma_start(out=y_d[:, c0:c0 + 4, go:go + n], in_=x_sb[:, c0:c0 + 4, go:go + n]), "st",
                  reads=[d_x[c][gt] for c in range(c0, c0 + 4)])

    def ffn(l, g, nxt):
        G = GROUPS[g]
        tiles = G["tiles"]
        sbufs = [(bB, d_bB), (bC, d_bC), (bD, d_bD)]
        for bi in range(11):
            wb, d_wb = next_batch()
            prefetch()
            if bi == 0:
                for t, (o, n) in enumerate(tiles):
                    for jj in range(2):
                        gate = wb[:, (2 * jj) * 1024:(2 * jj + 1) * 1024]
                        sbuf_, dsb = sbufs[jj % 3]
                        bank, d_b = next_bank()
                        mm_group(bank, d_b, n, [(gate[:, k * 128:(k + 1) * 128], u_sb[:, k, o:o + n]) for k in range(NCH)],
                                 [d_wb] + [d_u[k][t] for k in range(NCH)])
                        act(sbuf_[:, o:o + n], bank[:, 0:n], AF.Silu, [d_b], [dsb[t]])
                    for jj in range(2):
                        up = wb[:, (2 * jj + 1) * 1024:(2 * jj + 2) * 1024]
                        sbuf_, dsb = sbufs[jj % 3]
                        bank, d_b = next_bank()
                        mm_group(bank, d_b, n, [(up[:, k * 128:(k + 1) * 128], u_sb[:, k, o:o + n]) for k in range(NCH)],
                                 [d_wb] + [d_u[k][t] for k in range(NCH)])
                        tt(R[:, jj, o:o + n], bank[:, 0:n], sbuf_[:, o:o + n], ALU.mult, [d_b, dsb[t]], [d_R[jj][t]])
                continue
            for jj in range(2):
                j = bi * 2 + jj
                gate = wb[:, (2 * jj) * 1024:(2 * jj + 1) * 1024]
                up = wb[:, (2 * jj + 1) * 1024:(2 * jj + 2) * 1024]
                sbuf_, dsb = sbufs[j % 3]
                for t, (o, n) in enumerate(tiles):
                    bank, d_b = next_bank()
                    mm_group(bank, d_b, n, [(gate[:, k * 128:(k + 1) * 128], u_sb[:, k, o:o + n]) for k in range(NCH)],
                             [d_wb] + [d_u[k][t] for k in range(NCH)])
                    act(sbuf_[:, o:o + n], bank[:, 0:n], AF.Silu, [d_b], [dsb[t]])
                for t, (o, n) in enumerate(tiles):
                    bank, d_b = next_bank()
                    mm_group(bank, d_b, n, [(up[:, k * 128:(k + 1) * 128], u_sb[:, k, o:o + n]) for k in range(NCH)],
                             [d_wb] + [d_u[k][t] for k in range(NCH)])
                    tt(R[:, j, o:o + n], bank[:, 0:n], sbuf_[:, o:o + n], ALU.mult, [d_b, dsb[t]], [d_R[j][t]])
        hoist = {1: 0, 3: 1, 5: 2}
        for oc in range(NCH):
            wb, d_wb = next_batch()
            prefetch()
            for t, (o, n) in enumerate(tiles):
                gt = g * 3 + t
                go = G["off"] + o
                bank, d_b = next_bank()
                mm_group(bank, d_b, n, [(wb[:, k * 128:(k + 1) * 128], R[:, k, o:o + n]) for k in range(NFF)],
                         [d_wb] + [d_R[k][t] for k in range(NFF)])
                tt(x_sb[:, oc, go:go + n], x_sb[:, oc, go:go + n], bank[:, 0:n], ALU.add, [d_b, d_x[oc][gt]], [d_x[oc][gt]])
            if nxt is not None and oc in hoist:
                norm_tile(nxt[0], nxt[1], hoist[oc], 0)
            if nxt is None and oc in hoist:
                norm_tile(0, 0, hoist[oc], 0, final=True)
                store_tile(0, hoist[oc])

    seq = [(l, g) for l in range(depth) for g in range(2)]
    for t in range(3):
        norm_tile(0, 0, t, 0)
    deferred_loads()
    derived_constants()
    for i, (l, g) in enumerate(seq):
        if g == 0:
            S.dma("pool", lambda e, l=l: e.dma_start(out=gw[:], in_=ws_d[l, :, 0:W_GATES], max_dma_last_dim=4096), "gwl",
                  writes=[d_gw])
            S.dma("sp", lambda e, l=l: e.dma_start(out=st_in[:].rearrange("p c n -> p (c n)"), in_=stin_d[l]), "ldst",
                  writes=[d_stin])
        tail = mixer(l, g)
        tail[2]()
        tail[0]()
        merge(l, g, tail)
        if g == 1:
            S.dma("sp", lambda e, l=l: e.dma_start(out=sto_d[l], in_=st_out[:].rearrange("p c n -> p (c n)")), "st",
                  reads=[d_stout])
        wout_norm2(l, g)
        ffn(l, g, seq[i + 1] if i + 1 < len(seq) else None)

    for t in range(3):
        norm_tile(0, 1, t, 0, final=True)
        store_tile(1, t)

    sem_keys = list(Sched.ENGS) + sorted(S.dma_cnt.keys())
    sems = {k: es.enter_context(nc.semaphore("s_" + k)) for k in sem_keys}

    def emit(name, e):
        for waits, fn, key, inc in S.streams[name]:
            for k, v in waits:
                e.wait_ge(sems[k], v)
            ins = fn(e)
            ins.then_inc(sems[key], inc)
        if name == "sp":
            for k, v in S.dma_cnt.items():
                e.wait_ge(sems[k], v)

    with nc.Block() as block:
        @block.tensor
        def _(e):
            emit("pe", e)

        @block.scalar
        def _(e):
            emit("act", e)

        @block.vector
        def _(e):
            emit("dve", e)

        @block.gpsimd
        def _(e):
            emit("pool", e)

        @block.sync
        def _(e):
            emit("sp", e)
    es.close()
    return nc


def _fm(a):
    T = a.shape[0]
    return np.ascontiguousarray(a.reshape(T, NCH, 128).transpose(2, 1, 0))


def _pack_weights(inp):
    ws = np.zeros((DEPTH, 128, WS_LAYER), np.float32)
    for l in range(DEPTH):
        off = 0
        gwl = np.zeros((128, 2, 8, 128), np.float32)
        for gi, name in enumerate(("gate_a_w", "gate_x_w")):
            w = inp[name][l]
            w2 = w.reshape(8, 2, 64, 64)
            gwl[0:64, gi, :, 0:64] = w2[:, 0].transpose(1, 0, 2)
            gwl[64:128, gi, :, 64:128] = w2[:, 1].transpose(1, 0, 2)
        ws[l, :, off:off + W_GATES] = gwl.reshape(128, W_GATES); off += W_GATES

        def item(W, col0):
            K = W.shape[0]
            blk = W[:, col0:col0 + 128].reshape(K // 128, 128, 128)
            return blk.transpose(1, 0, 2).reshape(128, K)

        w_in = inp["w_in"][l]
        for c in range(8):
            for s in ("xr", "gr", "cc", "hc", "bc"):
                ws[l, :, off:off + 1024] = item(w_in, OFF[s] + c * 128); off += 1024
        wa, wbm = inp["w_branch_a"][l], inp["w_branch_b"][l]
        for c in range(8):
            ws[l, :, off:off + 1024] = item(w_in, OFF["ga"] + c * 128); off += 1024
            ws[l, :, off:off + 1024] = item(w_in, OFF["gb"] + c * 128); off += 1024
            ws[l, :, off:off + 1024] = item(wbm, c * 128); off += 1024
            ws[l, :, off:off + 1024] = item(wa, c * 128); off += 1024
        wo = inp["w_out"][l]
        for c in range(8):
            ws[l, :, off:off + 1024] = item(wo, c * 128); off += 1024
        wg, wu = inp["w_ff_gate"][l], inp["w_ff_up"][l]
        for j in range(NFF):
            ws[l, :, off:off + 1024] = item(wg, j * 128); off += 1024
            ws[l, :, off:off + 1024] = item(wu, j * 128); off += 1024
        wd = inp["w_ff_down"][l]
        for c in range(8):
            ws[l, :, off:off + W_DN] = item(wd, c * 128); off += W_DN
        assert off == WS_LAYER
    return ws


def _pack_vecs(inp):
    v = np.zeros((128, NVEC * DEPTH * 8 + 8), np.float32)
    rows = []
    for l in range(DEPTH):
        rows.append([inp["norm1_g"][l]] + [inp["rnn_conv_w"][l, k] for k in range(4)] + [inp["rnn_conv_b"][l],
                    inp["gate_a_b"][l], inp["gate_x_b"][l], inp["lru_lambda"][l]] +
                    [inp["sc_conv_w"][l, k] for k in range(3)] + [inp["norm2_g"][l]])
    for i in range(NVEC):
        for l in range(DEPTH):
            o = (i * DEPTH + l) * 8
            v[:, o:o + 8] = np.asarray(rows[l][i]).reshape(8, 128).T
    v[:, NVEC * DEPTH * 8:] = np.asarray(inp["final_norm_g"]).reshape(8, 128).T
    return v


def kernel(**inputs):
    inp = {k: np.asarray(v) for k, v in inputs.items()}
    nc = build_program(DEPTH)
    ws = _pack_weights(inp)
    vecs = _pack_vecs(inp)
    meta = _fm(inp["meta_tokens"].astype(np.float32))
    in_maps = []
    for i in range(NCORES):
        xp = _fm(inp["x_prompt"][i])
        xs = _fm(inp["x_sample"][i * NS:(i + 1) * NS].reshape(TS, D))
        stin = np.zeros((DEPTH, 128, NCH, NSTI), np.float32)
        h0 = inp["state_rnn_h"][:, i * NS:(i + 1) * NS]
        rc = inp["state_rnn_conv"][:, i * NS:(i + 1) * NS]
        sc = inp["state_sc_conv"][:, i * NS:(i + 1) * NS]
        stin[:, :, :, 0:16] = h0.reshape(DEPTH, NS, NCH, 128).transpose(0, 3, 2, 1)
        stin[:, :, :, 16:64] = rc.reshape(DEPTH, NS, 3, NCH, 128).transpose(0, 4, 3, 1, 2).reshape(DEPTH, 128, NCH, 48)
        stin[:, :, :, 64:96] = sc.reshape(DEPTH, NS, 2, NCH, 128).transpose(0, 4, 3, 1, 2).reshape(DEPTH, 128, NCH, 32)
        in_maps.append(dict(xp=xp, meta=meta, xs=xs, stin=np.ascontiguousarray(stin.reshape(DEPTH, 128, NCH * NSTI)),
                            vecs=vecs, ws=ws))
    res = run_bass_kernel_spmd(nc, in_maps, core_ids=list(range(NCORES)))
    y_prompt = np.zeros((NCORES, SEQ, D), np.float32)
    y_sample = np.zeros((NCORES * NS, ST, D), np.float32)
    rnn_h_p = np.zeros((DEPTH, NCORES, D), np.float32)
    rnn_c_p = np.zeros((DEPTH, NCORES, 3, D), np.float32)
    sc_c_p = np.zeros((DEPTH, NCORES, 2, D), np.float32)
    rnn_h_s = np.zeros((DEPTH, NCORES * NS, D), np.float32)
    rnn_c_s = np.zeros((DEPTH, NCORES * NS, 3, D), np.float32)
    sc_c_s = np.zeros((DEPTH, NCORES * NS, 2, D), np.float32)
    for i in range(NCORES):
        r = res.results[i]
        y = np.asarray(r["y"]).reshape(128, NCH, TTOT)
        yt = y.transpose(2, 1, 0).reshape(TTOT, D)
        y_prompt[i] = yt[NMETA:TP]
        y_sample[i * NS:(i + 1) * NS] = yt[TP:].reshape(NS, ST, D)
        so = np.asarray(r["sto"]).reshape(DEPTH, 128, NCH, NSTO)
        so = so.transpose(0, 3, 2, 1).reshape(DEPTH, NSTO, D)
        rnn_h_p[:, i] = so[:, 0]
        rnn_c_p[:, i] = so[:, 1:4]
        sc_c_p[:, i] = so[:, 4:6]
        rnn_h_s[:, i * NS:(i + 1) * NS] = so[:, 6:22]
        rnn_c_s[:, i * NS:(i + 1) * NS] = so[:, 22:70].reshape(DEPTH, NS, 3, D)
        sc_c_s[:, i * NS:(i + 1) * NS] = so[:, 70:102].reshape(DEPTH, NS, 2, D)
    return (y_prompt, y_sample, rnn_h_p, rnn_c_p, sc_c_p, rnn_h_s, rnn_c_s, sc_c_s)
```

```python
import numpy as np
from contextlib import ExitStack
import concourse.bass as bass
import concourse.mybir as mybir
from concourse.bass_utils import run_bass_kernel_spmd

F32 = mybir.dt.float32
BF16 = mybir.dt.bfloat16
AF = mybir.ActivationFunctionType
ALU = mybir.AluOpType

D = 1024
NCH = 8
DFF = 2816
NFF = 22
DEPTH = 4
NMETA = 16
SEQ = 2048
TP = NMETA + SEQ
NS = 16
ST = 4
TS = NS * ST
TTOT = TP + TS
NCORES = 8
EPS = 1e-6
OFF = dict(xr=0, gr=1024, bc=2048, cc=3072, hc=4096, ga=5120, gb=6144)
TPG = TP // 2
TW = 344
GROUPS = [
    dict(off=0, n=TPG, tiles=[(0, TW), (TW, TW), (2 * TW, TW)], samp=False),
    dict(off=TPG, n=TPG + TS, tiles=[(0, TW), (TW, TW), (2 * TW, TW + TS)], samp=True),
]
GW = TPG + TS
NVEC = 13
NSTI = 96
NSTO = 102
W_GATES = 2 * 8 * 128
W_MIX = 5 * 1024
W_MRG = 4 * 1024
W_OUT = 4 * 1024
W_FFN = 4 * 1024
W_DN = NFF * 128
WS_LAYER = W_GATES + 8 * W_MIX + 8 * W_MRG + 2 * W_OUT + 11 * W_FFN + 8 * W_DN
WBUF = 5120
XRW = 1160
SB0 = 1040
NSQ = 3

SAME_SYNC_ALL = True


class Dep:
    __slots__ = ("w", "r")

    def __init__(self):
        self.w = None
        self.r = {}


class Sched:
    ENGS = ("pe", "act", "dve", "pool", "sp")

    def __init__(self):
        self.streams = {e: [] for e in self.ENGS}
        self.tick = {e: 0 for e in self.ENGS}
        self.known = {e: {} for e in self.ENGS}
        self.dma_cnt = {}

    def _waits(self, eng, reads, writes, small):
        waits = {}

        def need(k, v):
            if k == eng and (eng == "pe" or not (small or SAME_SYNC_ALL)):
                return
            if self.known[eng].get(k, 0) >= v:
                return
            if waits.get(k, 0) < v:
                waits[k] = v

        for t in reads:
            if t.w is not None:
                need(*t.w)
        for t in writes:
            if t.w is not None:
                need(*t.w)
            for k, v in t.r.items():
                need(k, v)
        for k, v in waits.items():
            self.known[eng][k] = v
        return list(waits.items())

    def _mark(self, tok, reads, writes):
        k, v = tok
        for t in reads:
            t.r[k] = v
        for t in writes:
            t.w = tok
            t.r = {}

    def op(self, eng, fn, reads=(), writes=(), small=False):
        waits = self._waits(eng, reads, writes, small)
        self.tick[eng] += 1
        tok = (eng, self.tick[eng])
        self.streams[eng].append((waits, fn, eng, 1))
        self._mark(tok, reads, writes)

    def dma(self, eng, fn, semkey, reads=(), writes=()):
        waits = self._waits(eng, reads, writes, False)
        self.dma_cnt[semkey] = self.dma_cnt.get(semkey, 0) + 16
        tok = (semkey, self.dma_cnt[semkey])
        self.streams[eng].append((waits, fn, semkey, 16))
        self._mark(tok, reads, writes)


def build_program(depth=DEPTH):
    nc = bass.Bass("TRN2", target_bir_lowering=False)
    xp_d = nc.dram_tensor("xp", [128, NCH, SEQ], F32, kind="ExternalInput").ap()
    meta_d = nc.dram_tensor("meta", [128, NCH, NMETA], F32, kind="ExternalInput").ap()
    xs_d = nc.dram_tensor("xs", [128, NCH, TS], F32, kind="ExternalInput").ap()
    stin_d = nc.dram_tensor("stin", [DEPTH, 128, NCH * NSTI], F32, kind="ExternalInput").ap()
    vecs_d = nc.dram_tensor("vecs", [128, NVEC * DEPTH * 8 + 8], F32, kind="ExternalInput").ap()
    ws_d = nc.dram_tensor("ws", [DEPTH, 128, WS_LAYER], F32, kind="ExternalInput").ap()
    y_d = nc.dram_tensor("y", [128, NCH, TTOT], F32, kind="ExternalOutput").ap()
    sto_d = nc.dram_tensor("sto", [DEPTH, 128, NCH * NSTO], F32, kind="ExternalOutput").ap()

    S = Sched()
    es = ExitStack()

    def sb(name, shape, dt):
        return es.enter_context(nc.sbuf_tensor(name, shape, dt))

    x_sb = sb("x_sb", [128, NCH, TTOT], F32)
    u_sb = sb("u_sb", [128, NCH, GW], BF16)
    R = sb("R", [128, 24, GW], BF16)
    wbuf = [sb("wbuf0", [128, WBUF], BF16), sb("wbuf1", [128, WBUF], BF16)]
    gw = sb("gw", [128, W_GATES], BF16)
    vec_sb = sb("vec_sb", [128, NVEC * DEPTH * 8 + 8], F32)
    der_sb = sb("der_sb", [128, 4 * DEPTH * 8], F32)
    dtmp = [sb("dtmp%d" % i, [128, DEPTH * 8], F32) for i in range(6)]
    ones_bf = sb("ones_bf", [128, 128], BF16)
    xr_sb = sb("xr_sb", [128, XRW], F32)
    xc = sb("xc", [128, GW], F32)
    xcb = sb("xcb", [128, GW], BF16)
    vc = sb("vc", [128, GW], F32)
    bA = sb("bA", [128, GW], F32)
    bB = sb("bB", [128, GW], F32)
    bC = sb("bC", [128, GW], F32)
    bD = sb("bD", [128, GW], F32)
    bD1 = sb("bD1", [128, GW], F32)
    sqb = [sb("sqb%d" % i, [128, 416], BF16) for i in range(NSQ)]
    st_in = sb("st_in", [128, NCH, NSTI], F32)
    st_out = sb("st_out", [128, NCH, NSTO], F32)
    hcar = sb("hcar", [128, NCH], F32)
    xhalo = sb("xhalo", [128, NCH, 3], F32)
    chhalo = sb("chhalo", [128, NCH, 2], F32)
    tmp16 = sb("tmp16", [128, NS], F32)
    banks = [es.enter_context(nc.psum_tensor("ps%d" % i, [128, 512], F32)) for i in range(8)]

    d_x = [[Dep() for _ in range(6)] for _ in range(NCH)]
    d_u = [[Dep() for _ in range(3)] for _ in range(NCH)]
    d_R = [[Dep() for _ in range(3)] for _ in range(24)]
    d_wbuf = [Dep(), Dep()]
    d_gw = Dep()
    d_vec = Dep()
    d_der = Dep()
    d_dtmp = [Dep() for _ in range(6)]
    d_ones = Dep()
    d_xr = [Dep() for _ in range(3)]
    d_xrh = Dep()
    d_xc = [Dep() for _ in range(3)]
    d_xcb = [Dep() for _ in range(3)]
    d_vc = [Dep() for _ in range(3)]
    d_bA = [Dep() for _ in range(3)]
    d_bB = [Dep() for _ in range(3)]
    d_bC = [Dep() for _ in range(3)]
    d_bD = [Dep() for _ in range(3)]
    d_bD1 = [Dep() for _ in range(3)]
    d_sqb = [Dep() for _ in range(NSQ)]
    d_stin = Dep()
    d_stout = Dep()
    d_hcar = Dep()
    d_xhalo = Dep()
    d_chhalo = Dep()
    d_tmp16 = Dep()
    d_bank = [Dep() for _ in range(8)]
    d_y = Dep()
    d_meta = Dep()

    bank_ctr = [0]

    held = set()

    def next_bank():
        while True:
            b = bank_ctr[0] % 8
            bank_ctr[0] += 1
            if b not in held:
                return banks[b], d_bank[b]

    def hold_bank():
        bk = next_bank()
        held.add(banks.index(bk[0]))
        return bk

    def release_bank(bk):
        held.discard(banks.index(bk[0]))

    def V(l, i, c):
        o = (i * DEPTH + l) * 8 + c
        return vec_sb[:, o:o + 1]

    def DER(kind, l, c):
        o = kind * DEPTH * 8 + l * 8 + c
        return der_sb[:, o:o + 1]

    batches = []
    for l in range(depth):
        for g in range(2):
            off = W_GATES
            for _ in range(8):
                batches.append((l, off, W_MIX)); off += W_MIX
            for _ in range(8):
                batches.append((l, off, W_MRG)); off += W_MRG
            for _ in range(2):
                batches.append((l, off, W_OUT)); off += W_OUT
            for _ in range(11):
                batches.append((l, off, W_FFN)); off += W_FFN
            for _ in range(8):
                batches.append((l, off, W_DN)); off += W_DN
            assert off == WS_LAYER
    bstate = dict(issued=0, used=0)

    def issue_batch():
        i = bstate["issued"]
        if i >= len(batches):
            return
        l, off, n = batches[i]
        b = i % 2
        src = ws_d[l, :, off:off + n]
        dst = wbuf[b][:, 0:n]
        S.dma("pool", lambda e, src=src, dst=dst: e.dma_start(out=dst, in_=src, max_dma_last_dim=4096),
              "w%d" % b, reads=([d_x[0][2]] if i == 1 else ()), writes=(d_wbuf[b],))
        bstate["issued"] += 1

    def next_batch():
        i = bstate["used"]
        while bstate["issued"] <= min(i, len(batches) - 1):
            issue_batch()
        b = i % 2
        bstate["used"] += 1
        return wbuf[b], d_wbuf[b]

    def prefetch():
        if bstate["issued"] < bstate["used"] + 1:
            issue_batch()

    def mm_group(bank, d_b, n, pairs, reads):
        def fn(e, bank=bank, n=n, pairs=pairs):
            last = None
            for i, (lt, rh) in enumerate(pairs):
                last = e.matmul(out=bank[:, 0:n], lhsT=lt, rhs=rh, start=(i == 0), stop=(i == len(pairs) - 1))
            return last
        S.op("pe", fn, reads=reads, writes=(d_b,))

    def act(out, in_, func, reads, writes, bias=None, scale=None, small=False):
        kw = {}
        if bias is not None:
            kw["bias"] = bias
        if scale is not None:
            kw["scale"] = scale
        S.op("act", lambda e: e.activation(out=out, in_=in_, func=func, **kw), reads=reads, writes=writes, small=small)

    def tt(out, in0, in1, op, reads, writes, small=False):
        S.op("dve", lambda e: e.tensor_tensor(out=out, in0=in0, in1=in1, op=op), reads=reads, writes=writes, small=small)

    def ts(out, in0, s1, op0, reads, writes, s2=None, op1=None, small=False):
        if op1 is None:
            S.op("dve", lambda e: e.tensor_scalar(out=out, in0=in0, scalar1=s1, scalar2=None, op0=op0),
                 reads=reads, writes=writes, small=small)
        else:
            S.op("dve", lambda e: e.tensor_scalar(out=out, in0=in0, scalar1=s1, scalar2=s2, op0=op0, op1=op1),
                 reads=reads, writes=writes, small=small)

    def stt(out, in0, scalar, in1, op0, op1, reads, writes, small=False):
        S.op("dve", lambda e: e.scalar_tensor_tensor(out=out, in0=in0, scalar=scalar, in1=in1, op0=op0, op1=op1),
             reads=reads, writes=writes, small=small)

    def cp(out, in_, reads, writes, small=True):
        S.op("dve", lambda e: e.tensor_copy(out=out, in_=in_), reads=reads, writes=writes, small=small)

    S.dma("sp", lambda e: e.dma_start(out=vec_sb[:], in_=vecs_d), "ldv", writes=(d_vec,))
    NP0 = TPG - NMETA
    S.dma("sp", lambda e: e.dma_start(out=x_sb[:, :, 0:NMETA], in_=meta_d), "ldm", writes=[d_meta])
    for t, (o, n) in enumerate(GROUPS[0]["tiles"]):
        lo = max(o, NMETA)
        S.dma("sp", lambda e, lo=lo, o=o, n=n: e.dma_start(out=x_sb[:, :, lo:o + n], in_=xp_d[:, :, lo - NMETA:o + n - NMETA]),
              "ldx%d" % t, reads=([d_x[0][0]] if t > 0 else []), writes=[d_x[c][t] for c in range(NCH)])

    def deferred_loads():
        S.dma("sp", lambda e: e.dma_start(out=x_sb[:, :, TP:TTOT], in_=xs_d), "lds",
              reads=[d_u[NCH - 1][2]], writes=[d_x[c][5] for c in range(NCH)])
        for c0 in range(0, NCH, 4):
            S.dma("sp", lambda e, c0=c0: e.dma_start(out=x_sb[:, c0:c0 + 4, TPG:TP], in_=xp_d[:, c0:c0 + 4, NP0:SEQ]), "ldy%d" % c0,
                  reads=[d_u[NCH - 1][2]], writes=[d_x[c][t] for c in range(c0, c0 + 4) for t in range(3, 6)])
    S.op("dve", lambda e: e.memset(ones_bf[:], 1.0), writes=(d_ones,))

    def derived_constants():
        NL = DEPTH * 8
        lam = vec_sb[:, 8 * NL:9 * NL]
        t0, t1, t2, t3, t4, t5 = [t[:] for t in dtmp]
        dd = d_dtmp
        ts(t0, lam, -1.0, ALU.mult, [d_vec], [dd[0]], small=True)
        tt(t0, t0, lam, ALU.min, [d_vec, dd[0]], [dd[0]], small=True)
        act(t1, t0, AF.Exp, [dd[0]], [dd[1]], small=True)
        ts(t2, t1, 2.0, ALU.add, [dd[1]], [dd[2]], small=True)
        S.op("dve", lambda e: e.reciprocal(out=t2, in_=t2), reads=[dd[2]], writes=[dd[2]], small=True)
        tt(t3, t1, t2, ALU.mult, [dd[1], dd[2]], [dd[3]], small=True)
        tt(t4, t3, t3, ALU.mult, [dd[3]], [dd[4]], small=True)
        S.op("dve", lambda e: e.memset(t5, 0.0), writes=[dd[5]], small=True)
        for k in range(9, 0, -1):
            stt(t5, t5, 1.0 / (2 * k + 1), t4, ALU.add, ALU.mult, [dd[5], dd[4]], [dd[5]], small=True)
        stt(t5, t5, 1.0, t3, ALU.add, ALU.mult, [dd[5], dd[3]], [dd[5]], small=True)
        ts(t0, lam, -1.0, ALU.mult, [d_vec, dd[0]], [dd[0]], s2=0.0, op1=ALU.max, small=True)
        stt(t1, t5, 2.0, t0, ALU.mult, ALU.add, [dd[5], dd[0], dd[1]], [dd[1]], small=True)
        ts(der_sb[:, 0 * NL:1 * NL], vec_sb[:, 6 * NL:7 * NL], 0.5, ALU.mult, [d_vec], [d_der], small=True)
        ts(der_sb[:, 1 * NL:2 * NL], vec_sb[:, 7 * NL:8 * NL], 0.5, ALU.mult, [d_vec], [d_der], small=True)
        ts(der_sb[:, 2 * NL:3 * NL], t1, -4.0, ALU.mult, [dd[1]], [d_der], small=True)
        ts(der_sb[:, 3 * NL:4 * NL], t1, 2.0, ALU.mult, [dd[1]], [d_der], small=True)


    sq_ctr = [0]

    def norm_sq_chunk(g, t, c, bank, d_b):
        G = GROUPS[g]
        o, n = G["tiles"][t]
        gt = g * 3 + t
        go = G["off"] + o
        q = sq_ctr[0] % NSQ
        sq_ctr[0] += 1
        act(sqb[q][:, 0:n], x_sb[:, c, go:go + n], AF.Square, [d_x[c][gt]] + ([d_meta] if gt == 0 else []), [d_sqb[q]])
        S.op("pe", lambda e: e.matmul(out=bank[:, 0:n], lhsT=ones_bf[:], rhs=sqb[q][:, 0:n],
                                      start=(c == 0), stop=(c == NCH - 1)),
             reads=[d_sqb[q], d_ones], writes=[d_b])

    def norm_finish(l, g, t, gi, bank, d_b, final=False):
        G = GROUPS[g]
        o, n = G["tiles"][t]
        gt = g * 3 + t
        go = G["off"] + o
        act(bA[:, o:o + n], bank[:, 0:n], AF.Ln, [d_b], [d_bA[t]], bias=EPS, scale=1.0 / D)
        act(bA[:, o:o + n], bA[:, o:o + n], AF.Exp, [d_bA[t]], [d_bA[t]], scale=-0.5)
        for c in range(NCH):
            if final:
                fo = NVEC * DEPTH * 8 + c
                stt(x_sb[:, c, go:go + n], x_sb[:, c, go:go + n], vec_sb[:, fo:fo + 1], bA[:, o:o + n], ALU.mult, ALU.mult,
                    [d_x[c][gt], d_bA[t], d_vec], [d_x[c][gt]])
            else:
                stt(u_sb[:, c, o:o + n], x_sb[:, c, go:go + n], V(l, gi, c), bA[:, o:o + n], ALU.mult, ALU.mult,
                    [d_x[c][gt], d_bA[t], d_vec], [d_u[c][t]])

    def norm_tile(l, g, t, gi, final=False):
        bank, d_b = next_bank()
        for c in range(NCH):
            norm_sq_chunk(g, t, c, bank, d_b)
        norm_finish(l, g, t, gi, bank, d_b, final)

    def conv_taps(G, width, wvec, l, c, k0, outbuf, d_out, ks=None):
        xs3 = xr_sb[:, SB0:SB0 + NS * 7].rearrange("p (s t) -> p s t", t=7)
        xo = outbuf[:, TPG:TPG + TS].rearrange("p (s t) -> p s t", t=ST)
        rd = list(d_xr) + [d_xrh, d_vec]
        for k in (range(k0, width) if ks is None else ks):
            wk = V(l, wvec + (width - 1 - k), c)
            if k == 0:
                ts(outbuf[:, 0:TPG], xr_sb[:, 3:3 + TPG], wk, ALU.mult, rd, list(d_out))
            else:
                stt(outbuf[:, 0:TPG], xr_sb[:, 3 - k:3 - k + TPG], wk, outbuf[:, 0:TPG], ALU.mult, ALU.add,
                    rd + list(d_out), list(d_out))
            if G["samp"]:
                if k == 0:
                    ts(xo, xs3[:, :, 3:7], wk, ALU.mult, rd, [d_out[2]])
                else:
                    stt(xo, xs3[:, :, 3 - k:7 - k], wk, xo, ALU.mult, ALU.add, rd + [d_out[2]], [d_out[2]])

    def mixer(l, g):
        G = GROUPS[g]
        tiles = G["tiles"]
        samp_g = G["samp"]
        xs3 = xr_sb[:, SB0:SB0 + NS * 7].rearrange("p (s t) -> p s t", t=7)
        part2b_prev = [None]
        NG = G["n"]
        ALLA, ALLB, ALLC, ALLD = list(d_bA), list(d_bB), list(d_bC), list(d_bD)

        for c in range(NCH):
            wb, d_wb = next_batch()
            prefetch()
            it = lambda i, wb=wb: wb[:, i * 1024:(i + 1) * 1024]
            gD, d_gD = (bD, d_bD) if c % 2 == 0 else (bD1, d_bD1)
            ALLG = list(d_gD)

            def mmw(item, t, d_wb=d_wb):
                o, n = tiles[t]
                bank, d_b = next_bank()
                mm_group(bank, d_b, n, [(item[:, k * 128:(k + 1) * 128], u_sb[:, k, o:o + n]) for k in range(NCH)],
                         [d_wb] + [d_u[k][t] for k in range(NCH)])
                return bank, d_b, o, n

            if g == 0:
                S.op("dve", lambda e: e.memset(xr_sb[:, 0:3], 0.0), reads=[], writes=[d_xrh], small=True)
            else:
                cp(xr_sb[:, 0:3], xhalo[:, c, :], [d_xhalo], [d_xrh])
            if samp_g:
                cp(xs3[:, :, 0:3], st_in[:, c, 16:64].rearrange("p (s t) -> p s t", t=3), [d_stin], [d_xrh])
            xrb = []
            for t in range(3):
                bank, d_b, o, n = mmw(it(0), t)
                xrb.append((bank, d_b, o, n))
                samp = samp_g and t == 2
                npz = TW if samp else n
                act(xr_sb[:, 3 + o:3 + o + npz], bank[:, 0:npz], AF.Copy, [d_b], [d_xr[t]])
                if samp:
                    act(xs3[:, :, 3:7], bank[:, TW:TW + TS].rearrange("p (s t) -> p s t", t=ST), AF.Copy, [d_b], [d_xr[t]])
            for t, (bank, d_b, o, n) in enumerate(xrb):
                act(xc[:, o:o + n], bank[:, 0:n], AF.Identity, [d_b, d_vec], [d_xc[t]], bias=V(l, 5, c), scale=V(l, 4, c))
            conv_taps(G, 4, 1, l, c, 1, xc, d_xc)
            if g == 0:
                cp(xhalo[:, c, :], xr_sb[:, 3 + TPG - 3:3 + TPG], [d_xr[2]], [d_xhalo])
            else:
                cp(st_out[:, c, 1:4], xr_sb[:, 3 + TPG - 3:3 + TPG], [d_xr[2]], [d_stout])
                cp(st_out[:, c, 22:70].rearrange("p (s t) -> p s t", t=3), xs3[:, :, 4:7], [d_xr[2]], [d_stout])

            if part2b_prev[0] is not None:
                part2b_prev[0][2]()
            for t in range(3):
                bank, d_b, o, n = mmw(it(1), t)
                act(gD[:, o:o + n], bank[:, 0:n], AF.Copy, [d_b], [d_gD[t]])
            act(xcb[:, 0:NG], xc[:, 0:NG], AF.Copy, list(d_xc), list(d_xcb))

            if g == 1:
                cp(xr_sb[:, 1:3], chhalo[:, c, :], [d_chhalo], [d_xrh])
            if samp_g:
                cp(xs3[:, :, 1:3], st_in[:, c, 64:96].rearrange("p (s t) -> p s t", t=2), [d_stin], [d_xrh])
            for t in range(3):
                bank, d_b, o, n = mmw(it(2), t)
                samp = samp_g and t == 2
                npz = TW if samp else n
                act(xr_sb[:, 3 + o:3 + o + npz], bank[:, 0:npz], AF.Copy, [d_b], [d_xr[t]])
                if samp:
                    act(xs3[:, :, 3:7], bank[:, TW:TW + TS].rearrange("p (s t) -> p s t", t=ST), AF.Copy, [d_b], [d_xr[t]])
            if part2b_prev[0] is not None:
                part2b_prev[0][0]()
            act(gD[:, 0:NG], gD[:, 0:NG], AF.Gelu_apprx_tanh, ALLG, ALLG)
            for t in range(3):
                bank, d_b, o, n = mmw(it(3), t)
                samp = samp_g and t == 2
                npz = TW if samp else n
                tt(xr_sb[:, 3 + o:3 + o + npz], xr_sb[:, 3 + o:3 + o + npz], bank[:, 0:npz], ALU.mult, [d_b, d_xr[t]], [d_xr[t]])
                if samp:
                    tt(xs3[:, :, 3:7], xs3[:, :, 3:7], bank[:, TW:TW + TS].rearrange("p (s t) -> p s t", t=ST), ALU.mult,
                       [d_b, d_xr[t]], [d_xr[t]])
            steps = list(part2b_prev[0][1]) if part2b_prev[0] is not None else []

            def step():
                if steps:
                    steps.pop(0)()
            conv_taps(G, 3, 9, l, c, 0, vc, d_vc)
            for t in range(3):
                bank, d_b, o, n = mmw(it(4), t)
                tt(R[:, 8 + c, o:o + n], bank[:, 0:n], vc[:, o:o + n], ALU.mult, [d_b, d_vc[t]], [d_R[8 + c][t]])
            if g == 0:
                cp(chhalo[:, c, :], xr_sb[:, 3 + TPG - 2:3 + TPG], [d_xr[2]], [d_chhalo])
            else:
                cp(st_out[:, c, 4:6], xr_sb[:, 3 + TPG - 2:3 + TPG], [d_xr[2]], [d_stout])
                cp(st_out[:, c, 70:102].rearrange("p (s t) -> p s t", t=2), xs3[:, :, 5:7], [d_xr[2]], [d_stout])
            while steps:
                step()
            part2b_prev[0] = None

            for t, (o, n) in enumerate(tiles):
                br, d_br = next_bank()
                S.op("pe", lambda e, br=br, n=n, o=o, c=c: e.matmul(out=br[:, 0:n], lhsT=gw[:, c * 128:(c + 1) * 128],
                                                                     rhs=xcb[:, o:o + n], start=True, stop=True),
                     reads=[d_gw, d_xcb[t]], writes=[d_br])
                act(vc[:, o:o + n], br[:, 0:n], AF.Tanh, [d_br, d_der], [d_vc[t]], bias=DER(0, l, c), scale=0.5)
            for t, (o, n) in enumerate(tiles):
                bi, d_bi = next_bank()
                S.op("pe", lambda e, bi=bi, n=n, o=o, c=c: e.matmul(out=bi[:, 0:n], lhsT=gw[:, 1024 + c * 128:1024 + (c + 1) * 128],
                                                                     rhs=xcb[:, o:o + n], start=True, stop=True),
                     reads=[d_gw, d_xcb[t]], writes=[d_bi])
                act(bC[:, o:o + n], bi[:, 0:n], AF.Tanh, [d_bi, d_der], [d_bC[t]], bias=DER(1, l, c), scale=0.5)
            stt(bC[:, 0:NG], bC[:, 0:NG], 1.0, xc[:, 0:NG], ALU.add, ALU.mult, ALLC + list(d_xc), ALLC)

            def part2a_tail(c=c):
                act(bB[:, 0:NG], vc[:, 0:NG], AF.Tanh, list(d_vc) + [d_der], ALLB, bias=DER(3, l, c), scale=DER(3, l, c))
                act(bA[:, 0:NG], vc[:, 0:NG], AF.Exp, list(d_vc) + [d_der], ALLA, bias=DER(2, l, c), scale=DER(2, l, c))

            def part2b_act(c=c, gD=gD, ALLG=ALLG):
                act(bB[:, 0:NG], bB[:, 0:NG], AF.Sqrt, ALLB, ALLB, scale=0.25)

            def s_w(c=c, t=None):
                if t is None:
                    stt(bB[:, 0:NG], bA[:, 0:NG], 1.0, bB[:, 0:NG], ALU.add, ALU.mult, ALLA + ALLB, ALLB)
                else:
                    o, n = tiles[t]
                    stt(bB[:, o:o + n], bA[:, o:o + n], 1.0, bB[:, o:o + n], ALU.add, ALU.mult, [d_bA[t], d_bB[t]], [d_bB[t]])

            def s_uu(c=c, t=None):
                if t is None:
                    stt(bC[:, 0:NG], bB[:, 0:NG], 0.5e-6, bC[:, 0:NG], ALU.max, ALU.mult, ALLB + ALLC, ALLC)
                else:
                    o, n = tiles[t]
                    stt(bC[:, o:o + n], bB[:, o:o + n], 0.5e-6, bC[:, o:o + n], ALU.max, ALU.mult, [d_bB[t], d_bC[t]], [d_bC[t]])
                if samp_g and (t is None or t == 2):
                    a3 = bA[:, TPG:TPG + TS].rearrange("p (s t) -> p s t", t=ST)
                    u3 = bC[:, TPG:TPG + TS].rearrange("p (s t) -> p s t", t=ST)
                    tt(tmp16[:], a3[:, :, 0], st_in[:, c, 0:NS], ALU.mult, [d_bA[2], d_stin], [d_tmp16], small=True)
                    tt(u3[:, :, 0], u3[:, :, 0], tmp16[:], ALU.add, [d_tmp16, d_bC[2]], [d_bC[2]], small=True)
                    S.op("dve", lambda e, a3=a3: e.memset(a3[:, :, 0], 0.0), reads=[d_tmp16], writes=[d_bA[2]], small=True)

            def s_scan(t, c=c):
                o, n = tiles[t]
                samp = samp_g and t == 2
                npz = TW if samp else n
                if t == 0:
                    init = 0.0 if g == 0 else hcar[:, c:c + 1]
                    rdi = [] if g == 0 else [d_hcar]
                else:
                    init = bB[:, o - 1:o]
                    rdi = [d_bB[t - 1]]
                S.op("dve", lambda e, o=o, npz=npz, init=init: e.tensor_tensor_scan(
                    out=bB[:, o:o + npz], data0=bA[:, o:o + npz], data1=bC[:, o:o + npz], initial=init,
                    op0=ALU.mult, op1=ALU.add), reads=[d_bA[t], d_bC[t], d_bB[t]] + rdi, writes=[d_bB[t]])
                if samp:
                    h3 = bB[:, TPG:TPG + TS].rearrange("p (s t) -> p s t", t=ST)
                    S.op("dve", lambda e: e.tensor_tensor_scan(
                        out=bB[:, TPG:TPG + TS], data0=bA[:, TPG:TPG + TS], data1=bC[:, TPG:TPG + TS], initial=0.0,
                        op0=ALU.mult, op1=ALU.add), reads=[d_bA[t], d_bC[t], d_bB[t]], writes=[d_bB[t]], small=True)
                    cp(st_out[:, c, 0:1], bB[:, TPG - 1:TPG], [d_bB[t]], [d_stout])
                    cp(st_out[:, c, 6:22], h3[:, :, ST - 1], [d_bB[t]], [d_stout])
                if g == 0 and t == 2:
                    cp(hcar[:, c:c + 1], bB[:, TPG - 1:TPG], [d_bB[t]], [d_hcar])

            def s_ya(c=c, gD=gD, ALLG=ALLG, d_gD=d_gD, t=None):
                if t is None:
                    tt(R[:, c, 0:NG], bB[:, 0:NG], gD[:, 0:NG], ALU.mult, ALLB + ALLG, list(d_R[c]))
                else:
                    o, n = tiles[t]
                    tt(R[:, c, o:o + n], bB[:, o:o + n], gD[:, o:o + n], ALU.mult, [d_bB[t], d_gD[t]], [d_R[c][t]])

            if c < NCH - 1:
                part2b_dve = [s_w, s_uu, lambda: s_scan(0), lambda: s_scan(1), lambda: s_scan(2), s_ya]
            else:
                part2b_dve = []
                for tt_ in range(3):
                    part2b_dve += [lambda tt_=tt_: s_w(t=tt_), lambda tt_=tt_: s_uu(t=tt_),
                                   lambda tt_=tt_: s_scan(tt_), lambda tt_=tt_: s_ya(t=tt_)]

            part2b_prev[0] = (part2b_act, part2b_dve, part2a_tail)
        return part2b_prev[0]

    def merge(l, g, tail):
        G = GROUPS[g]
        tiles = G["tiles"]
        for c in range(NCH):
            wb, d_wb = next_batch()
            prefetch()
            it = lambda i, wb=wb: wb[:, i * 1024:(i + 1) * 1024]
            sA, dA, sC, dC = (xc, d_xc, vc, d_vc) if c == 0 else (bA, d_bA, bC, d_bC)
            for gi_item, dsig, sigbuf in ((0, dA, sA), (1, dC, sC)):
                if c == 0 and gi_item == 1 and tail is not None:
                    pass
                for t, (o, n) in enumerate(tiles):
                    bank, d_b = next_bank()
                    mm_group(bank, d_b, n, [(it(gi_item)[:, k * 128:(k + 1) * 128], u_sb[:, k, o:o + n]) for k in range(NCH)],
                             [d_wb] + [d_u[k][t] for k in range(NCH)])
                    if c == 0:
                        S.op("dve", lambda e, sigbuf=sigbuf, bank=bank, o=o, n=n: e.tensor_copy(out=sigbuf[:, o:o + n], in_=bank[:, 0:n]),
                             reads=[d_b], writes=[dsig[t]])
                        act(sigbuf[:, o:o + n], sigbuf[:, o:o + n], AF.Sigmoid, [dsig[t]], [dsig[t]])
                    else:
                        act(sigbuf[:, o:o + n], bank[:, 0:n], AF.Sigmoid, [d_b], [dsig[t]])
            if c == 0:
                for st_ in tail[1]:
                    st_()
            for t, (o, n) in enumerate(tiles):
                bank, d_b = next_bank()
                mm_group(bank, d_b, n, [(it(2)[:, k * 128:(k + 1) * 128], R[:, 8 + k, o:o + n]) for k in range(NCH)],
                         [d_wb] + [d_R[8 + k][t] for k in range(NCH)])
                tt(bD[:, o:o + n], bank[:, 0:n], sC[:, o:o + n], ALU.mult, [d_b, dC[t]], [d_bD[t]])
            for t, (o, n) in enumerate(tiles):
                bank, d_b = next_bank()
                mm_group(bank, d_b, n, [(it(3)[:, k * 128:(k + 1) * 128], R[:, k, o:o + n]) for k in range(NCH)],
                         [d_wb] + [d_R[k][t] for k in range(NCH)])
                tt(bB[:, o:o + n], bank[:, 0:n], sA[:, o:o + n], ALU.mult, [d_b, dA[t]], [d_bB[t]])
            for t, (o, n) in enumerate(tiles):
                tt(R[:, 16 + c, o:o + n], bB[:, o:o + n], bD[:, o:o + n], ALU.add, [d_bB[t], d_bD[t]], [d_R[16 + c][t]])

    def wout_norm2(l, g):
        G = GROUPS[g]
        tiles = G["tiles"]
        nb = [hold_bank() for _ in range(3)]
        act(dtmp[4][:], dtmp[2][:], AF.Ln, [d_dtmp[2]], [d_dtmp[4]], small=True)
        for bi in range(2):
            wb, d_wb = next_batch()
            prefetch()
            for j in range(4):
                oc = bi * 4 + j
                item = wb[:, j * 1024:(j + 1) * 1024]
                for t, (o, n) in enumerate(tiles):
                    gt = g * 3 + t
                    go = G["off"] + o
                    bank, d_b = next_bank()
                    mm_group(bank, d_b, n, [(item[:, k * 128:(k + 1) * 128], R[:, 16 + k, o:o + n]) for k in range(NCH)],
                             [d_wb] + [d_R[16 + k][t] for k in range(NCH)])
                    tt(x_sb[:, oc, go:go + n], x_sb[:, oc, go:go + n], bank[:, 0:n], ALU.add, [d_b, d_x[oc][gt]], [d_x[oc][gt]])
                if oc > 0:
                    for t in range(3):
                        norm_sq_chunk(g, t, oc - 1, nb[t][0], nb[t][1])
        for t in range(3):
            norm_sq_chunk(g, t, NCH - 1, nb[t][0], nb[t][1])
        for t in range(3):
            norm_finish(l, g, t, 12, nb[t][0], nb[t][1])
            release_bank(nb[t])

    def store_tile(g, t):
        G = GROUPS[g]
        o, n = G["tiles"][t]
        go = G["off"] + o
        gt = g * 3 + t
        for c0 in range(0, NCH, 4):
            S.dma("sp", lambda e, c0=c0: e.dma_start(out=y_d[:, c0:c0 + 4, go:go + n], in_=x_sb[:, c0:c0 + 4, go:go + n]), "st",
                  reads=[d_x[c][gt] for c in range(c0, c0 + 4)])

    def ffn(l, g, nxt):
        G = GROUPS[g]
        tiles = G["tiles"]
        sbufs = [(bB, d_bB), (bC, d_bC), (bD, d_bD)]
        for bi in range(11):
            wb, d_wb = next_batch()
            prefetch()
            if bi == 0:
                for t, (o, n) in enumerate(tiles):
                    for jj in range(2):
                        gate = wb[:, (2 * jj) * 1024:(2 * jj + 1) * 1024]
                        sbuf_, dsb = sbufs[jj % 3]
                        bank, d_b = next_bank()
                        mm_group(bank, d_b, n, [(gate[:, k * 128:(k + 1) * 128], u_sb[:, k, o:o + n]) for k in range(NCH)],
                                 [d_wb] + [d_u[k][t] for k in range(NCH)])
                        act(sbuf_[:, o:o + n], bank[:, 0:n], AF.Silu, [d_b], [dsb[t]])
                    for jj in range(2):
                        up = wb[:, (2 * jj + 1) * 1024:(2 * jj + 2) * 1024]
                        sbuf_, dsb = sbufs[jj % 3]
                        bank, d_b = next_bank()
                        mm_group(bank, d_b, n, [(up[:, k * 128:(k + 1) * 128], u_sb[:, k, o:o + n]) for k in range(NCH)],
                                 [d_wb] + [d_u[k][t] for k in range(NCH)])
                        tt(R[:, jj, o:o + n], bank[:, 0:n], sbuf_[:, o:o + n], ALU.mult, [d_b, dsb[t]], [d_R[jj][t]])
                continue
            for jj in range(2):
                j = bi * 2 + jj
                gate = wb[:, (2 * jj) * 1024:(2 * jj + 1) * 1024]
                up = wb[:, (2 * jj + 1) * 1024:(2 * jj + 2) * 1024]
                sbuf_, dsb = sbufs[j % 3]
                for t, (o, n) in enumerate(tiles):
                    bank, d_b = next_bank()
                    mm_group(bank, d_b, n, [(gate[:, k * 128:(k + 1) * 128], u_sb[:, k, o:o + n]) for k in range(NCH)],
                             [d_wb] + [d_u[k][t] for k in range(NCH)])
                    act(sbuf_[:, o:o + n], bank[:, 0:n], AF.Silu, [d_b], [dsb[t]])
                for t, (o, n) in enumerate(tiles):
                    bank, d_b = next_bank()
                    mm_group(bank, d_b, n, [(up[:, k * 128:(k + 1) * 128], u_sb[:, k, o:o + n]) for k in range(NCH)],
                             [d_wb] + [d_u[k][t] for k in range(NCH)])
                    tt(R[:, j, o:o + n], bank[:, 0:n], sbuf_[:, o:o + n], ALU.mult, [d_b, dsb[t]], [d_R[j][t]])
        hoist = {1: 0, 3: 1, 5: 2}
        for oc in range(NCH):
            wb, d_wb = next_batch()
            prefetch()
            for t, (o, n) in enumerate(tiles):
                gt = g * 3 + t
                go = G["off"] + o
                bank, d_b = next_bank()
                mm_group(bank, d_b, n, [(wb[:, k * 128:(k + 1) * 128], R[:, k, o:o + n]) for k in range(NFF)],
                         [d_wb] + [d_R[k][t] for k in range(NFF)])
                tt(x_sb[:, oc, go:go + n], x_sb[:, oc, go:go + n], bank[:, 0:n], ALU.add, [d_b, d_x[oc][gt]], [d_x[oc][gt]])
            if nxt is not None and oc in hoist:
                norm_tile(nxt[0], nxt[1], hoist[oc], 0)
            if nxt is None and oc in hoist:
                norm_tile(0, 0, hoist[oc], 0, final=True)
                store_tile(0, hoist[oc])

    seq = [(l, g) for l in range(depth) for g in range(2)]
    for t in range(3):
        norm_tile(0, 0, t, 0)
    deferred_loads()
    derived_constants()
    for i, (l, g) in enumerate(seq):
        if g == 0:
            S.dma("pool", lambda e, l=l: e.dma_start(out=gw[:], in_=ws_d[l, :, 0:W_GATES], max_dma_last_dim=4096), "gwl",
                  writes=[d_gw])
            S.dma("sp", lambda e, l=l: e.dma_start(out=st_in[:].rearrange("p c n -> p (c n)"), in_=stin_d[l]), "ldst",
                  writes=[d_stin])
        tail = mixer(l, g)
        tail[2]()
        tail[0]()
        merge(l, g, tail)
        if g == 1:
            S.dma("sp", lambda e, l=l: e.dma_start(out=sto_d[l], in_=st_out[:].rearrange("p c n -> p (c n)")), "st",
                  reads=[d_stout])
        wout_norm2(l, g)
        ffn(l, g, seq[i + 1] if i + 1 < len(seq) else None)

    for t in range(3):
        norm_tile(0, 1, t, 0, final=True)
        store_tile(1, t)

    sem_keys = list(Sched.ENGS) + sorted(S.dma_cnt.keys())
    sems = {k: es.enter_context(nc.semaphore("s_" + k)) for k in sem_keys}

    def emit(name, e):
        for waits, fn, key, inc in S.streams[name]:
            for k, v in waits:
                e.wait_ge(sems[k], v)
            ins = fn(e)
            ins.then_inc(sems[key], inc)
        if name == "sp":
            for k, v in S.dma_cnt.items():
                e.wait_ge(sems[k], v)

    with nc.Block() as block:
        @block.tensor
        def _(e):
            emit("pe", e)

        @block.scalar
        def _(e):
            emit("act", e)

        @block.vector
        def _(e):
            emit("dve", e)

        @block.gpsimd
        def _(e):
            emit("pool", e)

        @block.sync
        def _(e):
            emit("sp", e)
    es.close()
    return nc


def _fm(a):
    T = a.shape[0]
    return np.ascontiguousarray(a.reshape(T, NCH, 128).transpose(2, 1, 0))


def _pack_weights(inp):
    ws = np.zeros((DEPTH, 128, WS_LAYER), np.float32)
    for l in range(DEPTH):
        off = 0
        gwl = np.zeros((128, 2, 8, 128), np.float32)
        for gi, name in enumerate(("gate_a_w", "gate_x_w")):
            w = inp[name][l]
            w2 = w.reshape(8, 2, 64, 64)
            gwl[0:64, gi, :, 0:64] = w2[:, 0].transpose(1, 0, 2)
            gwl[64:128, gi, :, 64:128] = w2[:, 1].transpose(1, 0, 2)
        ws[l, :, off:off + W_GATES] = gwl.reshape(128, W_GATES); off += W_GATES

        def item(W, col0):
            K = W.shape[0]
            blk = W[:, col0:col0 + 128].reshape(K // 128, 128, 128)
            return blk.transpose(1, 0, 2).reshape(128, K)

        w_in = inp["w_in"][l]
        for c in range(8):
            for s in ("xr", "gr", "cc", "hc", "bc"):
                ws[l, :, off:off + 1024] = item(w_in, OFF[s] + c * 128); off += 1024
        wa, wbm = inp["w_branch_a"][l], inp["w_branch_b"][l]
        for c in range(8):
            ws[l, :, off:off + 1024] = item(w_in, OFF["ga"] + c * 128); off += 1024
            ws[l, :, off:off + 1024] = item(w_in, OFF["gb"] + c * 128); off += 1024
            ws[l, :, off:off + 1024] = item(wbm, c * 128); off += 1024
            ws[l, :, off:off + 1024] = item(wa, c * 128); off += 1024
        wo = inp["w_out"][l]
        for c in range(8):
            ws[l, :, off:off + 1024] = item(wo, c * 128); off += 1024
        wg, wu = inp["w_ff_gate"][l], inp["w_ff_up"][l]
        for j in range(NFF):
            ws[l, :, off:off + 1024] = item(wg, j * 128); off += 1024
            ws[l, :, off:off + 1024] = item(wu, j * 128); off += 1024
        wd = inp["w_ff_down"][l]
        for c in range(8):
            ws[l, :, off:off + W_DN] = item(wd, c * 128); off += W_DN
        assert off == WS_LAYER
    return ws


def _pack_vecs(inp):
    v = np.zeros((128, NVEC * DEPTH * 8 + 8), np.float32)
    rows = []
    for l in range(DEPTH):
        rows.append([inp["norm1_g"][l]] + [inp["rnn_conv_w"][l, k] for k in range(4)] + [inp["rnn_conv_b"][l],
                    inp["gate_a_b"][l], inp["gate_x_b"][l], inp["lru_lambda"][l]] +
                    [inp["sc_conv_w"][l, k] for k in range(3)] + [inp["norm2_g"][l]])
    for i in range(NVEC):
        for l in range(DEPTH):
            o = (i * DEPTH + l) * 8
            v[:, o:o + 8] = np.asarray(rows[l][i]).reshape(8, 128).T
    v[:, NVEC * DEPTH * 8:] = np.asarray(inp["final_norm_g"]).reshape(8, 128).T
    return v


def kernel(**inputs):
    inp = {k: np.asarray(v) for k, v in inputs.items()}
    nc = build_program(DEPTH)
    ws = _pack_weights(inp)
    vecs = _pack_vecs(inp)
    meta = _fm(inp["meta_tokens"].astype(np.float32))
    in_maps = []
    for i in range(NCORES):
        xp = _fm(inp["x_prompt"][i])
        xs = _fm(inp["x_sample"][i * NS:(i + 1) * NS].reshape(TS, D))
        stin = np.zeros((DEPTH, 128, NCH, NSTI), np.float32)
        h0 = inp["state_rnn_h"][:, i * NS:(i + 1) * NS]
        rc = inp["state_rnn_conv"][:, i * NS:(i + 1) * NS]
        sc = inp["state_sc_conv"][:, i * NS:(i + 1) * NS]
        stin[:, :, :, 0:16] = h0.reshape(DEPTH, NS, NCH, 128).transpose(0, 3, 2, 1)
        stin[:, :, :, 16:64] = rc.reshape(DEPTH, NS, 3, NCH, 128).transpose(0, 4, 3, 1, 2).reshape(DEPTH, 128, NCH, 48)
        stin[:, :, :, 64:96] = sc.reshape(DEPTH, NS, 2, NCH, 128).transpose(0, 4, 3, 1, 2).reshape(DEPTH, 128, NCH, 32)
        in_maps.append(dict(xp=xp, meta=meta, xs=xs, stin=np.ascontiguousarray(stin.reshape(DEPTH, 128, NCH * NSTI)),
                            vecs=vecs, ws=ws))
    res = run_bass_kernel_spmd(nc, in_maps, core_ids=list(range(NCORES)))
    y_prompt = np.zeros((NCORES, SEQ, D), np.float32)
    y_sample = np.zeros((NCORES * NS, ST, D), np.float32)
    rnn_h_p = np.zeros((DEPTH, NCORES, D), np.float32)
    rnn_c_p = np.zeros((DEPTH, NCORES, 3, D), np.float32)
    sc_c_p = np.zeros((DEPTH, NCORES, 2, D), np.float32)
    rnn_h_s = np.zeros((DEPTH, NCORES * NS, D), np.float32)
    rnn_c_s = np.zeros((DEPTH, NCORES * NS, 3, D), np.float32)
    sc_c_s = np.zeros((DEPTH, NCORES * NS, 2, D), np.float32)
    for i in range(NCORES):
        r = res.results[i]
        y = np.asarray(r["y"]).reshape(128, NCH, TTOT)
        yt = y.transpose(2, 1, 0).reshape(TTOT, D)
        y_prompt[i] = yt[NMETA:TP]
        y_sample[i * NS:(i + 1) * NS] = yt[TP:].reshape(NS, ST, D)
        so = np.asarray(r["sto"]).reshape(DEPTH, 128, NCH, NSTO)
        so = so.transpose(0, 3, 2, 1).reshape(DEPTH, NSTO, D)
        rnn_h_p[:, i] = so[:, 0]
        rnn_c_p[:, i] = so[:, 1:4]
        sc_c_p[:, i] = so[:, 4:6]
        rnn_h_s[:, i * NS:(i + 1) * NS] = so[:, 6:22]
        rnn_c_s[:, i * NS:(i + 1) * NS] = so[:, 22:70].reshape(DEPTH, NS, 3, D)
        sc_c_s[:, i * NS:(i + 1) * NS] = so[:, 70:102].reshape(DEPTH, NS, 2, D)
    return (y_prompt, y_sample, rnn_h_p, rnn_c_p, sc_c_p, rnn_h_s, rnn_c_s, sc_c_s)
```
